# Optimizing a Trainium2 kernel written in Bass

```python
import jax, jax.numpy as jnp
from jax import lax
import numpy as np

D_MODEL = 1024
BATCH = 4
SEQ = 8192
DEPTH = 2

CHUNK = 64
Q_BLOCK = 128
EPS = 1e-6
D_FF = 4 * D_MODEL
A_HEADS = 4
A_DQK = 128
A_DV = D_MODEL // A_HEADS
B_HEADS = 8
B_NOPE = 128
B_ROPE = 64
B_VDIM = D_MODEL // B_HEADS
B_QLORA = 384
B_KVLORA = 256
ROPE_THETA = 10000.0
N_A = DEPTH // 2
N_B = DEPTH - N_A
A_PROJ = A_HEADS * (2 * A_DQK + A_DV) + D_MODEL + 2 * A_HEADS

kernel_name = "yoco_mlstm_mla_hybrid"


def rms_norm(x, g):
    xf = x.astype(jnp.float32)
    y = xf * lax.rsqrt(jnp.mean(xf * xf, axis=-1, keepdims=True) + EPS)
    return (y * g.astype(jnp.float32)).astype(x.dtype)


def modulate(h, shift, scale):
    return h * (1 + scale[:, None, :]) + shift[:, None, :]


def rope_tables(positions):
    inv = ROPE_THETA ** (-jnp.arange(0, B_ROPE, 2, dtype=jnp.float32) / B_ROPE)
    ang = positions.astype(jnp.float32)[..., None] * inv
    return jnp.cos(ang), jnp.sin(ang)


def apply_rope(x, cos, sin):
    x1, x2 = jnp.split(x.astype(jnp.float32), 2, axis=-1)
    out = jnp.concatenate([x1 * cos - x2 * sin, x1 * sin + x2 * cos], axis=-1)
    return out.astype(x.dtype)


def sq_relu_mlp(h, w1, w2):
    return jnp.square(jax.nn.relu(h @ w1)) @ w2


def mlstm_chunkwise(q, k, v, i_pre, f_pre):
    Bn, S, H, dk = q.shape
    dv = v.shape[-1]
    nc = S // CHUNK
    f32 = jnp.float32

    def chunks(t):
        return t.astype(f32).reshape(Bn, nc, CHUNK, H, t.shape[-1]).transpose(1, 0, 3, 2, 4)

    def gchunks(t):
        return t.astype(f32).reshape(Bn, nc, CHUNK, H).transpose(1, 0, 3, 2)

    qc = chunks(q) * (dk ** -0.5)
    kc, vc = chunks(k), chunks(v)
    ic = gchunks(i_pre)
    bc = jnp.cumsum(jax.nn.log_sigmoid(gchunks(f_pre)), axis=-1)
    causal = jnp.tril(jnp.ones((CHUNK, CHUNK), dtype=bool))

    def step(carry, xs):
        C, n, m = carry
        qj, kj, vj, ij, bj = xs
        D = bj[..., :, None] - bj[..., None, :] + ij[..., None, :]
        D = jnp.where(causal, D, -jnp.inf)
        inter = bj + m[..., None]
        m_t = jnp.maximum(inter, jnp.max(D, axis=-1))
        w_inter = jnp.exp(inter - m_t)
        P = jnp.exp(D - m_t[..., None]) * jnp.einsum('bhld,bhsd->bhls', qj, kj)
        num = w_inter[..., None] * jnp.einsum('bhld,bhde->bhle', qj, C) + jnp.einsum('bhls,bhse->bhle', P, vj)
        den = w_inter * jnp.einsum('bhld,bhd->bhl', qj, n) + jnp.sum(P, axis=-1)
        h = num / jnp.maximum(jnp.abs(den), jnp.exp(-m_t))[..., None]
        b_last = bj[..., -1]
        g = b_last[..., None] - bj + ij
        m_new = jnp.maximum(b_last + m, jnp.max(g, axis=-1))
        decay = jnp.exp(b_last + m - m_new)
        wk = jnp.exp(g - m_new[..., None])
        C = decay[..., None, None] * C + jnp.einsum('bhs,bhsd,bhse->bhde', wk, kj, vj)
        n = decay[..., None] * n + jnp.einsum('bhs,bhsd->bhd', wk, kj)
        return (C, n, m_new), h

    init = (jnp.zeros((Bn, H, dk, dv), f32), jnp.zeros((Bn, H, dk), f32), jnp.zeros((Bn, H), f32))
    _, hs = lax.scan(step, init, (qc, kc, vc, ic, bc))
    return hs.transpose(1, 0, 3, 2, 4).reshape(Bn, S, H, dv).astype(q.dtype)


def mlstm_mixer(h, w_in, b_gates, head_norm, w_out):
    Bn, S, _ = h.shape
    sizes = [A_HEADS * A_DQK, A_HEADS * A_DQK, A_HEADS * A_DV, D_MODEL, A_HEADS, A_HEADS]
    idx = [int(s) for s in np.cumsum(sizes)[:-1]]
    q, k, v, o_pre, i_pre, f_pre = jnp.split(h @ w_in, idx, axis=-1)
    q = q.reshape(Bn, S, A_HEADS, A_DQK)
    k = k.reshape(Bn, S, A_HEADS, A_DQK)
    v = v.reshape(Bn, S, A_HEADS, A_DV)
    i_pre = i_pre + b_gates[:A_HEADS]
    f_pre = f_pre + b_gates[A_HEADS:]
    hh = mlstm_chunkwise(q, k, v, i_pre, f_pre)
    hh = rms_norm(hh, head_norm).reshape(Bn, S, D_MODEL)
    return (jax.nn.sigmoid(o_pre) * hh) @ w_out


def mla_shared_kv(x, c_act, kv_mod_w, kv_mod_b, kv_norm, w_dkv, kv_lora_norm, w_ukv, cos, sin):
    Bn, S, _ = x.shape
    shift, scale = jnp.split(c_act @ kv_mod_w + kv_mod_b, 2, axis=-1)
    hs = modulate(rms_norm(x, kv_norm), shift, scale)
    ckv, k_rope = jnp.split(hs @ w_dkv, [B_KVLORA], axis=-1)
    ckv = rms_norm(ckv, kv_lora_norm)
    kv = (ckv @ w_ukv).reshape(Bn, S, B_HEADS, B_NOPE + B_VDIM)
    k_nope, v = jnp.split(kv, [B_NOPE], axis=-1)
    k_rope = apply_rope(k_rope, cos, sin)
    return k_nope, k_rope, v


def mla_attention(q_nope, q_rope, k_nope, k_rope, v):
    Bn, S = q_nope.shape[:2]
    nb = S // Q_BLOCK
    scale = (B_NOPE + B_ROPE) ** -0.5
    key_chunk = jnp.arange(S) // CHUNK
    qn_b = q_nope.reshape(Bn, nb, Q_BLOCK, B_HEADS, B_NOPE).transpose(1, 0, 2, 3, 4)
    qr_b = q_rope.reshape(Bn, nb, Q_BLOCK, B_HEADS, B_ROPE).transpose(1, 0, 2, 3, 4)

    def block(args):
        qn, qr, blk = args
        s = jnp.einsum('bqhd,bkhd->bhqk', qn, k_nope) + jnp.einsum('bqhd,bkd->bhqk', qr, k_rope)
        s = s.astype(jnp.float32) * scale
        q_chunk = (blk * Q_BLOCK + jnp.arange(Q_BLOCK)) // CHUNK
        mask = key_chunk[None, :] <= q_chunk[:, None]
        p = jax.nn.softmax(jnp.where(mask, s, -jnp.inf), axis=-1).astype(v.dtype)
        return jnp.einsum('bhqk,bkhd->bqhd', p, v)

    out = lax.map(block, (qn_b, qr_b, jnp.arange(nb)))
    return out.transpose(1, 0, 2, 3, 4).reshape(Bn, S, B_HEADS * B_VDIM)


def mla_mixer(h, w_qa, q_norm, w_qb, w_o, k_nope, k_rope, v, cos, sin):
    Bn, S, _ = h.shape
    q = (rms_norm(h @ w_qa, q_norm) @ w_qb).reshape(Bn, S, B_HEADS, B_NOPE + B_ROPE)
    q_nope, q_rope = jnp.split(q, [B_NOPE], axis=-1)
    q_rope = apply_rope(q_rope, cos[:, :, None, :], sin[:, :, None, :])
    return mla_attention(q_nope, q_rope, k_nope, k_rope, v) @ w_o


def setup_inputs(seed: int = 0) -> dict:
    key = jax.random.key(seed)
    ks = jax.random.split(key, 24)
    nrm = lambda k, shape, fan: jax.random.normal(k, shape, jnp.float32) * fan ** -0.5
    x = jax.random.normal(ks[0], (BATCH, SEQ, D_MODEL), jnp.float32)
    c = jax.random.normal(ks[1], (BATCH, D_MODEL), jnp.float32)
    offsets = jax.random.randint(ks[2], (BATCH, 1), 0, 64) * CHUNK
    positions = (offsets + jnp.arange(SEQ, dtype=jnp.int32)[None, :]).astype(jnp.int32)
    mod_w = nrm(ks[3], (DEPTH, D_MODEL, 6 * D_MODEL), D_MODEL) * 0.5
    mod_b = 0.01 * jax.random.normal(ks[4], (DEPTH, 6 * D_MODEL), jnp.float32)
    norm_g = 1.0 + 0.02 * jax.random.normal(ks[5], (DEPTH, 4, D_MODEL), jnp.float32)
    ffn_w1 = nrm(ks[6], (DEPTH, D_MODEL, D_FF), D_MODEL)
    ffn_w2 = nrm(ks[7], (DEPTH, D_FF, D_MODEL), D_FF)
    a_w_in = nrm(ks[8], (N_A, D_MODEL, A_PROJ), D_MODEL)
    i_bias = 0.1 * jax.random.normal(ks[9], (N_A, A_HEADS), jnp.float32)
    f_bias = jnp.linspace(3.0, 6.0, A_HEADS, dtype=jnp.float32)[None, :] + 0.1 * jax.random.normal(ks[10], (N_A, A_HEADS), jnp.float32)
    a_b_gates = jnp.concatenate([i_bias, f_bias], axis=-1)
    a_head_norm = 1.0 + 0.02 * jax.random.normal(ks[11], (N_A, A_HEADS, A_DV), jnp.float32)
    a_w_out = nrm(ks[12], (N_A, D_MODEL, D_MODEL), D_MODEL)
    b_w_qa = nrm(ks[13], (N_B, D_MODEL, B_QLORA), D_MODEL)
    b_q_norm = 1.0 + 0.02 * jax.random.normal(ks[14], (N_B, B_QLORA), jnp.float32)
    b_w_qb = nrm(ks[15], (N_B, B_QLORA, B_HEADS * (B_NOPE + B_ROPE)), B_QLORA)
    b_w_o = nrm(ks[16], (N_B, B_HEADS * B_VDIM, D_MODEL), B_HEADS * B_VDIM)
    kv_mod_w = nrm(ks[17], (D_MODEL, 2 * D_MODEL), D_MODEL) * 0.5
    kv_mod_b = 0.01 * jax.random.normal(ks[18], (2 * D_MODEL,), jnp.float32)
    kv_norm = 1.0 + 0.02 * jax.random.normal(ks[19], (D_MODEL,), jnp.float32)
    w_dkv = nrm(ks[20], (D_MODEL, B_KVLORA + B_ROPE), D_MODEL)
    kv_lora_norm = 1.0 + 0.02 * jax.random.normal(ks[21], (B_KVLORA,), jnp.float32)
    w_ukv = nrm(ks[22], (B_KVLORA, B_HEADS * (B_NOPE + B_VDIM)), B_KVLORA)
    return {"x": x, "c": c, "positions": positions, "mod_w": mod_w, "mod_b": mod_b, "norm_g": norm_g,
            "ffn_w1": ffn_w1, "ffn_w2": ffn_w2, "a_w_in": a_w_in, "a_b_gates": a_b_gates,
            "a_head_norm": a_head_norm, "a_w_out": a_w_out, "b_w_qa": b_w_qa, "b_q_norm": b_q_norm,
            "b_w_qb": b_w_qb, "b_w_o": b_w_o, "kv_mod_w": kv_mod_w, "kv_mod_b": kv_mod_b,
            "kv_norm": kv_norm, "w_dkv": w_dkv, "kv_lora_norm": kv_lora_norm, "w_ukv": w_ukv}


def reference(x, c, positions, mod_w, mod_b, norm_g, ffn_w1, ffn_w2, a_w_in, a_b_gates, a_head_norm,
              a_w_out, b_w_qa, b_q_norm, b_w_qb, b_w_o, kv_mod_w, kv_mod_b, kv_norm, w_dkv,
              kv_lora_norm, w_ukv):
    cos, sin = rope_tables(positions)
    c_act = jax.nn.silu(c)
    k_nope = k_rope = v = None
    for layer in range(DEPTH):
        sh1, sc1, g1, sh2, sc2, g2 = jnp.split(c_act @ mod_w[layer] + mod_b[layer], 6, axis=-1)
        h = modulate(rms_norm(x, norm_g[layer, 0]), sh1, sc1)
        if layer < N_A:
            y = mlstm_mixer(h, a_w_in[layer], a_b_gates[layer], a_head_norm[layer], a_w_out[layer])
        else:
            if layer == N_A:
                k_nope, k_rope, v = mla_shared_kv(x, c_act, kv_mod_w, kv_mod_b, kv_norm, w_dkv,
                                                  kv_lora_norm, w_ukv, cos, sin)
            j = layer - N_A
            y = mla_mixer(h, b_w_qa[j], b_q_norm[j], b_w_qb[j], b_w_o[j], k_nope, k_rope, v, cos, sin)
        x = x + g1[:, None, :] * rms_norm(y, norm_g[layer, 1])
        h = modulate(rms_norm(x, norm_g[layer, 2]), sh2, sc2)
        y = sq_relu_mlp(h, ffn_w1[layer], ffn_w2[layer])
        x = x + g2[:, None, :] * rms_norm(y, norm_g[layer, 3])
    return x
```

```python
import contextlib, math
import numpy as np
import concourse.bass as bass
import concourse.mybir as mybir
from concourse.bass_utils import run_bass_kernel_spmd

F32 = mybir.dt.float32
BF16 = mybir.dt.bfloat16
I32 = mybir.dt.int32
AF = mybir.ActivationFunctionType
ALU = mybir.AluOpType
AX = mybir.AxisListType
NDS = 40


class Tile:
    def __init__(self, t, psum=False):
        self.t = t
        self.w = {}
        self.r = {}
        self.psum = psum

    def __getitem__(self, idx):
        return self.t[idx]


class KB:
    def __init__(self):
        self.nc = bass.Bass("TRN2", target_bir_lowering=False)
        self.es = contextlib.ExitStack()
        nc = self.nc
        self.E = {"pe": nc.tensor, "act": nc.scalar, "dve": nc.vector, "pool": nc.gpsimd, "sp": nc.sync}
        self.sem = {e: self.es.enter_context(nc.semaphore("s_" + e)) for e in ["pe", "act", "dve", "pool"]}
        self.cnt = {e: 0 for e in self.sem}
        self.known = {e: {} for e in self.E}
        self.dsems = [self.es.enter_context(nc.semaphore(f"d{i}")) for i in range(NDS)]
        self.dcnt = [0] * NDS
        self.dnext = 0
        self.nins = 0

    def _nm(self, n):
        self.uid = getattr(self, 'uid', 0) + 1
        return f"{n}_{self.uid}"

    def dram_in(self, name, shape, dt=F32):
        return self.nc.dram_tensor(name, list(shape), dt, kind="ExternalInput").ap()

    def dram_out(self, name, shape, dt=F32):
        return self.nc.dram_tensor(name, list(shape), dt, kind="ExternalOutput").ap()

    def sb(self, name, shape, dt=F32):
        return Tile(self.es.enter_context(self.nc.sbuf_tensor(self._nm("sb_" + name), list(shape), dt)))

    def ps(self, name, shape=(128, 512), dt=F32):
        return Tile(self.es.enter_context(self.nc.psum_tensor(self._nm("ps_" + name), list(shape), dt)), psum=True)

    def _need(self, e, key, v):
        if self.known[e].get(key, 0) >= v:
            return
        sem = self.sem[key] if isinstance(key, str) else self.dsems[key]
        self.E[e].wait_ge(sem, v)
        self.known[e][key] = v

    def _deps(self, e, reads, writes):
        for t in reads:
            if t.psum:
                continue
            for k, v in t.w.items():
                if e == "pe" and k == "pe":
                    continue
                self._need(e, k, v)
        for t in list(writes) + [t for t in reads if t.psum]:
            for k, v in list(t.w.items()) + list(t.r.items()):
                if e == "pe" and k == "pe":
                    continue
                self._need(e, k, v)

    def _mark(self, key, ev, reads, writes):
        for t in reads:
            if t.psum:
                t.w[key] = max(t.w.get(key, 0), ev)
            else:
                t.r[key] = max(t.r.get(key, 0), ev)
        for t in writes:
            t.w[key] = max(t.w.get(key, 0), ev)

    def op(self, e, fn, reads=(), writes=(), inc=True):
        self._deps(e, reads, writes)
        ins = fn(self.E[e])
        self.nins += 1
        if inc:
            self.cnt[e] += 1
            ins.then_inc(self.sem[e], 1)
            ev = self.cnt[e]
        else:
            ev = self.cnt[e] + 1
        self._mark(e, ev, reads, writes)
        return ins

    def dma(self, q, out_ap, in_ap, reads=(), writes=(), **kw):
        i = self.dnext
        self.dnext = (self.dnext + 1) % NDS
        if self.dcnt[i] > 0:
            self._need(q, i, self.dcnt[i])
        self._deps(q, reads, writes)
        ins = self.E[q].dma_start(out=out_ap, in_=in_ap, **kw)
        self.nins += 1
        self.dcnt[i] += 16
        ins.then_inc(self.dsems[i], 16)
        self._mark(i, self.dcnt[i], reads, writes)
        return ins

    def finish(self):
        for i in range(NDS):
            if self.dcnt[i] > 0:
                self._need("sp", i, self.dcnt[i])
        for e in self.sem:
            if self.cnt[e] > 0:
                self._need("sp", e, self.cnt[e])

    def mm(self, out_ps, out_ap, lhsT_t, lhsT_ap, rhs_t, rhs_ap, start, stop, inc=None):
        if inc is None:
            inc = stop
        return self.op("pe", lambda E: E.matmul(out_ap, lhsT_ap, rhs_ap, start=start, stop=stop),
                       reads=[lhsT_t, rhs_t], writes=[out_ps], inc=inc)

    def tr(self, out_ps, out_ap, in_t, in_ap, ident_t, ident_ap, inc=True):
        return self.op("pe", lambda E: E.transpose(out_ap, in_ap, ident_ap),
                       reads=[in_t, ident_t], writes=[out_ps], inc=inc)

    def act(self, out_t, out_ap, in_t, in_ap, func, bias=None, scale=None, accum=None, extra_reads=(), e="act"):
        kw = {}
        rd = [in_t] + list(extra_reads)
        wr = [out_t]
        if bias is not None:
            if isinstance(bias, tuple):
                rd.append(bias[0]); kw["bias"] = bias[1]
            else:
                kw["bias"] = bias
        if scale is not None:
            if isinstance(scale, tuple):
                rd.append(scale[0]); kw["scale"] = scale[1]
            else:
                kw["scale"] = scale
        if accum is not None:
            wr.append(accum[0]); kw["accum_out"] = accum[1]
        return self.op("act", lambda E: E.activation(out=out_ap, in_=in_ap, func=func, **kw), reads=rd, writes=wr)


def _scope(self):
    @contextlib.contextmanager
    def cm():
        old = self.es
        self.es = contextlib.ExitStack()
        try:
            yield
        finally:
            self.barrier()
            self.es.close()
            self.es = old
    return cm()


def _barrier(self):
    for e in ["pe", "act", "dve", "pool", "sp"]:
        for e2 in self.sem:
            if self.cnt[e2] > 0:
                self._need(e, e2, self.cnt[e2])
        for i in range(NDS):
            if self.dcnt[i] > 0:
                self._need(e, i, self.dcnt[i])


KB.scope = _scope
KB.barrier = _barrier


EPS = 1e-6
TWO_PI = 2.0 * math.pi


def fm(v):
    v = np.asarray(v, np.float32)
    return np.ascontiguousarray(v.reshape(-1, 128).T)


def bc(ap, n=128):
    return bass.AP(ap.tensor, ap.offset, [[0, n]] + [list(x) for x in ap.ap[1:]])


def ada_cols(k, cact, mw_d, ncols, modb_cols_d, pm, pm_ap, mw, modv):
    noc = ncols // 128
    for kc in range(8):
        k.dma("sp", mw[:, kc, 0:ncols], mw_d[kc * 128:(kc + 1) * 128, :], writes=[mw])
    for oc in range(noc):
        for kc in range(8):
            k.mm(pm, pm_ap[:, oc:oc + 1], mw, mw[:, kc, oc * 128:(oc + 1) * 128], cact, cact[:, kc:kc + 1],
                 kc == 0, kc == 7)
    mb = k.sb("adac_b", [128, noc])
    k.dma("sp", mb[:, :], modb_cols_d, writes=[mb])
    k.op("dve", lambda E: E.tensor_tensor(out=modv[:, 0:noc], in0=pm_ap[:, 0:noc], in1=mb[:, :], op=ALU.add),
         reads=[pm, mb], writes=[modv])


def ada_AB(k, cact, mw_d, modb_cols_d, ng_cols_d, pm, pm_ap, mw, A, modv):
    ada_cols(k, cact, mw_d, 2048, modb_cols_d, pm, pm_ap, mw, modv)
    ng = k.sb("adaab_ng", [128, 8])
    k.dma("sp", ng[:, :], ng_cols_d, writes=[ng])
    k.op("dve", lambda E: E.scalar_tensor_tensor(out=A[:, :], in0=modv[:, 8:16], scalar=1.0, in1=ng[:, :],
                                                 op0=ALU.add, op1=ALU.mult), reads=[modv, ng], writes=[A])


def make_crep(k, cact, name="crep"):
    ones = k.sb(name + "_1", [128, 128])
    k.op("dve", lambda E: E.memset(ones[:, :], 1.0), writes=[ones])
    crep = k.sb(name, [128, 8, 128])
    for kc in range(8):
        k.op("dve", lambda E: E.tensor_scalar(out=crep[:, kc, :], in0=ones[:, :], scalar1=cact[:, kc:kc + 1],
                                              scalar2=None, op0=ALU.mult), reads=[ones, cact], writes=[crep])
    return crep


def ada_rows(k, crep, mw_d, modb_row_d, ng_row_d, pb, mw, Gb):
    for kc in range(8):
        k.dma("sp", mw[:, kc, 0:1024], mw_d[kc * 128:(kc + 1) * 128, :], writes=[mw])
    rb = k.sb("adar_rb", [128, 1024]); rg = k.sb("adar_rg", [128, 1024])
    k.dma("sp", rb[:, :], bc(modb_row_d), writes=[rb])
    k.dma("sp", rg[:, :], bc(ng_row_d), writes=[rg])
    for n2 in range(2):
        for kc in range(8):
            k.mm(pb[n2], pb[n2][:, :], crep, crep[:, kc, :], mw, mw[:, kc, n2 * 512:(n2 + 1) * 512], kc == 0, kc == 7)
        sl = slice(n2 * 512, (n2 + 1) * 512)
        k.op("dve", lambda E: E.tensor_tensor(out=Gb[:, sl], in0=pb[n2][:, :], in1=rb[:, sl], op=ALU.add),
             reads=[pb[n2], rb], writes=[Gb])
    k.op("dve", lambda E: E.tensor_tensor(out=Gb[:, :], in0=Gb[:, :], in1=rg[:, :], op=ALU.mult),
         reads=[Gb, rg], writes=[Gb])


def norm_T(k, xt, j4, ABs, outs, ident, pT, sc):
    ss, rstd, junk, xn, eps = sc["ss"], sc["rstd"], sc["junk"], sc["xn"], sc["eps"]
    for j in range(j4):
        k.act(junk, junk[:, :], xt, xt[:, j, :], AF.Square, scale=1.0 / 32.0, accum=(ss, ss[:, j:j + 1]))
    k.act(rstd, rstd[:, 0:j4], ss, ss[:, 0:j4], AF.Ln, bias=(eps, eps[:, 0:1]))
    k.act(rstd, rstd[:, 0:j4], rstd, rstd[:, 0:j4], AF.Exp, scale=-0.5)
    for j in range(j4):
        k.op("dve", lambda E: E.tensor_scalar(out=xn[:, j, :], in0=xt[:, j, :], scalar1=rstd[:, j:j + 1],
                                              scalar2=None, op0=ALU.mult), reads=[xt, rstd], writes=[xn])
    W = j4 * 128
    for kc in range(8):
        p = pT[kc % 2]
        for j in range(j4):
            k.tr(p, p[:, j * 128:(j + 1) * 128], xn, xn[:, j, kc * 128:(kc + 1) * 128], ident, ident[:, :],
                 inc=(j == j4 - 1))
        for (A, B), hT in zip(ABs, outs):
            k.act(hT, hT[:, kc, 0:W], p, p[:, 0:W], AF.Identity, scale=(A, A[:, kc:kc + 1]), bias=(B, B[:, kc:kc + 1]))


def load_consts(k, ident_d):
    ident = k.sb("ident", [128, 128], BF16)
    k.dma("pool", ident[:, :], ident_d, writes=[ident])
    eps = k.sb("eps", [128, 1])
    k.op("dve", lambda E: E.memset(eps[:, :], EPS), writes=[eps])
    return ident, eps


def load_w(k, name, w_d, K, N, chunk=None):
    kc = K // 128
    t = k.sb(name, [128, kc, N], BF16)
    v = w_d.rearrange("(k p) n -> p k n", p=128)
    step = chunk or kc
    for c0 in range(0, kc, step):
        k.dma("pool", t[:, c0:c0 + step, :], v[:, c0:c0 + step, :], writes=[t])
    return t


def sandwich(k, pY, xrow_t, xrow_ap_fn, Gb, sc2, eps):
    ss2, rs, tmp, junk = sc2["ss2"], sc2["rs"], sc2["tmp"], sc2["junk"]
    for n2 in range(2):
        k.act(junk, junk[:, 0:512], pY[n2], pY[n2][:, :], AF.Square, scale=1.0 / 32.0, accum=(ss2, ss2[:, n2:n2 + 1]))
    k.op("dve", lambda E: E.tensor_tensor(out=rs[:, 0:1], in0=ss2[:, 0:1], in1=ss2[:, 1:2], op=ALU.add),
         reads=[ss2], writes=[rs])
    k.act(rs, rs[:, 1:2], rs, rs[:, 0:1], AF.Ln, bias=(eps, eps[:, 0:1]))
    k.act(rs, rs[:, 2:3], rs, rs[:, 1:2], AF.Exp, scale=-0.5)
    for n2 in range(2):
        sl = slice(n2 * 512, (n2 + 1) * 512)
        tm = tmp[n2]
        k.op("dve", lambda E: E.scalar_tensor_tensor(out=tm[:, :], in0=pY[n2][:, :], scalar=rs[:, 2:3], in1=Gb[:, sl],
                                                     op0=ALU.mult, op1=ALU.mult), reads=[pY[n2], rs, Gb], writes=[tm])
        xa = xrow_ap_fn(sl)
        k.op("pool", lambda E: E.tensor_tensor(out=xa, in0=xa, in1=tm[:, :], op=ALU.add),
             reads=[xrow_t, tm], writes=[xrow_t])


def emit_postmix(k, T, x_d, mix_parts, x1_d, cT_d, modw_d, modb_cols_d, modb_rows_d, ng_cols_d, ng_rows_d,
                 wo_d, wout_d, ident_d, gate):
    NT = T // 512
    with k.scope():
        ident, eps = load_consts(k, ident_d)
        pT = [k.ps("pT0", [128, 1024], BF16), k.ps("pT1", [128, 1024], BF16)]
        pA = [k.ps("pA0"), k.ps("pA1")]; pY = [k.ps("pY0"), k.ps("pY1")]
        A1 = k.sb("A1", [128, 8]); mv1 = k.sb("mv1", [128, 16]); Gb1 = k.sb("Gb1", [128, 1024])
        with k.scope():
            cact = k.sb("cact", [128, 8])
            k.dma("sp", cact[:, :], cT_d, writes=[cact])
            k.act(cact, cact[:, :], cact, cact[:, :], AF.Silu)
            mw = k.sb("mw", [128, 8, 2048])
            if gate:
                ada_AB(k, cact, modw_d[:, 0:2048], modb_cols_d[:, 0:16], ng_cols_d[:, 0:8], pA[0], pA[0][:, 0:16],
                       mw, A1, mv1)
            crep = make_crep(k, cact)
            ada_rows(k, crep, modw_d[:, 2048:3072], modb_rows_d[0:1, 2048:3072], ng_rows_d[1:2, :], pY, mw, Gb1)
        wout = load_w(k, "wout", wout_d, 1024, 1024)
        wo = load_w(k, "wo", wo_d, 1024, 1024) if gate else None
        xt = [k.sb(f"xt{i}", [128, 4, 1024]) for i in range(2)]
        ht = k.sb("ht", [128, 4, 1024])
        gated = k.sb("gated", [128, 4, 1024], BF16)
        gT = k.sb("gT", [128, 8, 512], BF16)
        sc = dict(ss=k.sb("ss", [128, 4]), rstd=k.sb("rstd", [128, 4]), junk=k.sb("junk", [128, 1024], BF16),
                  xn=k.sb("xn", [128, 4, 1024], BF16), eps=eps)
        hT = k.sb("hT", [128, 8, 512], BF16) if gate else None
        sg = [k.sb(f"sg{i}", [128, 512]) for i in range(2)]
        sc2 = dict(ss2=k.sb("ss2", [128, 2]), rs=k.sb("rs", [128, 4]),
                   tmp=[k.sb("tmpa", [128, 512]), k.sb("tmpb", [128, 512])], junk=sc["junk"])

        def load(t):
            r = slice(t * 512, (t + 1) * 512)
            k.dma("sp", xt[t % 2][:, :, :], x_d[r, :].rearrange("(j p) d -> p j d", p=128), writes=[xt[t % 2]])

        load(0)
        for t in range(NT):
            r = slice(t * 512, (t + 1) * 512)
            if t + 1 < NT:
                load(t + 1)
            X = xt[t % 2]
            for i, part in enumerate(mix_parts):
                k.dma("sp", ht[:, :, i * 512:(i + 1) * 512], part[r, :].rearrange("(j p) d -> p j d", p=128), writes=[ht])
            if gate:
                norm_T(k, X, 4, [(A1, mv1)], [hT], ident, pT, sc)
                for j in range(4):
                    for n2 in range(2):
                        sl = slice(n2 * 512, (n2 + 1) * 512)
                        for kc in range(8):
                            k.mm(pA[n2], pA[n2][:, :], hT, hT[:, kc, j * 128:(j + 1) * 128], wo, wo[:, kc, sl],
                                 kc == 0, kc == 7)
                        s_ = sg[n2]
                        k.act(s_, s_[:, :], pA[n2], pA[n2][:, :], AF.Sigmoid)
                        k.op("dve", lambda E: E.tensor_tensor(out=gated[:, j, sl], in0=s_[:, :], in1=ht[:, j, sl],
                                                              op=ALU.mult), reads=[s_, ht], writes=[gated])
            else:
                for j in range(4):
                    k.op("dve", lambda E: E.tensor_copy(out=gated[:, j, :], in_=ht[:, j, :]), reads=[ht], writes=[gated])
            for c in range(8):
                p = pT[c % 2]
                for j in range(4):
                    k.tr(p, p[:, j * 128:(j + 1) * 128], gated, gated[:, j, c * 128:(c + 1) * 128], ident, ident[:, :],
                         inc=(j == 3))
                if c % 2 == 0:
                    k.act(gT, gT[:, c, :], p, p[:, 0:512], AF.Copy)
                else:
                    k.op("dve", lambda E: E.tensor_copy(out=gT[:, c, :], in_=p[:, 0:512]), reads=[p], writes=[gT])
            for j in range(4):
                for n2 in range(2):
                    sl = slice(n2 * 512, (n2 + 1) * 512)
                    for c in range(8):
                        k.mm(pY[n2], pY[n2][:, :], gT, gT[:, c, j * 128:(j + 1) * 128], wout, wout[:, c, sl],
                             c == 0, c == 7)
                sandwich(k, pY, X, lambda sl, X=X, j=j: X[:, j, sl], Gb1, sc2, eps)
            k.dma("pool", x1_d[r, :].rearrange("(j p) d -> p j d", p=128), X[:, :, :], reads=[X])


def emit_ffn(k, T, xin_d, xout_d, cT_d, modw_d, modb_cols_d, modb_rows_d, ng_cols_d, ng_rows_d, w1_d, w2_d, ident_d):
    TT = 256
    NT = T // TT
    with k.scope():
        ident, eps = load_consts(k, ident_d)
        pT = [k.ps("pT0", [128, 1024], BF16), k.ps("pT1", [128, 1024], BF16)]
        pU = [k.ps("pU0"), k.ps("pU1")]; pY = [k.ps("pY0"), k.ps("pY1")]
        A3 = k.sb("A3", [128, 8]); mv3 = k.sb("mv3", [128, 16]); Gb2 = k.sb("Gb2", [128, 1024])
        with k.scope():
            cact = k.sb("cact", [128, 8])
            k.dma("sp", cact[:, :], cT_d, writes=[cact])
            k.act(cact, cact[:, :], cact, cact[:, :], AF.Silu)
            mw = k.sb("mw", [128, 8, 2048])
            ada_AB(k, cact, modw_d[:, 3072:5120], modb_cols_d[:, 24:40], ng_cols_d[:, 16:24], pU[0], pU[0][:, 0:16],
                   mw, A3, mv3)
            crep = make_crep(k, cact)
            ada_rows(k, crep, modw_d[:, 5120:6144], modb_rows_d[0:1, 5120:6144], ng_rows_d[3:4, :], pY, mw, Gb2)
        W1 = load_w(k, "W1", w1_d, 1024, 4096, chunk=1)
        W2 = load_w(k, "W2", w2_d, 4096, 1024, chunk=4)
        xt = [k.sb(f"xt{i}", [128, 2, 1024]) for i in range(2)]
        sc = dict(ss=k.sb("ss", [128, 4]), rstd=k.sb("rstd", [128, 4]), junk=k.sb("junk", [128, 1024], BF16),
                  xn=k.sb("xn", [128, 2, 1024], BF16), eps=eps)
        hT = [k.sb(f"hT{i}", [128, 8, TT], BF16) for i in range(2)]
        uT = k.sb("uT", [128, 32, TT], BF16)
        rr = [k.sb(f"rr{i}", [128, TT]) for i in range(2)]
        sc2 = dict(ss2=k.sb("ss2", [128, 2]), rs=k.sb("rs", [128, 4]),
                   tmp=[k.sb("tmpa", [128, 512]), k.sb("tmpb", [128, 512])], junk=sc["junk"])

        def load(t):
            r = slice(t * TT, (t + 1) * TT)
            k.dma("sp", xt[t % 2][:, :, :], xin_d[r, :].rearrange("(j p) d -> p j d", p=128), writes=[xt[t % 2]])

        load(0)
        norm_T(k, xt[0], 2, [(A3, mv3)], [hT[0]], ident, pT, sc)
        for t in range(NT):
            r = slice(t * TT, (t + 1) * TT)
            X = xt[t % 2]; H = hT[t % 2]
            if t + 1 < NT:
                load(t + 1)
            for fc in range(32):
                p = pU[fc % 2]
                for kc in range(8):
                    k.mm(p, p[:, 0:TT], W1, W1[:, kc, fc * 128:(fc + 1) * 128], H, H[:, kc, :], kc == 0, kc == 7)
                r_ = rr[fc % 2]
                k.act(r_, r_[:, :], p, p[:, 0:TT], AF.Relu)
                k.op("dve", lambda E: E.tensor_tensor(out=uT[:, fc, :], in0=r_[:, :], in1=r_[:, :], op=ALU.mult),
                     reads=[r_], writes=[uT])
            if t + 1 < NT:
                norm_T(k, xt[(t + 1) % 2], 2, [(A3, mv3)], [hT[(t + 1) % 2]], ident, pT, sc)
            for j in range(2):
                for n2 in range(2):
                    sl = slice(n2 * 512, (n2 + 1) * 512)
                    for fc in range(32):
                        k.mm(pY[n2], pY[n2][:, :], uT, uT[:, fc, j * 128:(j + 1) * 128], W2, W2[:, fc, sl],
                             fc == 0, fc == 31)
                sandwich(k, pY, X, lambda sl, X=X, j=j: X[:, j, sl], Gb2, sc2, eps)
            k.dma("pool", xout_d[r, :].rearrange("(j p) d -> p j d", p=128), X[:, :, :], reads=[X])


def rope_sincos(k, tt, ti, tf, out_sc, scale_ap, bias_ap, shape_ap):
    a = shape_ap
    k.op("dve", lambda E: E.tensor_copy(out=a(ti), in_=a(tt)), reads=[tt], writes=[ti])
    k.op("dve", lambda E: E.tensor_copy(out=a(tf), in_=a(ti)), reads=[ti], writes=[tf])
    k.op("dve", lambda E: E.tensor_tensor(out=a(tf), in0=a(tt), in1=a(tf), op=ALU.subtract), reads=[tt, tf], writes=[tf])
    k.op("dve", lambda E: E.scalar_tensor_tensor(out=a(tt), in0=a(tf), scalar=0.0, in1=a(tf), op0=ALU.is_lt, op1=ALU.add),
         reads=[tf], writes=[tt])
    k.act(out_sc, a(out_sc), tt, a(tt), AF.Sin, scale=scale_ap, bias=bias_ap)


def emit_l1prep(k, T, x_d, posT_d, qlatT_d, ckvT_d, kropeT_d, cT_d, modw_d, modb_cols_d, ng_cols_d,
                kvmodw_d, kvmodb_cols_d, kvng_cols_d, wqa_d, wdkv_d, qn_row_d, kvn_row_d, invb_d, ident_d):
    NT = T // 512
    with k.scope():
        ident, eps = load_consts(k, ident_d)
        pT = [k.ps("pT0", [128, 1024], BF16), k.ps("pT1", [128, 1024], BF16)]
        pQ = k.ps("pQ"); pKV = k.ps("pKV"); pm = k.ps("pm")
        Aq = k.sb("Aq", [128, 8]); mvq = k.sb("mvq", [128, 16]); Akv = k.sb("Akv", [128, 8]); mvkv = k.sb("mvkv", [128, 16])
        with k.scope():
            cact = k.sb("cact", [128, 8])
            k.dma("sp", cact[:, :], cT_d, writes=[cact])
            k.act(cact, cact[:, :], cact, cact[:, :], AF.Silu)
            mw = k.sb("mw", [128, 8, 2048])
            ada_AB(k, cact, modw_d[:, 0:2048], modb_cols_d[:, 0:16], ng_cols_d[:, 0:8], pm, pm[:, 0:16], mw, Aq, mvq)
            ada_AB(k, cact, kvmodw_d, kvmodb_cols_d, kvng_cols_d, pm, pm[:, 16:32], mw, Akv, mvkv)
        wqa = load_w(k, "wqa", wqa_d, 1024, 384)
        wdkv = load_w(k, "wdkv", wdkv_d, 1024, 320)
        qnb = k.sb("qnb", [128, 384]); kvnb = k.sb("kvnb", [128, 256]); invb = k.sb("invb", [128, 32])
        k.dma("sp", qnb[:, :], bc(qn_row_d), writes=[qnb]); k.dma("sp", kvnb[:, :], bc(kvn_row_d), writes=[kvnb])
        k.dma("sp", invb[:, :], invb_d, writes=[invb])
        negpi = k.sb("negpi", [128, 1])
        k.op("dve", lambda E: E.memset(negpi[:, :], -math.pi), writes=[negpi])
        xt = [k.sb(f"xt{i}", [128, 4, 1024]) for i in range(2)]
        sc = dict(ss=k.sb("ss", [128, 4]), rstd=k.sb("rstd", [128, 4]), junk=k.sb("junk", [128, 1024], BF16),
                  xn=k.sb("xn", [128, 4, 1024], BF16), eps=eps)
        h1T = k.sb("h1T", [128, 8, 512], BF16); hsT = k.sb("hsT", [128, 8, 512], BF16)
        posi = k.sb("posi", [128, 4], I32); posf = k.sb("posf", [128, 4])
        tt = k.sb("tt", [128, 2, 4, 32]); ti = k.sb("ti", [128, 2, 4, 32], I32); tf = k.sb("tf", [128, 2, 4, 32])
        scs = k.sb("scs", [128, 2, 4, 32])
        st = k.sb("st", [128, 8]); junk2 = k.sb("junk2", [128, 384], BF16)
        qn = k.sb("qn", [128, 384], BF16); cn = k.sb("cn", [128, 256], BF16); kr = k.sb("kr", [128, 64], BF16)
        r1 = k.sb("r1", [128, 64]); r2 = k.sb("r2", [128, 64])
        qlT = k.sb("qlT", [128, 3, 512], BF16); ckT = k.sb("ckT", [128, 2, 512], BF16); krT = k.sb("krT", [64, 512], BF16)

        def load(t):
            r = slice(t * 512, (t + 1) * 512)
            k.dma("sp", xt[t % 2][:, :, :], x_d[r, :].rearrange("(j p) d -> p j d", p=128), writes=[xt[t % 2]])

        full = lambda T_: T_[:, :, :, :]
        load(0)
        for t in range(NT):
            r = slice(t * 512, (t + 1) * 512)
            if t + 1 < NT:
                load(t + 1)
            X = xt[t % 2]
            k.dma("sp", posi[:, :], posT_d[:, t * 4:(t + 1) * 4], writes=[posi])
            k.op("dve", lambda E: E.tensor_copy(out=posf[:, :], in_=posi[:, :]), reads=[posi], writes=[posf])
            for j in range(4):
                k.op("dve", lambda E: E.tensor_scalar(out=tt[:, 0, j, :], in0=invb[:, :], scalar1=posf[:, j:j + 1],
                                                      scalar2=0.5, op0=ALU.mult, op1=ALU.add), reads=[invb, posf], writes=[tt])
            k.op("dve", lambda E: E.tensor_scalar(out=tt[:, 1, :, :], in0=tt[:, 0, :, :], scalar1=0.25, scalar2=None,
                                                  op0=ALU.add), reads=[tt], writes=[tt])
            rope_sincos(k, tt, ti, tf, scs, TWO_PI, (negpi, negpi[:, 0:1]), full)
            norm_T(k, X, 4, [(Aq, mvq), (Akv, mvkv)], [h1T, hsT], ident, pT, sc)
            for j in range(4):
                js = slice(j * 128, (j + 1) * 128)
                for kc in range(8):
                    k.mm(pQ, pQ[:, 0:384], h1T, h1T[:, kc, js], wqa, wqa[:, kc, :], kc == 0, kc == 7)
                for kc in range(8):
                    k.mm(pKV, pKV[:, 0:320], hsT, hsT[:, kc, js], wdkv, wdkv[:, kc, :], kc == 0, kc == 7)
                k.act(junk2, junk2[:, 0:384], pQ, pQ[:, 0:384], AF.Square, scale=384.0 ** -0.5, accum=(st, st[:, 0:1]))
                k.act(junk2, junk2[:, 0:256], pKV, pKV[:, 0:256], AF.Square, scale=1.0 / 16.0, accum=(st, st[:, 1:2]))
                k.act(st, st[:, 2:4], st, st[:, 0:2], AF.Ln, bias=(eps, eps[:, 0:1]))
                k.act(st, st[:, 4:6], st, st[:, 2:4], AF.Exp, scale=-0.5)
                k.op("dve", lambda E: E.scalar_tensor_tensor(out=qn[:, :], in0=pQ[:, 0:384], scalar=st[:, 4:5], in1=qnb[:, :],
                                                             op0=ALU.mult, op1=ALU.mult), reads=[pQ, st, qnb], writes=[qn])
                k.op("dve", lambda E: E.scalar_tensor_tensor(out=cn[:, :], in0=pKV[:, 0:256], scalar=st[:, 5:6], in1=kvnb[:, :],
                                                             op0=ALU.mult, op1=ALU.mult), reads=[pKV, st, kvnb], writes=[cn])
                sin_ = scs[:, 0, j, :]; cos_ = scs[:, 1, j, :]
                k.act(r1, r1[:, :], pKV, pKV[:, 256:320], AF.Copy)
                k.op("dve", lambda E: E.tensor_tensor(out=r2[:, 0:32], in0=r1[:, 32:64], in1=sin_, op=ALU.mult), reads=[r1, scs], writes=[r2])
                k.op("dve", lambda E: E.tensor_tensor(out=r2[:, 32:64], in0=r1[:, 0:32], in1=sin_, op=ALU.mult), reads=[r1, scs], writes=[r2])
                k.op("pool", lambda E: E.tensor_tensor(out=r1[:, 0:32], in0=r1[:, 0:32], in1=cos_, op=ALU.mult), reads=[r1, scs, r2], writes=[r1])
                k.op("pool", lambda E: E.tensor_tensor(out=r1[:, 32:64], in0=r1[:, 32:64], in1=cos_, op=ALU.mult), reads=[r1, scs], writes=[r1])
                k.op("dve", lambda E: E.tensor_tensor(out=kr[:, 0:32], in0=r1[:, 0:32], in1=r2[:, 0:32], op=ALU.subtract), reads=[r1, r2], writes=[kr])
                k.op("dve", lambda E: E.tensor_tensor(out=kr[:, 32:64], in0=r1[:, 32:64], in1=r2[:, 32:64], op=ALU.add), reads=[r1, r2], writes=[kr])
                p = pT[j % 2]
                for c in range(3):
                    k.tr(p, p[:, c * 128:(c + 1) * 128], qn, qn[:, c * 128:(c + 1) * 128], ident, ident[:, :], inc=False)
                for c in range(2):
                    k.tr(p, p[:, (3 + c) * 128:(4 + c) * 128], cn, cn[:, c * 128:(c + 1) * 128], ident, ident[:, :], inc=False)
                k.tr(p, p[0:64, 640:768], kr, kr[:, :], ident, ident[:, :], inc=True)
                k.act(qlT, qlT[:, :, js], p, p[:, 0:384].rearrange("p (c t) -> p c t", c=3), AF.Copy)
                k.op("dve", lambda E: E.tensor_copy(out=ckT[:, :, js], in_=p[:, 384:640].rearrange("p (c t) -> p c t", c=2)),
                     reads=[p], writes=[ckT])
                k.op("dve", lambda E: E.tensor_copy(out=krT[:, js], in_=p[0:64, 640:768]), reads=[p], writes=[krT])
            k.dma("pool", qlatT_d.rearrange("(c p) t -> p c t", p=128)[:, :, r], qlT[:, :, :], reads=[qlT])
            k.dma("pool", ckvT_d.rearrange("(c p) t -> p c t", p=128)[:, :, r], ckT[:, :, :], reads=[ckT])
            k.dma("pool", kropeT_d[:, r], krT[:, :], reads=[krT])


def emit_attn(k, S, qlatT_d, ckvT_d, kropeT_d, pos_row_d, wqb_d, wqbs_d, wuk_d, wuv_d, inv2_d, sgn_d, maskd_d, out_d):
    NQ = S // 512
    NKB = S // 128
    SCALE = 192.0 ** -0.5
    with k.scope():
        pS = [k.ps("pS0"), k.ps("pS1")]
        pAcc = [k.ps("pAcc0"), k.ps("pAcc1")]
        pQ = [k.ps("pQ0"), k.ps("pQ1")]
        wqb = load_w(k, "wqb", wqb_d, 384, 768); wqbs = load_w(k, "wqbs", wqbs_d, 384, 256)
        wuk = load_w(k, "wuk", wuk_d, 256, 512); wuv = load_w(k, "wuv", wuv_d, 256, 512)
        krT = k.sb("krT", [64, S], BF16)
        k.dma("sp", krT[:, :], kropeT_d, writes=[krT])
        knT = k.sb("knT", [128, 4, S], BF16)
        vext = k.sb("vext", [128, NKB, 4, 129], BF16)
        k.op("pool", lambda E: E.memset(vext[:, :, :, 128:129], 1.0), writes=[vext])
        maskd = k.sb("maskd", [128, 4, 512], BF16)
        k.dma("pool", maskd[:, :, :], maskd_d.rearrange("m p q -> p m q"), writes=[maskd])
        inv2 = k.sb("inv2", [64, 1]); sgn = k.sb("sgn", [64, 2])
        k.dma("sp", inv2[:, :], inv2_d, writes=[inv2]); k.dma("sp", sgn[:, :], sgn_d, writes=[sgn])
        with k.scope():
            ckT = k.sb("ckT", [128, 2, S], BF16)
            k.dma("sp", ckT[:, :, :], ckvT_d.rearrange("(c p) t -> p c t", p=128), writes=[ckT])
            for tq in range(NQ):
                ts = slice(tq * 512, (tq + 1) * 512)
                for h in range(4):
                    p = pQ[h % 2]
                    for c in range(2):
                        k.mm(p, p[:, :], wuk, wuk[:, c, h * 128:(h + 1) * 128], ckT, ckT[:, c, ts], c == 0, c == 1)
                    if h % 2 == 0:
                        k.act(knT, knT[:, h, ts], p, p[:, :], AF.Copy)
                    else:
                        k.op("dve", lambda E: E.tensor_copy(out=knT[:, h, ts], in_=p[:, :]), reads=[p], writes=[knT])
            for kb in range(NKB):
                p = pS[kb % 2]
                for c in range(2):
                    k.mm(p, p[:, :], ckT, ckT[:, c, kb * 128:(kb + 1) * 128], wuv, wuv[:, c, :], c == 0, c == 1)
                if kb % 2 == 0:
                    k.act(vext, vext[:, kb, :, 0:128], p, p[:, :].rearrange("p (h e) -> p h e", h=4), AF.Copy)
                else:
                    k.op("dve", lambda E: E.tensor_copy(out=vext[:, kb, :, 0:128],
                                                        in_=p[:, :].rearrange("p (h e) -> p h e", h=4)),
                         reads=[p], writes=[vext])
        qlT = [k.sb(f"qlT{i}", [128, 3, 512], BF16) for i in range(2)]
        posi = k.sb("posi", [64, 512], I32)
        tt = k.sb("tt", [64, 2, 512]); ti = k.sb("ti", [64, 2, 512], I32); tf = k.sb("tf", [64, 2, 512])
        cs2 = k.sb("cs2", [64, 2, 512])
        qnT = [k.sb(f"qnT{i}", [128, 512], BF16) for i in range(2)]
        qrT = [k.sb(f"qrT{i}", [64, 512], BF16) for i in range(2)]
        ra = k.sb("ra", [64, 512]); rb = k.sb("rb", [64, 512])
        PTb = [k.sb(f"PTb{i}", [128, 512], BF16) for i in range(3)]
        ot = [k.sb(f"ot{i}", [128, 4, 512]) for i in range(1)]
        rec = k.sb("rec", [128, 4])
        negpi = k.sb("negpi", [64, 1]); twopi = k.sb("twopi", [64, 1])
        k.op("dve", lambda E: E.memset(negpi[:, :], -math.pi), writes=[negpi])
        k.op("dve", lambda E: E.memset(twopi[:, :], TWO_PI), writes=[twopi])
        qv = qlatT_d.rearrange("(c p) t -> p c t", p=128)
        unit = 0
        for tq in range(NQ):
            q0 = tq * 512
            ts = slice(q0, q0 + 512)
            QL = qlT[tq % 2]
            k.dma("sp", QL[:, :, :], qv[:, :, ts], writes=[QL])
            k.dma("sp", posi[:, :], bc(pos_row_d[0:1, ts], 64), writes=[posi])
            k.op("dve", lambda E: E.tensor_copy(out=tf[:, 0, :], in_=posi[:, :]), reads=[posi], writes=[tf])
            k.op("dve", lambda E: E.tensor_scalar(out=tt[:, 0, :], in0=tf[:, 0, :], scalar1=inv2[:, 0:1], scalar2=0.5,
                                                  op0=ALU.mult, op1=ALU.add), reads=[tf, inv2], writes=[tt])
            k.op("dve", lambda E: E.tensor_scalar(out=tt[:, 1, :], in0=tt[:, 0, :], scalar1=0.25, scalar2=None,
                                                  op0=ALU.add), reads=[tt], writes=[tt])
            a3 = lambda T_: T_[:, :, :]
            k.op("dve", lambda E: E.tensor_copy(out=a3(ti), in_=a3(tt)), reads=[tt], writes=[ti])
            k.op("dve", lambda E: E.tensor_copy(out=a3(tf), in_=a3(ti)), reads=[ti], writes=[tf])
            k.op("dve", lambda E: E.tensor_tensor(out=a3(tf), in0=a3(tt), in1=a3(tf), op=ALU.subtract), reads=[tt, tf], writes=[tf])
            k.op("dve", lambda E: E.scalar_tensor_tensor(out=a3(tt), in0=a3(tf), scalar=0.0, in1=a3(tf), op0=ALU.is_lt,
                                                         op1=ALU.add), reads=[tf], writes=[tt])
            k.act(cs2, cs2[:, 0, :], tt, tt[:, 0, :], AF.Sin, scale=(sgn, sgn[:, 0:1]), bias=(sgn, sgn[:, 1:2]))
            k.act(cs2, cs2[:, 1, :], tt, tt[:, 1, :], AF.Sin, scale=(twopi, twopi[:, 0:1]), bias=(negpi, negpi[:, 0:1]))
            O = ot[0]
            nkb = (q0 + 512) // 128
            for h in range(4):
                QN = qnT[h % 2]; QR = qrT[h % 2]
                p = pQ[0]
                for c in range(3):
                    k.mm(p, p[:, :], wqb, wqb[:, c, h * 192:h * 192 + 128], QL, QL[:, c, :], c == 0, c == 2)
                k.act(QN, QN[:, :], p, p[:, :], AF.Copy)
                p = pQ[1]
                for c in range(3):
                    k.mm(p, p[0:64, :], wqb, wqb[:, c, h * 192 + 128:h * 192 + 192], QL, QL[:, c, :], c == 0, c == 2)
                k.op("dve", lambda E: E.tensor_tensor(out=ra[:, :], in0=p[0:64, :], in1=cs2[:, 1, :], op=ALU.mult),
                     reads=[p, cs2], writes=[ra])
                for c in range(3):
                    k.mm(p, p[0:64, :], wqbs, wqbs[:, c, h * 64:(h + 1) * 64], QL, QL[:, c, :], c == 0, c == 2)
                k.op("dve", lambda E: E.tensor_tensor(out=rb[:, :], in0=p[0:64, :], in1=cs2[:, 0, :], op=ALU.mult),
                     reads=[p, cs2], writes=[rb])
                k.op("pool", lambda E: E.tensor_tensor(out=QR[:, :], in0=ra[:, :], in1=rb[:, :], op=ALU.add),
                     reads=[ra, rb], writes=[QR])
                for kb in range(nkb):
                    k0 = kb * 128
                    ks = slice(k0, k0 + 128)
                    ps_ = pS[unit % 2]; PB = PTb[unit % 3]
                    unit += 1
                    k.mm(ps_, ps_[:, :], knT, knT[:, h, ks], QN, QN[:, :], True, False)
                    k.mm(ps_, ps_[:, :], krT, krT[:, ks], QR, QR[:, :], False, True)
                    k.act(PB, PB[:, :], ps_, ps_[:, :], AF.Exp, scale=SCALE)
                    if k0 >= q0:
                        m = (k0 - q0) // 128
                        k.op("dve", lambda E: E.tensor_tensor(out=PB[:, :], in0=PB[:, :], in1=maskd[:, m, :], op=ALU.mult),
                             reads=[PB, maskd], writes=[PB])
                    for qs in range(4):
                        if k0 > q0 + qs * 128 + 127:
                            continue
                        acc = pAcc[qs // 2]
                        a_ap = acc[:, (qs % 2) * 129:(qs % 2) * 129 + 129]
                        first = (kb == 0)
                        last = (kb == (q0 + qs * 128) // 128)
                        k.op("pe", lambda E: E.matmul(a_ap, PB[:, qs * 128:(qs + 1) * 128], vext[:, kb, h, :],
                                                      start=(first and qs % 2 == 0), stop=last, skip_group_check=True),
                             reads=[PB, vext], writes=[acc], inc=True)
                for qs in range(4):
                    acc = pAcc[qs // 2]
                    a_ap = acc[:, (qs % 2) * 129:(qs % 2) * 129 + 129]
                    k.op("dve", lambda E: E.reciprocal(out=rec[:, qs:qs + 1], in_=a_ap[:, 128:129]), reads=[acc], writes=[rec])
                    k.op("dve", lambda E: E.tensor_scalar(out=O[:, qs, h * 128:(h + 1) * 128], in0=a_ap[:, 0:128],
                                                          scalar1=rec[:, qs:qs + 1], scalar2=None, op0=ALU.mult),
                         reads=[acc, rec], writes=[O])
            k.dma("pool", out_d[ts, :].rearrange("(j p) d -> p j d", p=128), O[:, :, :], reads=[O])


EPS = 1e-6


def fm(v):
    v = np.asarray(v, np.float32)
    return np.ascontiguousarray(v.reshape(-1, 128).T)


def adaln_cols(k, cact, mw_dram, ncols, modb_sb, pm, pm_ap, mw, name):
    noc = ncols // 128
    for kc in range(8):
        k.dma("sp", mw[:, kc, 0:ncols], mw_dram[kc * 128:(kc + 1) * 128, :], writes=[mw])
    for oc in range(noc):
        for kc in range(8):
            k.mm(pm, pm_ap[:, oc:oc + 1], mw, mw[:, kc, oc * 128:(oc + 1) * 128], cact, cact[:, kc:kc + 1],
                 kc == 0, kc == 7)
    modv = k.sb(name, [128, noc])
    k.op("dve", lambda E: E.tensor_tensor(out=modv[:, :], in0=pm_ap[:, 0:noc], in1=modb_sb[:, 0:noc], op=ALU.add),
         reads=[pm, modb_sb], writes=[modv])
    return modv


def rows_to_T(k, xt, j4, A, B, ident, pT, hT, sc):
    ss, rstd, junk, xn, eps = sc["ss"], sc["rstd"], sc["junk"], sc["xn"], sc["eps"]
    for j in range(j4):
        k.act(junk, junk[:, :], xt, xt[:, j, :], AF.Square, scale=1.0 / 32.0, accum=(ss, ss[:, j:j + 1]))
    k.act(rstd, rstd[:, 0:j4], ss, ss[:, 0:j4], AF.Ln, bias=(eps, eps[:, 0:1]))
    k.act(rstd, rstd[:, 0:j4], rstd, rstd[:, 0:j4], AF.Exp, scale=-0.5)
    for j in range(j4):
        k.op("dve", lambda E: E.tensor_scalar(out=xn[:, j, :], in0=xt[:, j, :], scalar1=rstd[:, j:j + 1],
                                              scalar2=None, op0=ALU.mult),
             reads=[xt, rstd], writes=[xn])
    for kc in range(8):
        p = pT[kc % 2]
        for j in range(j4):
            k.tr(p, p[:, j * 128:(j + 1) * 128], xn, xn[:, j, kc * 128:(kc + 1) * 128], ident, ident[:, :],
                 inc=(j == j4 - 1))
        k.act(hT, hT[:, kc, 0:j4 * 128], p, p[:, 0:j4 * 128], AF.Identity,
              scale=(A, A[:, kc:kc + 1]), bias=(B, B[:, kc:kc + 1]))


def build_l1(S):
    k = KB()
    NT = S // 512
    x = k.dram_in("x", [S, 1024]); cT = k.dram_in("cT", [128, 8]); modw = k.dram_in("modw", [1024, 2048])
    modb = k.dram_in("modb", [128, 16]); ng = k.dram_in("ng", [128, 8])
    wq = k.dram_in("wq", [1024, 256]); wk = k.dram_in("wk", [1024, 256]); wv = k.dram_in("wv", [1024, 512])
    wg = k.dram_in("wg", [1024, 4]); bgb = k.dram_in("bgb", [64, 32]); hnb = k.dram_in("hnb", [64, 512])
    ident_d = k.dram_in("ident", [128, 128]); tri_d = k.dram_in("tri", [64, 64])
    hh = k.dram_out("hh", [S, 512])

    ident = k.sb("ident", [128, 128], BF16); tri = k.sb("tri", [64, 64]); ones = k.sb("ones", [64, 128])
    bgb_sb = k.sb("bgb_sb", [64, 32]); hnb_sb = k.sb("hnb_sb", [64, 512])
    eps = k.sb("eps", [128, 1]); one_c = k.sb("one_c", [128, 1]); lnsc = k.sb("lnsc", [128, 1])
    cact = k.sb("cact", [128, 8]); modb_sb = k.sb("modb_sb", [128, 16]); ng_sb = k.sb("ng_sb", [128, 8])
    k.dma("pool", ident[:, :], ident_d, writes=[ident])
    k.dma("sp", tri[:, :], tri_d, writes=[tri])
    k.dma("sp", bgb_sb[:, :], bgb, writes=[bgb_sb]); k.dma("sp", hnb_sb[:, :], hnb, writes=[hnb_sb])
    k.dma("sp", cact[:, :], cT, writes=[cact]); k.dma("sp", modb_sb[:, :], modb, writes=[modb_sb])
    k.dma("sp", ng_sb[:, :], ng, writes=[ng_sb])
    k.op("dve", lambda E: E.memset(ones[:, :], 1.0), writes=[ones])
    k.op("dve", lambda E: E.memset(eps[:, :], EPS), writes=[eps])
    k.op("dve", lambda E: E.memset(one_c[:, :], 1.0), writes=[one_c])
    k.op("dve", lambda E: E.memset(lnsc[:, :], math.log(128.0 ** -0.5)), writes=[lnsc])
    wq_sb = k.sb("wq_sb", [128, 8, 256], BF16); wk_sb = k.sb("wk_sb", [128, 8, 256], BF16)
    wv_sb = k.sb("wv_sb", [128, 8, 512], BF16); wg_sb = k.sb("wg_sb", [128, 8, 4], BF16)
    for (t, d) in ((wq_sb, wq), (wk_sb, wk), (wv_sb, wv), (wg_sb, wg)):
        k.dma("pool", t[:, :, :], d.rearrange("(k p) n -> p k n", p=128), writes=[t])

    pT = [k.ps("pT0", [128, 1024], BF16), k.ps("pT1", [128, 1024], BF16)]
    pqk = k.ps("pqk"); pmisc = k.ps("pmisc")
    pN = [k.ps("pN0"), k.ps("pN1")]; pD = [k.ps("pD0"), k.ps("pD1")]
    pk_ap = pmisc[0:64, 0:256]; pg_ap = pmisc[0:64, 256:288]; pcs_ap = pmisc[0:64, 288:304]
    ptot_ap = pmisc[:, 304:320]; pmod_ap = pmisc[:, 320:336]

    k.act(cact, cact[:, :], cact, cact[:, :], AF.Silu)
    mw = k.sb("mw", [128, 8, 2048])
    modv = adaln_cols(k, cact, modw, 2048, modb_sb, pmisc, pmod_ap, mw, "modv")
    A1 = k.sb("A1", [128, 8])
    k.op("dve", lambda E: E.scalar_tensor_tensor(out=A1[:, :], in0=modv[:, 8:16], scalar=1.0, in1=ng_sb[:, :],
                                                 op0=ALU.add, op1=ALU.mult), reads=[modv, ng_sb], writes=[A1])
    B1 = modv

    xt = [k.sb(f"xt{i}", [128, 4, 1024]) for i in range(2)]
    sc = dict(ss=k.sb("ss", [128, 4]), rstd=k.sb("rstd", [128, 4]), junk=k.sb("junk", [128, 1024], BF16),
              xn=k.sb("xn", [128, 4, 1024], BF16), eps=eps)
    hT = k.sb("hT", [128, 8, 512], BF16)
    qT = k.sb("qT", [128, 2, 512], BF16); kT = k.sb("kT", [128, 2, 512], BF16)
    G = k.sb("G", [64, 8, 4]); e1 = k.sb("e1", [64, 16]); sp_ = k.sb("sp_", [64, 16])
    t1 = k.sb("t1", [64, 16]); t2 = k.sb("t2", [64, 16])
    Aa = k.sb("Aa", [64, 16]); KK = k.sb("KK", [64, 16]); EB = k.sb("EB", [64, 16]); Gt = k.sb("Gt", [128, 16])
    vext = [k.sb(f"vext{i}", [64, 2, 257], BF16) for i in range(8)]
    kp = [k.sb(f"kp{i}", [64, 2, 128], BF16) for i in range(8)]
    PT = [[k.sb(f"PT{h}_{i}", [64, 64], BF16) for i in range(2)] for h in range(2)]
    sm = [[k.sb(f"sm{h}_{i}", [64, 16]) for i in range(2)] for h in range(2)]
    junk64 = k.sb("junk64", [64, 256], BF16)
    ho = [k.sb(f"ho{i}", [64, 2, 256]) for i in range(3)]
    C32 = [k.sb(f"C32_{h}", [128, 257]) for h in range(2)]
    Cb = [k.sb(f"Cb_{h}", [128, 257], BF16) for h in range(2)]
    for h in range(2):
        k.op("dve", lambda E: E.memset(C32[h][:, :], 0.0), writes=[C32[h]])
        k.op("dve", lambda E: E.memset(Cb[h][:, :], 0.0), writes=[Cb[h]])
    for i in range(8):
        k.op("dve", lambda E: E.memset(vext[i][:, :, 256:257], 1.0), writes=[vext[i]])

    def load_x(t):
        k.dma("sp", xt[t % 2][:, :, :], x[t * 512:(t + 1) * 512, :].rearrange("(j p) d -> p j d", p=128),
              writes=[xt[t % 2]])

    load_x(0)
    for t in range(NT):
        if t + 1 < NT:
            load_x(t + 1)
        rows_to_T(k, xt[t % 2], 4, A1, B1, ident, pT, hT, sc)
        for (dst, w) in ((qT, wq_sb), (kT, wk_sb)):
            for h in range(2):
                for kc in range(8):
                    k.mm(pqk, pqk[:, :], w, w[:, kc, h * 128:(h + 1) * 128], hT, hT[:, kc, :], kc == 0, kc == 7)
                k.act(dst, dst[:, h, :], pqk, pqk[:, :], AF.Copy)
        for c in range(8):
            for kc in range(8):
                k.mm(pmisc, pmisc[0:64, 256 + c * 4:260 + c * 4], hT, hT[:, kc, c * 64:(c + 1) * 64],
                     wg_sb, wg_sb[:, kc, :], kc == 0, kc == 7, inc=(kc == 7 and c == 7))
        k.op("dve", lambda E: E.tensor_tensor(out=G[:, :, :], in0=pg_ap.rearrange("p (c g) -> p c g", g=4),
                                              in1=bgb_sb[:, :].rearrange("p (c g) -> p c g", g=4), op=ALU.add),
             reads=[pmisc, bgb_sb], writes=[G])
        k.act(e1, e1[:, :].rearrange("p (c g) -> p c g", g=2), G, G[:, :, 2:4], AF.Exp, scale=-1.0)
        k.act(sp_, sp_[:, :], e1, e1[:, :], AF.Ln, bias=(one_c, one_c[0:64, 0:1]))
        k.mm(pmisc, pcs_ap, tri, tri[:, :], sp_, sp_[:, :], True, True)
        k.mm(pmisc, ptot_ap, ones, ones[:, :], sp_, sp_[:, :], True, True)
        k.act(EB, EB[:, :], pmisc, pcs_ap, AF.Exp, scale=-1.0, bias=(lnsc, lnsc[0:64, 0:1]))
        k.op("dve", lambda E: E.tensor_tensor(out=t1[:, :].rearrange("p (c g) -> p c g", g=2), in0=G[:, :, 0:2],
                                              in1=pcs_ap.rearrange("p (c g) -> p c g", g=2), op=ALU.add),
             reads=[G, pmisc], writes=[t1])
        k.act(Aa, Aa[:, :], t1, t1[:, :], AF.Exp)
        k.op("dve", lambda E: E.tensor_tensor(out=t2[:, :], in0=t1[:, :], in1=pmisc[0:64, 304:320], op=ALU.subtract),
             reads=[t1, pmisc], writes=[t2])
        k.act(KK, KK[:, :], t2, t2[:, :], AF.Exp)
        k.act(Gt, Gt[:, :], pmisc, ptot_ap, AF.Exp, scale=-1.0)
        for c in range(8):
            for kc in range(8):
                k.mm(pqk, pqk[0:64, :], hT, hT[:, kc, c * 64:(c + 1) * 64], wv_sb, wv_sb[:, kc, :], kc == 0, kc == 7)
            for kc in range(8):
                k.mm(pmisc, pk_ap, hT, hT[:, kc, c * 64:(c + 1) * 64], wk_sb, wk_sb[:, kc, :], kc == 0, kc == 7)
            k.act(vext[c], vext[c][:, :, 0:256], pqk, pqk[0:64, :].rearrange("p (h e) -> p h e", h=2), AF.Copy)
            for h in range(2):
                idx = c * 2 + h
                k.op("dve", lambda E: E.tensor_scalar(out=kp[c][:, h, :], in0=pmisc[0:64, h * 128:(h + 1) * 128],
                                                      scalar1=KK[:, idx:idx + 1], scalar2=None, op0=ALU.mult),
                     reads=[pmisc, KK], writes=[kp[c]])
        for c in range(8):
            gc = t * 8 + c
            cs = slice(c * 64, (c + 1) * 64)
            pS = pqk
            for h in range(2):
                k.mm(pS, pS[0:64, h * 64:(h + 1) * 64], kT, kT[:, h, cs], qT, qT[:, h, cs], True, True)
            for h in range(2):
                idx = c * 2 + h
                P = PT[h][gc % 2]
                k.op("dve", lambda E: E.scalar_tensor_tensor(out=P[:, :], in0=pS[0:64, h * 64:(h + 1) * 64],
                                                             scalar=Aa[:, idx:idx + 1], in1=tri[:, :],
                                                             op0=ALU.mult, op1=ALU.mult),
                     reads=[pS, Aa, tri], writes=[P])
            for h in range(2):
                P = PT[h][gc % 2]
                k.mm(pN[h], pN[h][0:64, 0:257], qT, qT[:, h, cs], Cb[h], Cb[h][:, :], True, False)
                k.mm(pN[h], pN[h][0:64, 0:257], P, P[:, :], vext[c], vext[c][:, h, :], False, True)
                k.mm(pD[h], pD[h][:, 0:257], kp[c], kp[c][:, h, :], vext[c], vext[c][:, h, :], True, True)
            for h in range(2):
                idx = c * 2 + h
                k.op("dve", lambda E: E.scalar_tensor_tensor(out=C32[h][:, :], in0=C32[h][:, :],
                                                             scalar=Gt[:, idx:idx + 1], in1=pD[h][:, 0:257],
                                                             op0=ALU.mult, op1=ALU.add),
                     reads=[C32[h], Gt, pD[h]], writes=[C32[h]])
                k.act(Cb[h], Cb[h][:, :], C32[h], C32[h][:, :], AF.Copy)
            hob = ho[gc % 3]
            for h in range(2):
                idx = c * 2 + h
                s_ = sm[h][gc % 2]
                ebc = EB[:, idx:idx + 1]
                k.act(s_, s_[:, 0:1], pN[h], pN[h][0:64, 256:257], AF.Abs, scale=(EB, ebc))
                k.act(junk64, junk64[:, :], pN[h], pN[h][0:64, 0:256], AF.Square, accum=(s_, s_[:, 4:5]))
                k.op("dve", lambda E: E.tensor_scalar(out=s_[:, 1:2], in0=s_[:, 0:1], scalar1=1.0, scalar2=None,
                                                      op0=ALU.max), reads=[s_], writes=[s_])
                k.op("dve", lambda E: E.reciprocal(out=s_[:, 2:3], in_=s_[:, 1:2]), reads=[s_], writes=[s_])
                k.op("dve", lambda E: E.tensor_tensor(out=s_[:, 3:4], in0=s_[:, 2:3], in1=ebc, op=ALU.mult),
                     reads=[s_, EB], writes=[s_])
                k.op("dve", lambda E: E.scalar_tensor_tensor(out=s_[:, 5:6], in0=s_[:, 3:4], scalar=s_[:, 3:4],
                                                             in1=s_[:, 4:5], op0=ALU.mult, op1=ALU.mult),
                     reads=[s_], writes=[s_])
                k.act(s_, s_[:, 6:7], s_, s_[:, 5:6], AF.Ln, scale=1.0 / 256.0, bias=(eps, eps[0:64, 0:1]))
                k.act(s_, s_[:, 7:8], s_, s_[:, 6:7], AF.Exp, scale=-0.5)
                k.op("dve", lambda E: E.tensor_tensor(out=s_[:, 8:9], in0=s_[:, 7:8], in1=s_[:, 3:4], op=ALU.mult),
                     reads=[s_], writes=[s_])
                k.op("dve", lambda E: E.scalar_tensor_tensor(out=hob[:, h, :], in0=pN[h][0:64, 0:256],
                                                             scalar=s_[:, 8:9], in1=hnb_sb[:, h * 256:(h + 1) * 256],
                                                             op0=ALU.mult, op1=ALU.mult),
                     reads=[pN[h], s_, hnb_sb], writes=[hob])
            k.dma("pool", hh[t * 512 + c * 64: t * 512 + (c + 1) * 64, :], hob[:, :, :].rearrange("p h e -> p (h e)"),
                  reads=[hob])
    k.finish()
    return k


def l1_inputs(inp, b, hp, S):
    a = inp["a_w_in"][0]
    gi = [3072 + 2 * hp, 3073 + 2 * hp, 3076 + 2 * hp, 3077 + 2 * hp]
    bg = inp["a_b_gates"][0][[2 * hp, 2 * hp + 1, 4 + 2 * hp, 5 + 2 * hp]]
    return {
        "x": np.ascontiguousarray(inp["x"][b, :S]), "cT": fm(inp["c"][b]),
        "modw": np.ascontiguousarray(inp["mod_w"][0][:, 0:2048]), "modb": fm(inp["mod_b"][0][0:2048]),
        "ng": fm(inp["norm_g"][0, 0]),
        "wq": np.ascontiguousarray(a[:, hp * 256:(hp + 1) * 256]),
        "wk": np.ascontiguousarray(a[:, 512 + hp * 256:512 + (hp + 1) * 256]),
        "wv": np.ascontiguousarray(a[:, 1024 + hp * 512:1024 + (hp + 1) * 512]),
        "wg": np.ascontiguousarray(a[:, gi]),
        "bgb": np.ascontiguousarray(np.tile(bg[None, :], (64, 8))),
        "hnb": np.ascontiguousarray(np.tile(inp["a_head_norm"][0][2 * hp:2 * hp + 2].reshape(1, 512), (64, 1))),
        "ident": np.eye(128, dtype=np.float32),
        "tri": np.triu(np.ones((64, 64), np.float32)),
    }


import ml_dtypes
_INV = (10000.0 ** (-np.arange(0, 64, 2, dtype=np.float64) / 64)) / (2 * np.pi)


def _common_decl(k):
    return dict(cT=k.dram_in("cT", [128, 8]), modw=k.dram_in("modw", [1024, 6144]),
                modb_cols=k.dram_in("modb_cols", [128, 48]), modb_rows=k.dram_in("modb_rows", [1, 6144]),
                ng_cols=k.dram_in("ng_cols", [128, 32]), ng_rows=k.dram_in("ng_rows", [4, 1024]),
                ident=k.dram_in("ident", [128, 128]))


def _common_inputs(inp, b, layer):
    return {
        "cT": fm(inp["c"][b]),
        "modw": np.ascontiguousarray(inp["mod_w"][layer]),
        "modb_cols": fm(inp["mod_b"][layer]),
        "modb_rows": np.ascontiguousarray(inp["mod_b"][layer][None, :]),
        "ng_cols": np.ascontiguousarray(np.concatenate([fm(inp["norm_g"][layer, i]) for i in range(4)], axis=1)),
        "ng_rows": np.ascontiguousarray(inp["norm_g"][layer]),
        "ident": np.eye(128, dtype=np.float32),
    }


def build_l2(T):
    k = KB()
    c = _common_decl(k)
    x = k.dram_in("x", [T, 1024]); hA = k.dram_in("hA", [T, 512]); hB = k.dram_in("hB", [T, 512])
    wo = k.dram_in("wo", [1024, 1024]); wout = k.dram_in("wout", [1024, 1024])
    w1 = k.dram_in("w1", [1024, 4096]); w2 = k.dram_in("w2", [4096, 1024])
    modw1 = k.dram_in("modw1", [1024, 2048]); modb1 = k.dram_in("modb1_cols", [128, 16]); ng1 = k.dram_in("ng1_cols", [128, 8])
    posT = k.dram_in("posT", [128, T // 128], I32)
    kvmodw = k.dram_in("kvmodw", [1024, 2048]); kvmodb = k.dram_in("kvmodb_cols", [128, 16]); kvng = k.dram_in("kvng_cols", [128, 8])
    wqa = k.dram_in("wqa", [1024, 384]); wdkv = k.dram_in("wdkv", [1024, 320])
    qn_row = k.dram_in("qn_row", [1, 384]); kvn_row = k.dram_in("kvn_row", [1, 256]); invb = k.dram_in("invb", [128, 32])
    x1 = k.nc.dram_tensor("x1_scratch", [T, 1024], F32).ap()
    x2 = k.dram_out("x2", [T, 1024])
    qlatT = k.dram_out("qlatT", [384, T], BF16); ckvT = k.dram_out("ckvT", [256, T], BF16); kropeT = k.dram_out("kropeT", [64, T], BF16)
    emit_postmix(k, T, x, [hA, hB], x1, c["cT"], c["modw"], c["modb_cols"], c["modb_rows"], c["ng_cols"], c["ng_rows"],
                 wo, wout, c["ident"], gate=True)
    emit_ffn(k, T, x1, x2, c["cT"], c["modw"], c["modb_cols"], c["modb_rows"], c["ng_cols"], c["ng_rows"], w1, w2, c["ident"])
    emit_l1prep(k, T, x2, posT, qlatT, ckvT, kropeT, c["cT"], modw1, modb1, ng1,
                kvmodw, kvmodb, kvng, wqa, wdkv, qn_row, kvn_row, invb, c["ident"])
    k.finish()
    return k


def build_l3(S):
    k = KB()
    qlatT = k.dram_in("qlatT", [384, S], BF16); ckvT = k.dram_in("ckvT", [256, S], BF16); kropeT = k.dram_in("kropeT", [64, S], BF16)
    pos_row = k.dram_in("pos_row", [1, S], I32)
    wqb = k.dram_in("wqb", [384, 768]); wqbs = k.dram_in("wqbs", [384, 256]); wuk = k.dram_in("wuk", [256, 512]); wuv = k.dram_in("wuv", [256, 512])
    inv2 = k.dram_in("inv2", [64, 1]); sgn = k.dram_in("sgn", [64, 2]); maskd = k.dram_in("maskd", [4, 128, 512])
    out = k.dram_out("att", [S, 512])
    emit_attn(k, S, qlatT, ckvT, kropeT, pos_row, wqb, wqbs, wuk, wuv, inv2, sgn, maskd, out)
    k.finish()
    return k


def build_l4(T):
    k = KB()
    c = _common_decl(k)
    x = k.dram_in("x", [T, 1024]); aA = k.dram_in("aA", [T, 512]); aB = k.dram_in("aB", [T, 512])
    wout = k.dram_in("wout", [1024, 1024])
    w1 = k.dram_in("w1", [1024, 4096]); w2 = k.dram_in("w2", [4096, 1024])
    x1 = k.nc.dram_tensor("x1_scratch", [T, 1024], F32).ap()
    y = k.dram_out("y", [T, 1024])
    emit_postmix(k, T, x, [aA, aB], x1, c["cT"], c["modw"], c["modb_cols"], c["modb_rows"], c["ng_cols"], c["ng_rows"],
                 None, wout, c["ident"], gate=False)
    emit_ffn(k, T, x1, y, c["cT"], c["modw"], c["modb_cols"], c["modb_rows"], c["ng_cols"], c["ng_rows"], w1, w2, c["ident"])
    k.finish()
    return k


def _attn_consts():
    inv2 = np.concatenate([_INV, _INV]).astype(np.float32)[:, None]
    sgn = np.zeros((64, 2), np.float32)
    sgn[0:32, 0] = -2 * np.pi; sgn[0:32, 1] = np.pi; sgn[32:, 0] = 2 * np.pi; sgn[32:, 1] = -np.pi
    kk = np.arange(128)[:, None]; q = np.arange(512)[None, :]
    maskd = np.stack([(((m * 128 + kk) // 64) <= (q // 64)).astype(np.float32) for m in range(4)])
    return {"inv2": inv2, "sgn": sgn, "maskd": maskd}


def _attn_weights(inp, hg):
    wqb = inp["b_w_qb"][0]; wukv = inp["w_ukv"]
    hs = range(hg * 4, hg * 4 + 4)
    a = np.concatenate([wqb[:, h * 192:(h + 1) * 192] for h in hs], axis=1)
    s = np.concatenate([np.concatenate([wqb[:, h * 192 + 160:h * 192 + 192], wqb[:, h * 192 + 128:h * 192 + 160]], axis=1)
                        for h in hs], axis=1)
    uk = np.concatenate([wukv[:, h * 256:h * 256 + 128] for h in hs], axis=1)
    uv = np.concatenate([wukv[:, h * 256 + 128:h * 256 + 256] for h in hs], axis=1)
    return {"wqb": np.ascontiguousarray(a), "wqbs": np.ascontiguousarray(s), "wuk": np.ascontiguousarray(uk),
            "wuv": np.ascontiguousarray(uv)}


def kernel(**inp):
    inp = {kk: np.asarray(v) for kk, v in inp.items()}
    B, S = inp["x"].shape[0], inp["x"].shape[1]
    T = S // 2
    cores = list(range(8))
    k1 = build_l1(S)
    r1 = run_bass_kernel_spmd(k1.nc, [l1_inputs(inp, c // 2, c % 2, S) for c in cores], core_ids=cores).results
    k2 = build_l2(T)
    maps = []
    for c in cores:
        b, half = c // 2, c % 2
        r = slice(half * T, (half + 1) * T)
        m = _common_inputs(inp, b, 0)
        m.update({
            "x": np.ascontiguousarray(inp["x"][b, r]),
            "hA": np.ascontiguousarray(r1[2 * b]["hh"][r]), "hB": np.ascontiguousarray(r1[2 * b + 1]["hh"][r]),
            "wo": np.ascontiguousarray(inp["a_w_in"][0][:, 2048:3072]), "wout": np.ascontiguousarray(inp["a_w_out"][0]),
            "w1": np.ascontiguousarray(inp["ffn_w1"][0]), "w2": np.ascontiguousarray(inp["ffn_w2"][0]),
            "modw1": np.ascontiguousarray(inp["mod_w"][1][:, 0:2048]), "modb1_cols": fm(inp["mod_b"][1][0:2048]),
            "ng1_cols": fm(inp["norm_g"][1, 0]),
            "posT": np.ascontiguousarray(inp["positions"][b, r].reshape(T // 128, 128).T.astype(np.int32)),
            "kvmodw": np.ascontiguousarray(inp["kv_mod_w"]), "kvmodb_cols": fm(inp["kv_mod_b"]), "kvng_cols": fm(inp["kv_norm"]),
            "wqa": np.ascontiguousarray(inp["b_w_qa"][0]), "wdkv": np.ascontiguousarray(inp["w_dkv"]),
            "qn_row": np.ascontiguousarray(inp["b_q_norm"][0][None, :]),
            "kvn_row": np.ascontiguousarray(inp["kv_lora_norm"][None, :]),
            "invb": np.ascontiguousarray(np.tile(_INV.astype(np.float32)[None, :], (128, 1))),
        })
        maps.append(m)
    r2 = run_bass_kernel_spmd(k2.nc, maps, core_ids=cores).results
    k3 = build_l3(S)
    maps = []
    ac = _attn_consts()
    for c in cores:
        b, hg = c // 2, c % 2
        m = dict(ac); m.update(_attn_weights(inp, hg))
        for nm in ("qlatT", "ckvT", "kropeT"):
            m[nm] = np.ascontiguousarray(np.concatenate([r2[2 * b][nm], r2[2 * b + 1][nm]], axis=1))
        m["pos_row"] = np.ascontiguousarray(inp["positions"][b:b + 1, :].astype(np.int32))
        maps.append(m)
    r3 = run_bass_kernel_spmd(k3.nc, maps, core_ids=cores).results
    k4 = build_l4(T)
    maps = []
    for c in cores:
        b, half = c // 2, c % 2
        r = slice(half * T, (half + 1) * T)
        m = _common_inputs(inp, b, 1)
        m.update({
            "x": np.ascontiguousarray(r2[c]["x2"]),
            "aA": np.ascontiguousarray(r3[2 * b]["att"][r]), "aB": np.ascontiguousarray(r3[2 * b + 1]["att"][r]),
            "wout": np.ascontiguousarray(inp["b_w_o"][0]),
            "w1": np.ascontiguousarray(inp["ffn_w1"][1]), "w2": np.ascontiguousarray(inp["ffn_w2"][1]),
        })
        maps.append(m)
    r4 = run_bass_kernel_spmd(k4.nc, maps, core_ids=cores).results
    out = np.empty((B, S, 1024), np.float32)
    for c in cores:
        b, half = c // 2, c % 2
        out[b, half * T:(half + 1) * T] = r4[c]["y"]
    return out
```

```python
import ml_dtypes
import contextlib, math
import numpy as np
import concourse.bass as bass
import concourse.mybir as mybir
from concourse.bass_utils import run_bass_kernel_spmd

F32 = mybir.dt.float32
BF16 = mybir.dt.bfloat16
I32 = mybir.dt.int32
AF = mybir.ActivationFunctionType
ALU = mybir.AluOpType
AX = mybir.AxisListType
NDS = 40


class Tile:
    def __init__(self, t, psum=False):
        self.t = t
        self.w = {}
        self.r = {}
        self.psum = psum

    def __getitem__(self, idx):
        return self.t[idx]


class KB:
    def __init__(self):
        self.nc = bass.Bass("TRN2", target_bir_lowering=False)
        self.es = contextlib.ExitStack()
        nc = self.nc
        self.E = {"pe": nc.tensor, "act": nc.scalar, "dve": nc.vector, "pool": nc.gpsimd, "sp": nc.sync}
        self.sem = {e: self.es.enter_context(nc.semaphore("s_" + e)) for e in ["pe", "act", "dve", "pool"]}
        self.cnt = {e: 0 for e in self.sem}
        self.known = {e: {} for e in self.E}
        self.dsems = [self.es.enter_context(nc.semaphore(f"d{i}")) for i in range(NDS)]
        self.dcnt = [0] * NDS
        self.dnext = 0
        self.nins = 0

    def _nm(self, n):
        self.uid = getattr(self, 'uid', 0) + 1
        return f"{n}_{self.uid}"

    def dram_in(self, name, shape, dt=F32):
        return self.nc.dram_tensor(name, list(shape), dt, kind="ExternalInput").ap()

    def dram_out(self, name, shape, dt=F32):
        return self.nc.dram_tensor(name, list(shape), dt, kind="ExternalOutput").ap()

    def sb(self, name, shape, dt=F32):
        return Tile(self.es.enter_context(self.nc.sbuf_tensor(self._nm("sb_" + name), list(shape), dt)))

    def ps(self, name, shape=(128, 512), dt=F32):
        return Tile(self.es.enter_context(self.nc.psum_tensor(self._nm("ps_" + name), list(shape), dt)), psum=True)

    def _need(self, e, key, v):
        if self.known[e].get(key, 0) >= v:
            return
        sem = self.sem[key] if isinstance(key, str) else self.dsems[key]
        self.E[e].wait_ge(sem, v)
        self.known[e][key] = v

    def _deps(self, e, reads, writes):
        for t in reads:
            if t.psum:
                continue
            for k, v in t.w.items():
                if e == "pe" and k == "pe":
                    continue
                self._need(e, k, v)
        for t in list(writes) + [t for t in reads if t.psum]:
            for k, v in list(t.w.items()) + list(t.r.items()):
                if e == "pe" and k == "pe":
                    continue
                self._need(e, k, v)

    def _mark(self, key, ev, reads, writes):
        for t in reads:
            if t.psum:
                t.w[key] = max(t.w.get(key, 0), ev)
            else:
                t.r[key] = max(t.r.get(key, 0), ev)
        for t in writes:
            t.w[key] = max(t.w.get(key, 0), ev)

    def op(self, e, fn, reads=(), writes=(), inc=True):
        self._deps(e, reads, writes)
        ins = fn(self.E[e])
        self.nins += 1
        if inc:
            self.cnt[e] += 1
            ins.then_inc(self.sem[e], 1)
            ev = self.cnt[e]
        else:
            ev = self.cnt[e] + 1
        self._mark(e, ev, reads, writes)
        return ins

    def dma(self, q, out_ap, in_ap, reads=(), writes=(), **kw):
        i = self.dnext
        self.dnext = (self.dnext + 1) % NDS
        if self.dcnt[i] > 0:
            self._need(q, i, self.dcnt[i])
        self._deps(q, reads, writes)
        ins = self.E[q].dma_start(out=out_ap, in_=in_ap, **kw)
        self.nins += 1
        self.dcnt[i] += 16
        ins.then_inc(self.dsems[i], 16)
        self._mark(i, self.dcnt[i], reads, writes)
        return ins

    def finish(self):
        for i in range(NDS):
            if self.dcnt[i] > 0:
                self._need("sp", i, self.dcnt[i])
        for e in self.sem:
            if self.cnt[e] > 0:
                self._need("sp", e, self.cnt[e])

    def mm(self, out_ps, out_ap, lhsT_t, lhsT_ap, rhs_t, rhs_ap, start, stop, inc=None):
        if inc is None:
            inc = stop
        return self.op("pe", lambda E: E.matmul(out_ap, lhsT_ap, rhs_ap, start=start, stop=stop),
                       reads=[lhsT_t, rhs_t], writes=[out_ps], inc=inc)

    def tr(self, out_ps, out_ap, in_t, in_ap, ident_t, ident_ap, inc=True):
        return self.op("pe", lambda E: E.transpose(out_ap, in_ap, ident_ap),
                       reads=[in_t, ident_t], writes=[out_ps], inc=inc)

    def act(self, out_t, out_ap, in_t, in_ap, func, bias=None, scale=None, accum=None, extra_reads=(), e="act"):
        kw = {}
        rd = [in_t] + list(extra_reads)
        wr = [out_t]
        if bias is not None:
            if isinstance(bias, tuple):
                rd.append(bias[0]); kw["bias"] = bias[1]
            else:
                kw["bias"] = bias
        if scale is not None:
            if isinstance(scale, tuple):
                rd.append(scale[0]); kw["scale"] = scale[1]
            else:
                kw["scale"] = scale
        if accum is not None:
            wr.append(accum[0]); kw["accum_out"] = accum[1]
        return self.op("act", lambda E: E.activation(out=out_ap, in_=in_ap, func=func, **kw), reads=rd, writes=wr)


def _scope(self):
    @contextlib.contextmanager
    def cm():
        old = self.es
        self.es = contextlib.ExitStack()
        try:
            yield
        finally:
            self.barrier()
            self.es.close()
            self.es = old
    return cm()


def _barrier(self):
    for e in ["pe", "act", "dve", "pool", "sp"]:
        for e2 in self.sem:
            if self.cnt[e2] > 0:
                self._need(e, e2, self.cnt[e2])
        for i in range(NDS):
            if self.dcnt[i] > 0:
                self._need(e, i, self.dcnt[i])


KB.scope = _scope
KB.barrier = _barrier


EPS = 1e-6
TWO_PI = 2.0 * math.pi


def fm(v):
    v = np.asarray(v, np.float32)
    return np.ascontiguousarray(v.reshape(-1, 128).T)


def bc(ap, n=128):
    return bass.AP(ap.tensor, ap.offset, [[0, n]] + [list(x) for x in ap.ap[1:]])


def ada_cols(k, cact, mw_d, ncols, modb_cols_d, pm, pm_ap, mw, modv):
    noc = ncols // 128
    for kc in range(8):
        k.dma("sp", mw[:, kc, 0:ncols], mw_d[kc * 128:(kc + 1) * 128, :], writes=[mw])
    for oc in range(noc):
        for kc in range(8):
            k.mm(pm, pm_ap[:, oc:oc + 1], mw, mw[:, kc, oc * 128:(oc + 1) * 128], cact, cact[:, kc:kc + 1],
                 kc == 0, kc == 7)
    mb = k.sb("adac_b", [128, noc])
    k.dma("sp", mb[:, :], modb_cols_d, writes=[mb])
    k.op("dve", lambda E: E.tensor_tensor(out=modv[:, 0:noc], in0=pm_ap[:, 0:noc], in1=mb[:, :], op=ALU.add),
         reads=[pm, mb], writes=[modv])


def ada_AB(k, cact, mw_d, modb_cols_d, ng_cols_d, pm, pm_ap, mw, A, modv):
    ada_cols(k, cact, mw_d, 2048, modb_cols_d, pm, pm_ap, mw, modv)
    ng = k.sb("adaab_ng", [128, 8])
    k.dma("sp", ng[:, :], ng_cols_d, writes=[ng])
    k.op("dve", lambda E: E.scalar_tensor_tensor(out=A[:, :], in0=modv[:, 8:16], scalar=1.0, in1=ng[:, :],
                                                 op0=ALU.add, op1=ALU.mult), reads=[modv, ng], writes=[A])


def make_crep(k, cact, name="crep"):
    ones = k.sb(name + "_1", [128, 128])
    k.op("dve", lambda E: E.memset(ones[:, :], 1.0), writes=[ones])
    crep = k.sb(name, [128, 8, 128])
    for kc in range(8):
        k.op("dve", lambda E: E.tensor_scalar(out=crep[:, kc, :], in0=ones[:, :], scalar1=cact[:, kc:kc + 1],
                                              scalar2=None, op0=ALU.mult), reads=[ones, cact], writes=[crep])
    return crep


def ada_rows(k, crep, mw_d, modb_row_d, ng_row_d, pb, mw, Gb):
    for kc in range(8):
        k.dma("sp", mw[:, kc, 0:1024], mw_d[kc * 128:(kc + 1) * 128, :], writes=[mw])
    rb = k.sb("adar_rb", [128, 1024]); rg = k.sb("adar_rg", [128, 1024])
    k.dma("sp", rb[:, :], bc(modb_row_d), writes=[rb])
    k.dma("sp", rg[:, :], bc(ng_row_d), writes=[rg])
    for n2 in range(2):
        for kc in range(8):
            k.mm(pb[n2], pb[n2][:, :], crep, crep[:, kc, :], mw, mw[:, kc, n2 * 512:(n2 + 1) * 512], kc == 0, kc == 7)
        sl = slice(n2 * 512, (n2 + 1) * 512)
        k.op("dve", lambda E: E.tensor_tensor(out=Gb[:, sl], in0=pb[n2][:, :], in1=rb[:, sl], op=ALU.add),
             reads=[pb[n2], rb], writes=[Gb])
    k.op("dve", lambda E: E.tensor_tensor(out=Gb[:, :], in0=Gb[:, :], in1=rg[:, :], op=ALU.mult),
         reads=[Gb, rg], writes=[Gb])


def norm_T(k, xt, j4, ABs, outs, ident, pT, sc):
    ss, rstd, junk, xn, eps = sc["ss"], sc["rstd"], sc["junk"], sc["xn"], sc["eps"]
    for j in range(j4):
        k.act(junk, junk[:, :], xt, xt[:, j, :], AF.Square, scale=1.0 / 32.0, accum=(ss, ss[:, j:j + 1]))
    k.act(rstd, rstd[:, 0:j4], ss, ss[:, 0:j4], AF.Ln, bias=(eps, eps[:, 0:1]))
    k.act(rstd, rstd[:, 0:j4], rstd, rstd[:, 0:j4], AF.Exp, scale=-0.5)
    for j in range(j4):
        k.op("dve", lambda E: E.tensor_scalar(out=xn[:, j, :], in0=xt[:, j, :], scalar1=rstd[:, j:j + 1],
                                              scalar2=None, op0=ALU.mult), reads=[xt, rstd], writes=[xn])
    W = j4 * 128
    for kc in range(8):
        p = pT[kc % 2]
        for j in range(j4):
            k.tr(p, p[:, j * 128:(j + 1) * 128], xn, xn[:, j, kc * 128:(kc + 1) * 128], ident, ident[:, :],
                 inc=(j == j4 - 1))
        for (A, B), hT in zip(ABs, outs):
            k.act(hT, hT[:, kc, 0:W], p, p[:, 0:W], AF.Identity, scale=(A, A[:, kc:kc + 1]), bias=(B, B[:, kc:kc + 1]))


def load_consts(k, ident_d):
    ident = k.sb("ident", [128, 128], BF16)
    k.dma("pool", ident[:, :], ident_d, writes=[ident])
    eps = k.sb("eps", [128, 1])
    k.op("dve", lambda E: E.memset(eps[:, :], EPS), writes=[eps])
    return ident, eps


def load_w(k, name, w_d, K, N, chunk=None):
    kc = K // 128
    t = k.sb(name, [128, kc, N], BF16)
    v = w_d.rearrange("(k p) n -> p k n", p=128)
    step = chunk or kc
    for c0 in range(0, kc, step):
        k.dma("pool", t[:, c0:c0 + step, :], v[:, c0:c0 + step, :], writes=[t])
    return t


def sandwich(k, pY, xrow_t, xrow_ap_fn, Gb, sc2, eps):
    ss2, rs, tmp, junk = sc2["ss2"], sc2["rs"], sc2["tmp"], sc2["junk"]
    for n2 in range(2):
        k.act(junk, junk[:, 0:512], pY[n2], pY[n2][:, :], AF.Square, scale=1.0 / 32.0, accum=(ss2, ss2[:, n2:n2 + 1]))
    k.op("dve", lambda E: E.tensor_tensor(out=rs[:, 0:1], in0=ss2[:, 0:1], in1=ss2[:, 1:2], op=ALU.add),
         reads=[ss2], writes=[rs])
    k.act(rs, rs[:, 1:2], rs, rs[:, 0:1], AF.Ln, bias=(eps, eps[:, 0:1]))
    k.act(rs, rs[:, 2:3], rs, rs[:, 1:2], AF.Exp, scale=-0.5)
    for n2 in range(2):
        sl = slice(n2 * 512, (n2 + 1) * 512)
        tm = tmp[n2]
        k.op("dve", lambda E: E.scalar_tensor_tensor(out=tm[:, :], in0=pY[n2][:, :], scalar=rs[:, 2:3], in1=Gb[:, sl],
                                                     op0=ALU.mult, op1=ALU.mult), reads=[pY[n2], rs, Gb], writes=[tm])
        xa = xrow_ap_fn(sl)
        k.op("pool", lambda E: E.tensor_tensor(out=xa, in0=xa, in1=tm[:, :], op=ALU.add),
             reads=[xrow_t, tm], writes=[xrow_t])


def emit_postmix(k, T, x_d, mix_parts, x1_d, cT_d, modw_d, modb_cols_d, modb_rows_d, ng_cols_d, ng_rows_d,
                 wo_d, wout_d, ident_d, gate, mix_dt=F32):
    NT = T // 512
    with k.scope():
        ident, eps = load_consts(k, ident_d)
        pT = [k.ps("pT0", [128, 1024], BF16), k.ps("pT1", [128, 1024], BF16)]
        pA = [k.ps("pA0"), k.ps("pA1")]; pY = [k.ps("pY0"), k.ps("pY1")]
        A1 = k.sb("A1", [128, 8]); mv1 = k.sb("mv1", [128, 16]); Gb1 = k.sb("Gb1", [128, 1024])
        with k.scope():
            cact = k.sb("cact", [128, 8])
            k.dma("sp", cact[:, :], cT_d, writes=[cact])
            k.act(cact, cact[:, :], cact, cact[:, :], AF.Silu)
            mw = k.sb("mw", [128, 8, 2048])
            if gate:
                ada_AB(k, cact, modw_d[:, 0:2048], modb_cols_d[:, 0:16], ng_cols_d[:, 0:8], pA[0], pA[0][:, 0:16],
                       mw, A1, mv1)
            crep = make_crep(k, cact)
            ada_rows(k, crep, modw_d[:, 2048:3072], modb_rows_d[0:1, 2048:3072], ng_rows_d[1:2, :], pY, mw, Gb1)
        wout = load_w(k, "wout", wout_d, 1024, 1024)
        wo = load_w(k, "wo", wo_d, 1024, 1024) if gate else None
        xt = [k.sb(f"xt{i}", [128, 4, 1024]) for i in range(2)]
        ht = k.sb("ht", [128, 4, 1024], mix_dt)
        gated = k.sb("gated", [128, 4, 1024], BF16) if (gate or mix_dt != BF16) else ht
        gT = k.sb("gT", [128, 8, 512], BF16)
        sc = dict(ss=k.sb("ss", [128, 4]), rstd=k.sb("rstd", [128, 4]), junk=k.sb("junk", [128, 1024], BF16),
                  xn=k.sb("xn", [128, 4, 1024], BF16), eps=eps)
        hT = k.sb("hT", [128, 8, 512], BF16) if gate else None
        sg = [k.sb(f"sg{i}", [128, 512]) for i in range(2)]
        sc2 = dict(ss2=k.sb("ss2", [128, 2]), rs=k.sb("rs", [128, 4]),
                   tmp=[k.sb("tmpa", [128, 512]), k.sb("tmpb", [128, 512])], junk=sc["junk"])

        def load(t):
            r = slice(t * 512, (t + 1) * 512)
            k.dma("sp", xt[t % 2][:, :, :], x_d[r, :].rearrange("(j p) d -> p j d", p=128), writes=[xt[t % 2]])

        load(0)
        for t in range(NT):
            r = slice(t * 512, (t + 1) * 512)
            if t + 1 < NT:
                load(t + 1)
            X = xt[t % 2]
            for i, part in enumerate(mix_parts):
                k.dma("sp", ht[:, :, i * 512:(i + 1) * 512], part[r, :].rearrange("(j p) d -> p j d", p=128), writes=[ht])
            if gate:
                norm_T(k, X, 4, [(A1, mv1)], [hT], ident, pT, sc)
                for j in range(4):
                    for n2 in range(2):
                        sl = slice(n2 * 512, (n2 + 1) * 512)
                        for kc in range(8):
                            k.mm(pA[n2], pA[n2][:, :], hT, hT[:, kc, j * 128:(j + 1) * 128], wo, wo[:, kc, sl],
                                 kc == 0, kc == 7)
                        s_ = sg[n2]
                        k.act(s_, s_[:, :], pA[n2], pA[n2][:, :], AF.Sigmoid)
                        k.op("dve", lambda E: E.tensor_tensor(out=gated[:, j, sl], in0=s_[:, :], in1=ht[:, j, sl],
                                                              op=ALU.mult), reads=[s_, ht], writes=[gated])
            elif gated is not ht:
                for j in range(4):
                    k.op("dve", lambda E: E.tensor_copy(out=gated[:, j, :], in_=ht[:, j, :]), reads=[ht], writes=[gated])
            for c in range(8):
                p = pT[c % 2]
                for j in range(4):
                    k.tr(p, p[:, j * 128:(j + 1) * 128], gated, gated[:, j, c * 128:(c + 1) * 128], ident, ident[:, :],
                         inc=(j == 3))
                if c % 2 == 0:
                    k.act(gT, gT[:, c, :], p, p[:, 0:512], AF.Copy)
                else:
                    k.op("dve", lambda E: E.tensor_copy(out=gT[:, c, :], in_=p[:, 0:512]), reads=[p], writes=[gT])
            for j in range(4):
                for n2 in range(2):
                    sl = slice(n2 * 512, (n2 + 1) * 512)
                    for c in range(8):
                        k.mm(pY[n2], pY[n2][:, :], gT, gT[:, c, j * 128:(j + 1) * 128], wout, wout[:, c, sl],
                             c == 0, c == 7)
                sandwich(k, pY, X, lambda sl, X=X, j=j: X[:, j, sl], Gb1, sc2, eps)
            k.dma("pool", x1_d[r, :].rearrange("(j p) d -> p j d", p=128), X[:, :, :], reads=[X])


def emit_ffn(k, T, xin_d, xout_d, cT_d, modw_d, modb_cols_d, modb_rows_d, ng_cols_d, ng_rows_d, w1_d, w2_d, ident_d):
    TT = 256
    NT = T // TT
    with k.scope():
        ident, eps = load_consts(k, ident_d)
        pT = [k.ps("pT0", [128, 1024], BF16), k.ps("pT1", [128, 1024], BF16)]
        pU = [k.ps("pU0"), k.ps("pU1")]; pY = [k.ps("pY0"), k.ps("pY1")]
        A3 = k.sb("A3", [128, 8]); mv3 = k.sb("mv3", [128, 16]); Gb2 = k.sb("Gb2", [128, 1024])
        with k.scope():
            cact = k.sb("cact", [128, 8])
            k.dma("sp", cact[:, :], cT_d, writes=[cact])
            k.act(cact, cact[:, :], cact, cact[:, :], AF.Silu)
            mw = k.sb("mw", [128, 8, 2048])
            ada_AB(k, cact, modw_d[:, 3072:5120], modb_cols_d[:, 24:40], ng_cols_d[:, 16:24], pU[0], pU[0][:, 0:16],
                   mw, A3, mv3)
            crep = make_crep(k, cact)
            ada_rows(k, crep, modw_d[:, 5120:6144], modb_rows_d[0:1, 5120:6144], ng_rows_d[3:4, :], pY, mw, Gb2)
        W1 = load_w(k, "W1", w1_d, 1024, 4096, chunk=1)
        W2 = load_w(k, "W2", w2_d, 4096, 1024, chunk=4)
        xt = [k.sb(f"xt{i}", [128, 2, 1024]) for i in range(2)]
        sc = dict(ss=k.sb("ss", [128, 4]), rstd=k.sb("rstd", [128, 4]), junk=k.sb("junk", [128, 1024], BF16),
                  xn=k.sb("xn", [128, 2, 1024], BF16), eps=eps)
        hT = [k.sb(f"hT{i}", [128, 8, TT], BF16) for i in range(2)]
        uT = k.sb("uT", [128, 32, TT], BF16)
        rr = [k.sb(f"rr{i}", [128, TT]) for i in range(2)]
        sc2 = dict(ss2=k.sb("ss2", [128, 2]), rs=k.sb("rs", [128, 4]),
                   tmp=[k.sb("tmpa", [128, 512]), k.sb("tmpb", [128, 512])], junk=sc["junk"])

        def load(t):
            r = slice(t * TT, (t + 1) * TT)
            k.dma("sp", xt[t % 2][:, :, :], xin_d[r, :].rearrange("(j p) d -> p j d", p=128), writes=[xt[t % 2]])

        load(0)
        norm_T(k, xt[0], 2, [(A3, mv3)], [hT[0]], ident, pT, sc)
        for t in range(NT):
            r = slice(t * TT, (t + 1) * TT)
            X = xt[t % 2]; H = hT[t % 2]
            if t + 1 < NT:
                load(t + 1)
            for fc in range(32):
                p = pU[fc % 2]
                for kc in range(8):
                    k.mm(p, p[:, 0:TT], W1, W1[:, kc, fc * 128:(fc + 1) * 128], H, H[:, kc, :], kc == 0, kc == 7)
                r_ = rr[fc % 2]
                k.act(r_, r_[:, :], p, p[:, 0:TT], AF.Relu)
                k.op("dve", lambda E: E.tensor_tensor(out=uT[:, fc, :], in0=r_[:, :], in1=r_[:, :], op=ALU.mult),
                     reads=[r_], writes=[uT])
            if t + 1 < NT:
                norm_T(k, xt[(t + 1) % 2], 2, [(A3, mv3)], [hT[(t + 1) % 2]], ident, pT, sc)
            for j in range(2):
                for n2 in range(2):
                    sl = slice(n2 * 512, (n2 + 1) * 512)
                    for fc in range(32):
                        k.mm(pY[n2], pY[n2][:, :], uT, uT[:, fc, j * 128:(j + 1) * 128], W2, W2[:, fc, sl],
                             fc == 0, fc == 31)
                sandwich(k, pY, X, lambda sl, X=X, j=j: X[:, j, sl], Gb2, sc2, eps)
            k.dma("pool", xout_d[r, :].rearrange("(j p) d -> p j d", p=128), X[:, :, :], reads=[X])


def rope_sincos(k, tt, ti, tf, out_sc, scale_ap, bias_ap, shape_ap):
    a = shape_ap
    k.op("dve", lambda E: E.tensor_copy(out=a(ti), in_=a(tt)), reads=[tt], writes=[ti])
    k.op("dve", lambda E: E.tensor_copy(out=a(tf), in_=a(ti)), reads=[ti], writes=[tf])
    k.op("dve", lambda E: E.tensor_tensor(out=a(tf), in0=a(tt), in1=a(tf), op=ALU.subtract), reads=[tt, tf], writes=[tf])
    k.op("dve", lambda E: E.scalar_tensor_tensor(out=a(tt), in0=a(tf), scalar=0.0, in1=a(tf), op0=ALU.is_lt, op1=ALU.add),
         reads=[tf], writes=[tt])
    k.act(out_sc, a(out_sc), tt, a(tt), AF.Sin, scale=scale_ap, bias=bias_ap)


def emit_l1prep(k, T, x_d, posT_d, qlatT_d, ckvT_d, kropeT_d, cT_d, modw_d, modb_cols_d, ng_cols_d,
                kvmodw_d, kvmodb_cols_d, kvng_cols_d, wqa_d, wdkv_d, qn_row_d, kvn_row_d, invb_d, ident_d):
    NT = T // 512
    with k.scope():
        ident, eps = load_consts(k, ident_d)
        pT = [k.ps("pT0", [128, 1024], BF16), k.ps("pT1", [128, 1024], BF16)]
        pQ = k.ps("pQ"); pKV = k.ps("pKV"); pm = k.ps("pm")
        Aq = k.sb("Aq", [128, 8]); mvq = k.sb("mvq", [128, 16]); Akv = k.sb("Akv", [128, 8]); mvkv = k.sb("mvkv", [128, 16])
        with k.scope():
            cact = k.sb("cact", [128, 8])
            k.dma("sp", cact[:, :], cT_d, writes=[cact])
            k.act(cact, cact[:, :], cact, cact[:, :], AF.Silu)
            mw = k.sb("mw", [128, 8, 2048])
            ada_AB(k, cact, modw_d[:, 0:2048], modb_cols_d[:, 0:16], ng_cols_d[:, 0:8], pm, pm[:, 0:16], mw, Aq, mvq)
            ada_AB(k, cact, kvmodw_d, kvmodb_cols_d, kvng_cols_d, pm, pm[:, 16:32], mw, Akv, mvkv)
        wqa = load_w(k, "wqa", wqa_d, 1024, 384)
        wdkv = load_w(k, "wdkv", wdkv_d, 1024, 320)
        qnb = k.sb("qnb", [128, 384]); kvnb = k.sb("kvnb", [128, 256]); invb = k.sb("invb", [128, 32])
        k.dma("sp", qnb[:, :], bc(qn_row_d), writes=[qnb]); k.dma("sp", kvnb[:, :], bc(kvn_row_d), writes=[kvnb])
        k.dma("sp", invb[:, :], invb_d, writes=[invb])
        negpi = k.sb("negpi", [128, 1])
        k.op("dve", lambda E: E.memset(negpi[:, :], -math.pi), writes=[negpi])
        xt = [k.sb(f"xt{i}", [128, 4, 1024]) for i in range(2)]
        sc = dict(ss=k.sb("ss", [128, 4]), rstd=k.sb("rstd", [128, 4]), junk=k.sb("junk", [128, 1024], BF16),
                  xn=k.sb("xn", [128, 4, 1024], BF16), eps=eps)
        h1T = k.sb("h1T", [128, 8, 512], BF16); hsT = k.sb("hsT", [128, 8, 512], BF16)
        posi = k.sb("posi", [128, 4], I32); posf = k.sb("posf", [128, 4])
        tt = k.sb("tt", [128, 2, 4, 32]); ti = k.sb("ti", [128, 2, 4, 32], I32); tf = k.sb("tf", [128, 2, 4, 32])
        scs = k.sb("scs", [128, 2, 4, 32])
        st = k.sb("st", [128, 8]); junk2 = k.sb("junk2", [128, 384], BF16)
        qn = k.sb("qn", [128, 384], BF16); cn = k.sb("cn", [128, 256], BF16); kr = k.sb("kr", [128, 64], BF16)
        r1 = k.sb("r1", [128, 64]); r2 = k.sb("r2", [128, 64])
        qlT = k.sb("qlT", [128, 3, 512], BF16); ckT = k.sb("ckT", [128, 2, 512], BF16); krT = k.sb("krT", [64, 512], BF16)

        def load(t):
            r = slice(t * 512, (t + 1) * 512)
            k.dma("sp", xt[t % 2][:, :, :], x_d[r, :].rearrange("(j p) d -> p j d", p=128), writes=[xt[t % 2]])

        full = lambda T_: T_[:, :, :, :]
        load(0)
        for t in range(NT):
            r = slice(t * 512, (t + 1) * 512)
            if t + 1 < NT:
                load(t + 1)
            X = xt[t % 2]
            k.dma("sp", posi[:, :], posT_d[:, t * 4:(t + 1) * 4], writes=[posi])
            k.op("dve", lambda E: E.tensor_copy(out=posf[:, :], in_=posi[:, :]), reads=[posi], writes=[posf])
            for j in range(4):
                k.op("dve", lambda E: E.tensor_scalar(out=tt[:, 0, j, :], in0=invb[:, :], scalar1=posf[:, j:j + 1],
                                                      scalar2=0.5, op0=ALU.mult, op1=ALU.add), reads=[invb, posf], writes=[tt])
            k.op("dve", lambda E: E.tensor_scalar(out=tt[:, 1, :, :], in0=tt[:, 0, :, :], scalar1=0.25, scalar2=None,
                                                  op0=ALU.add), reads=[tt], writes=[tt])
            rope_sincos(k, tt, ti, tf, scs, TWO_PI, (negpi, negpi[:, 0:1]), full)
            norm_T(k, X, 4, [(Aq, mvq), (Akv, mvkv)], [h1T, hsT], ident, pT, sc)
            for j in range(4):
                js = slice(j * 128, (j + 1) * 128)
                for kc in range(8):
                    k.mm(pQ, pQ[:, 0:384], h1T, h1T[:, kc, js], wqa, wqa[:, kc, :], kc == 0, kc == 7)
                for kc in range(8):
                    k.mm(pKV, pKV[:, 0:320], hsT, hsT[:, kc, js], wdkv, wdkv[:, kc, :], kc == 0, kc == 7)
                k.act(junk2, junk2[:, 0:384], pQ, pQ[:, 0:384], AF.Square, scale=384.0 ** -0.5, accum=(st, st[:, 0:1]))
                k.act(junk2, junk2[:, 0:256], pKV, pKV[:, 0:256], AF.Square, scale=1.0 / 16.0, accum=(st, st[:, 1:2]))
                k.act(st, st[:, 2:4], st, st[:, 0:2], AF.Ln, bias=(eps, eps[:, 0:1]))
                k.act(st, st[:, 4:6], st, st[:, 2:4], AF.Exp, scale=-0.5)
                k.op("dve", lambda E: E.scalar_tensor_tensor(out=qn[:, :], in0=pQ[:, 0:384], scalar=st[:, 4:5], in1=qnb[:, :],
                                                             op0=ALU.mult, op1=ALU.mult), reads=[pQ, st, qnb], writes=[qn])
                k.op("dve", lambda E: E.scalar_tensor_tensor(out=cn[:, :], in0=pKV[:, 0:256], scalar=st[:, 5:6], in1=kvnb[:, :],
                                                             op0=ALU.mult, op1=ALU.mult), reads=[pKV, st, kvnb], writes=[cn])
                sin_ = scs[:, 0, j, :]; cos_ = scs[:, 1, j, :]
                k.act(r1, r1[:, :], pKV, pKV[:, 256:320], AF.Copy)
                k.op("dve", lambda E: E.tensor_tensor(out=r2[:, 0:32], in0=r1[:, 32:64], in1=sin_, op=ALU.mult), reads=[r1, scs], writes=[r2])
                k.op("dve", lambda E: E.tensor_tensor(out=r2[:, 32:64], in0=r1[:, 0:32], in1=sin_, op=ALU.mult), reads=[r1, scs], writes=[r2])
                k.op("pool", lambda E: E.tensor_tensor(out=r1[:, 0:32], in0=r1[:, 0:32], in1=cos_, op=ALU.mult), reads=[r1, scs, r2], writes=[r1])
                k.op("pool", lambda E: E.tensor_tensor(out=r1[:, 32:64], in0=r1[:, 32:64], in1=cos_, op=ALU.mult), reads=[r1, scs], writes=[r1])
                k.op("dve", lambda E: E.tensor_tensor(out=kr[:, 0:32], in0=r1[:, 0:32], in1=r2[:, 0:32], op=ALU.subtract), reads=[r1, r2], writes=[kr])
                k.op("dve", lambda E: E.tensor_tensor(out=kr[:, 32:64], in0=r1[:, 32:64], in1=r2[:, 32:64], op=ALU.add), reads=[r1, r2], writes=[kr])
                p = pT[j % 2]
                for c in range(3):
                    k.tr(p, p[:, c * 128:(c + 1) * 128], qn, qn[:, c * 128:(c + 1) * 128], ident, ident[:, :], inc=False)
                for c in range(2):
                    k.tr(p, p[:, (3 + c) * 128:(4 + c) * 128], cn, cn[:, c * 128:(c + 1) * 128], ident, ident[:, :], inc=False)
                k.tr(p, p[0:64, 640:768], kr, kr[:, :], ident, ident[:, :], inc=True)
                k.act(qlT, qlT[:, :, js], p, p[:, 0:384].rearrange("p (c t) -> p c t", c=3), AF.Copy)
                k.op("dve", lambda E: E.tensor_copy(out=ckT[:, :, js], in_=p[:, 384:640].rearrange("p (c t) -> p c t", c=2)),
                     reads=[p], writes=[ckT])
                k.op("dve", lambda E: E.tensor_copy(out=krT[:, js], in_=p[0:64, 640:768]), reads=[p], writes=[krT])
            k.dma("pool", qlatT_d.rearrange("(c p) t -> p c t", p=128)[:, :, r], qlT[:, :, :], reads=[qlT])
            k.dma("pool", ckvT_d.rearrange("(c p) t -> p c t", p=128)[:, :, r], ckT[:, :, :], reads=[ckT])
            k.dma("pool", kropeT_d[:, r], krT[:, :], reads=[krT])


def emit_attn(k, S, qlatT_d, ckvT_d, kropeT_d, pos_row_d, wqb_d, wqbs_d, wuk_d, wuv_d, inv2_d, sgn_d, maskd_d, out_d):
    NQ = S // 512
    NKB = S // 128
    SCALE = 192.0 ** -0.5
    with k.scope():
        pS = [k.ps("pS0"), k.ps("pS1")]
        pAcc = [k.ps("pAcc0"), k.ps("pAcc1")]
        pQ = [k.ps("pQ0"), k.ps("pQ1")]
        wqb = load_w(k, "wqb", wqb_d, 384, 768); wqbs = load_w(k, "wqbs", wqbs_d, 384, 256)
        wuk = load_w(k, "wuk", wuk_d, 256, 512); wuv = load_w(k, "wuv", wuv_d, 256, 512)
        krT = k.sb("krT", [64, S], BF16)
        k.dma("sp", krT[:, :], kropeT_d, writes=[krT])
        knT = k.sb("knT", [128, 4, S], BF16)
        vext = k.sb("vext", [128, NKB, 4, 129], BF16)
        k.op("pool", lambda E: E.memset(vext[:, :, :, 128:129], 1.0), writes=[vext])
        maskd = k.sb("maskd", [128, 4, 512], BF16)
        k.dma("pool", maskd[:, :, :], maskd_d.rearrange("m p q -> p m q"), writes=[maskd])
        inv2 = k.sb("inv2", [64, 1]); sgn = k.sb("sgn", [64, 2])
        k.dma("sp", inv2[:, :], inv2_d, writes=[inv2]); k.dma("sp", sgn[:, :], sgn_d, writes=[sgn])
        with k.scope():
            ckT = k.sb("ckT", [128, 2, S], BF16)
            k.dma("sp", ckT[:, :, :], ckvT_d.rearrange("(c p) t -> p c t", p=128), writes=[ckT])
            for tq in range(NQ):
                ts = slice(tq * 512, (tq + 1) * 512)
                for h in range(4):
                    p = pQ[h % 2]
                    for c in range(2):
                        k.mm(p, p[:, :], wuk, wuk[:, c, h * 128:(h + 1) * 128], ckT, ckT[:, c, ts], c == 0, c == 1)
                    if h % 2 == 0:
                        k.act(knT, knT[:, h, ts], p, p[:, :], AF.Copy)
                    else:
                        k.op("dve", lambda E: E.tensor_copy(out=knT[:, h, ts], in_=p[:, :]), reads=[p], writes=[knT])
            for kb in range(NKB):
                p = pS[kb % 2]
                for c in range(2):
                    k.mm(p, p[:, :], ckT, ckT[:, c, kb * 128:(kb + 1) * 128], wuv, wuv[:, c, :], c == 0, c == 1)
                if kb % 2 == 0:
                    k.act(vext, vext[:, kb, :, 0:128], p, p[:, :].rearrange("p (h e) -> p h e", h=4), AF.Copy)
                else:
                    k.op("dve", lambda E: E.tensor_copy(out=vext[:, kb, :, 0:128],
                                                        in_=p[:, :].rearrange("p (h e) -> p h e", h=4)),
                         reads=[p], writes=[vext])
        qlT = [k.sb(f"qlT{i}", [128, 3, 512], BF16) for i in range(2)]
        posi = k.sb("posi", [64, 512], I32)
        tt = k.sb("tt", [64, 2, 512]); ti = k.sb("ti", [64, 2, 512], I32); tf = k.sb("tf", [64, 2, 512])
        cs2 = k.sb("cs2", [64, 2, 512])
        qnT = [k.sb(f"qnT{i}", [128, 512], BF16) for i in range(2)]
        qrT = [k.sb(f"qrT{i}", [64, 512], BF16) for i in range(2)]
        ra = k.sb("ra", [64, 512]); rb = k.sb("rb", [64, 512])
        PTb = [k.sb(f"PTb{i}", [128, 512], BF16) for i in range(3)]
        ot = [k.sb(f"ot{i}", [128, 4, 512]) for i in range(1)]
        rec = k.sb("rec", [128, 4])
        negpi = k.sb("negpi", [64, 1]); twopi = k.sb("twopi", [64, 1])
        k.op("dve", lambda E: E.memset(negpi[:, :], -math.pi), writes=[negpi])
        k.op("dve", lambda E: E.memset(twopi[:, :], TWO_PI), writes=[twopi])
        qv = qlatT_d.rearrange("(c p) t -> p c t", p=128)
        unit = 0
        for tq in range(NQ):
            q0 = tq * 512
            ts = slice(q0, q0 + 512)
            QL = qlT[tq % 2]
            k.dma("sp", QL[:, :, :], qv[:, :, ts], writes=[QL])
            k.dma("sp", posi[:, :], bc(pos_row_d[0:1, ts], 64), writes=[posi])
            k.op("dve", lambda E: E.tensor_copy(out=tf[:, 0, :], in_=posi[:, :]), reads=[posi], writes=[tf])
            k.op("dve", lambda E: E.tensor_scalar(out=tt[:, 0, :], in0=tf[:, 0, :], scalar1=inv2[:, 0:1], scalar2=0.5,
                                                  op0=ALU.mult, op1=ALU.add), reads=[tf, inv2], writes=[tt])
            k.op("dve", lambda E: E.tensor_scalar(out=tt[:, 1, :], in0=tt[:, 0, :], scalar1=0.25, scalar2=None,
                                                  op0=ALU.add), reads=[tt], writes=[tt])
            a3 = lambda T_: T_[:, :, :]
            k.op("dve", lambda E: E.tensor_copy(out=a3(ti), in_=a3(tt)), reads=[tt], writes=[ti])
            k.op("dve", lambda E: E.tensor_copy(out=a3(tf), in_=a3(ti)), reads=[ti], writes=[tf])
            k.op("dve", lambda E: E.tensor_tensor(out=a3(tf), in0=a3(tt), in1=a3(tf), op=ALU.subtract), reads=[tt, tf], writes=[tf])
            k.op("dve", lambda E: E.scalar_tensor_tensor(out=a3(tt), in0=a3(tf), scalar=0.0, in1=a3(tf), op0=ALU.is_lt,
                                                         op1=ALU.add), reads=[tf], writes=[tt])
            k.act(cs2, cs2[:, 0, :], tt, tt[:, 0, :], AF.Sin, scale=(sgn, sgn[:, 0:1]), bias=(sgn, sgn[:, 1:2]))
            k.act(cs2, cs2[:, 1, :], tt, tt[:, 1, :], AF.Sin, scale=(twopi, twopi[:, 0:1]), bias=(negpi, negpi[:, 0:1]))
            O = ot[0]
            nkb = (q0 + 512) // 128

            def qproj(h):
                QN = qnT[h % 2]; QR = qrT[h % 2]
                p = pQ[0]
                for c in range(3):
                    k.mm(p, p[:, :], wqb, wqb[:, c, h * 192:h * 192 + 128], QL, QL[:, c, :], c == 0, c == 2)
                k.act(QN, QN[:, :], p, p[:, :], AF.Copy)
                p = pQ[1]
                for c in range(3):
                    k.mm(p, p[0:64, :], wqb, wqb[:, c, h * 192 + 128:h * 192 + 192], QL, QL[:, c, :], c == 0, c == 2)
                k.op("dve", lambda E: E.tensor_tensor(out=ra[:, :], in0=p[0:64, :], in1=cs2[:, 1, :], op=ALU.mult),
                     reads=[p, cs2], writes=[ra])
                for c in range(3):
                    k.mm(p, p[0:64, :], wqbs, wqbs[:, c, h * 64:(h + 1) * 64], QL, QL[:, c, :], c == 0, c == 2)
                k.op("dve", lambda E: E.tensor_tensor(out=rb[:, :], in0=p[0:64, :], in1=cs2[:, 0, :], op=ALU.mult),
                     reads=[p, cs2], writes=[rb])
                k.op("pool", lambda E: E.tensor_tensor(out=QR[:, :], in0=ra[:, :], in1=rb[:, :], op=ALU.add),
                     reads=[ra, rb], writes=[QR])

            def smm(h, kb, u):
                ks = slice(kb * 128, kb * 128 + 128)
                ps_ = pS[u % 2]
                k.mm(ps_, ps_[:, :], knT, knT[:, h, ks], qnT[h % 2], qnT[h % 2][:, :], True, False)
                k.mm(ps_, ps_[:, :], krT, krT[:, ks], qrT[h % 2], qrT[h % 2][:, :], False, True)

            qproj(0)
            for h in range(4):
                u0 = unit
                smm(h, 0, u0)
                if h + 1 < 4:
                    qproj(h + 1)
                for kb in range(nkb):
                    k0 = kb * 128
                    u = u0 + kb
                    ps_ = pS[u % 2]; PB = PTb[u % 3]
                    if kb + 1 < nkb:
                        smm(h, kb + 1, u + 1)
                    k.act(PB, PB[:, :], ps_, ps_[:, :], AF.Exp, scale=SCALE)
                    if k0 >= q0:
                        m = (k0 - q0) // 128
                        k.op("dve", lambda E: E.tensor_tensor(out=PB[:, :], in0=PB[:, :], in1=maskd[:, m, :], op=ALU.mult),
                             reads=[PB, maskd], writes=[PB])
                    qss = [qs for qs in range(4) if not (k0 > q0 + qs * 128 + 127)]
                    for qs in qss:
                        acc = pAcc[qs // 2]
                        a_ap = acc[:, (qs % 2) * 129:(qs % 2) * 129 + 129]
                        first = (kb == 0)
                        last = (kb == (q0 + qs * 128) // 128)
                        k.op("pe", lambda E: E.matmul(a_ap, PB[:, qs * 128:(qs + 1) * 128], vext[:, kb, h, :],
                                                      start=(first and qs % 2 == 0), stop=last, skip_group_check=True),
                             reads=[PB, vext], writes=[acc], inc=(qs == qss[-1]))
                unit = u0 + nkb
                for qs in range(4):
                    acc = pAcc[qs // 2]
                    a_ap = acc[:, (qs % 2) * 129:(qs % 2) * 129 + 129]
                    k.op("dve", lambda E: E.reciprocal(out=rec[:, qs:qs + 1], in_=a_ap[:, 128:129]), reads=[acc], writes=[rec])
                    k.op("dve", lambda E: E.tensor_scalar(out=O[:, qs, h * 128:(h + 1) * 128], in0=a_ap[:, 0:128],
                                                          scalar1=rec[:, qs:qs + 1], scalar2=None, op0=ALU.mult),
                         reads=[acc, rec], writes=[O])
            k.dma("pool", out_d[ts, :].rearrange("(j p) d -> p j d", p=128), O[:, :, :], reads=[O])


def rows_to_T(k, xt, j4, A, B, ident, pT, hT, sc):
    norm_T(k, xt, j4, [(A, B)], [hT], ident, pT, sc)


def adaln_cols(k, cact, mw_dram, ncols, modb_sb, pm, pm_ap, mw, name):
    noc = ncols // 128
    for kc in range(8):
        k.dma("sp", mw[:, kc, 0:ncols], mw_dram[kc * 128:(kc + 1) * 128, :], writes=[mw])
    for oc in range(noc):
        for kc in range(8):
            k.mm(pm, pm_ap[:, oc:oc + 1], mw, mw[:, kc, oc * 128:(oc + 1) * 128], cact, cact[:, kc:kc + 1],
                 kc == 0, kc == 7)
    modv = k.sb(name, [128, noc])
    k.op("dve", lambda E: E.tensor_tensor(out=modv[:, :], in0=pm_ap[:, 0:noc], in1=modb_sb[:, 0:noc], op=ALU.add),
         reads=[pm, modb_sb], writes=[modv])
    return modv


def emit_mlstm(k, S, x, cT, modw, modb, ng, wq, wk, wv, wg, bgb, hnb, ident_d, tri_d, hh):
    NT = S // 512
    with k.scope():
        ident = k.sb("ident", [128, 128], BF16); tri = k.sb("tri", [64, 64]); ones = k.sb("ones", [64, 128])
        bgb_sb = k.sb("bgb_sb", [64, 32]); hnb_sb = k.sb("hnb_sb", [64, 512])
        eps = k.sb("eps", [128, 1]); one_c = k.sb("one_c", [128, 1]); lnsc = k.sb("lnsc", [128, 1])
        cact = k.sb("cact", [128, 8]); modb_sb = k.sb("modb_sb", [128, 16]); ng_sb = k.sb("ng_sb", [128, 8])
        k.dma("pool", ident[:, :], ident_d, writes=[ident])
        k.dma("sp", tri[:, :], tri_d, writes=[tri])
        k.dma("sp", bgb_sb[:, :], bgb, writes=[bgb_sb]); k.dma("sp", hnb_sb[:, :], hnb, writes=[hnb_sb])
        k.dma("sp", cact[:, :], cT, writes=[cact]); k.dma("sp", modb_sb[:, :], modb, writes=[modb_sb])
        k.dma("sp", ng_sb[:, :], ng, writes=[ng_sb])
        k.op("dve", lambda E: E.memset(ones[:, :], 1.0), writes=[ones])
        k.op("dve", lambda E: E.memset(eps[:, :], EPS), writes=[eps])
        k.op("dve", lambda E: E.memset(one_c[:, :], 1.0), writes=[one_c])
        k.op("dve", lambda E: E.memset(lnsc[:, :], math.log(128.0 ** -0.5)), writes=[lnsc])
        wq_sb = k.sb("wq_sb", [128, 8, 256], BF16); wk_sb = k.sb("wk_sb", [128, 8, 256], BF16)
        wv_sb = k.sb("wv_sb", [128, 8, 512], BF16); wg_sb = k.sb("wg_sb", [128, 8, 4], BF16)
        for (t, d) in ((wq_sb, wq), (wk_sb, wk), (wv_sb, wv), (wg_sb, wg)):
            k.dma("pool", t[:, :, :], d.rearrange("(k p) n -> p k n", p=128), writes=[t])

        pT = [k.ps("pT0", [128, 1024], BF16), k.ps("pT1", [128, 1024], BF16)]
        pqk = k.ps("pqk"); pmisc = k.ps("pmisc")
        pN = [k.ps("pN0"), k.ps("pN1")]; pD = [k.ps("pD0"), k.ps("pD1")]
        pk_ap = pmisc[0:64, 0:256]; pg_ap = pmisc[0:64, 256:288]; pcs_ap = pmisc[0:64, 288:304]
        ptot_ap = pmisc[:, 304:320]; pmod_ap = pmisc[:, 320:336]

        k.act(cact, cact[:, :], cact, cact[:, :], AF.Silu)
        mw = k.sb("mw", [128, 8, 2048])
        modv = adaln_cols(k, cact, modw, 2048, modb_sb, pmisc, pmod_ap, mw, "modv")
        A1 = k.sb("A1", [128, 8])
        k.op("dve", lambda E: E.scalar_tensor_tensor(out=A1[:, :], in0=modv[:, 8:16], scalar=1.0, in1=ng_sb[:, :],
                                                     op0=ALU.add, op1=ALU.mult), reads=[modv, ng_sb], writes=[A1])
        B1 = modv

        xt = [k.sb(f"xt{i}", [128, 4, 1024]) for i in range(2)]
        sc = dict(ss=k.sb("ss", [128, 4]), rstd=k.sb("rstd", [128, 4]), junk=k.sb("junk", [128, 1024], BF16),
                  xn=k.sb("xn", [128, 4, 1024], BF16), eps=eps)
        hT = k.sb("hT", [128, 8, 512], BF16)
        qT = k.sb("qT", [128, 2, 512], BF16); kT = k.sb("kT", [128, 2, 512], BF16)
        G = k.sb("G", [64, 8, 4]); e1 = k.sb("e1", [64, 16]); sp_ = k.sb("sp_", [64, 16])
        t1 = k.sb("t1", [64, 16]); t2 = k.sb("t2", [64, 16])
        Aa = k.sb("Aa", [64, 16]); KK = k.sb("KK", [64, 16]); EB = k.sb("EB", [64, 16]); Gt = k.sb("Gt", [128, 16])
        vext = [k.sb(f"vext{i}", [64, 2, 257], BF16) for i in range(8)]
        kp = [k.sb(f"kp{i}", [64, 2, 128], BF16) for i in range(8)]
        PT = [[k.sb(f"PT{h}_{i}", [64, 64], BF16) for i in range(2)] for h in range(2)]
        sm = [[k.sb(f"sm{h}_{i}", [64, 16]) for i in range(2)] for h in range(2)]
        junk64 = k.sb("junk64", [64, 256], BF16)
        ho = [k.sb(f"ho{i}", [64, 2, 256], hh.dtype) for i in range(3)]
        C32 = [k.sb(f"C32_{h}", [128, 257]) for h in range(2)]
        Cb = [k.sb(f"Cb_{h}", [128, 257], BF16) for h in range(2)]
        for h in range(2):
            k.op("dve", lambda E: E.memset(C32[h][:, :], 0.0), writes=[C32[h]])
            k.op("dve", lambda E: E.memset(Cb[h][:, :], 0.0), writes=[Cb[h]])
        for i in range(8):
            k.op("dve", lambda E: E.memset(vext[i][:, :, 256:257], 1.0), writes=[vext[i]])

        def load_x(t):
            k.dma("sp", xt[t % 2][:, :, :], x[t * 512:(t + 1) * 512, :].rearrange("(j p) d -> p j d", p=128),
                  writes=[xt[t % 2]])

        load_x(0)
        for t in range(NT):
            if t + 1 < NT:
                load_x(t + 1)
            rows_to_T(k, xt[t % 2], 4, A1, B1, ident, pT, hT, sc)
            for (dst, w) in ((qT, wq_sb), (kT, wk_sb)):
                for h in range(2):
                    for kc in range(8):
                        k.mm(pqk, pqk[:, :], w, w[:, kc, h * 128:(h + 1) * 128], hT, hT[:, kc, :], kc == 0, kc == 7)
                    k.act(dst, dst[:, h, :], pqk, pqk[:, :], AF.Copy)
            for c in range(8):
                for kc in range(8):
                    k.mm(pmisc, pmisc[0:64, 256 + c * 4:260 + c * 4], hT, hT[:, kc, c * 64:(c + 1) * 64],
                         wg_sb, wg_sb[:, kc, :], kc == 0, kc == 7, inc=(kc == 7 and c == 7))
            k.op("dve", lambda E: E.tensor_tensor(out=G[:, :, :], in0=pg_ap.rearrange("p (c g) -> p c g", g=4),
                                                  in1=bgb_sb[:, :].rearrange("p (c g) -> p c g", g=4), op=ALU.add),
                 reads=[pmisc, bgb_sb], writes=[G])
            k.act(e1, e1[:, :].rearrange("p (c g) -> p c g", g=2), G, G[:, :, 2:4], AF.Exp, scale=-1.0)
            k.act(sp_, sp_[:, :], e1, e1[:, :], AF.Ln, bias=(one_c, one_c[0:64, 0:1]))
            k.mm(pmisc, pcs_ap, tri, tri[:, :], sp_, sp_[:, :], True, True)
            k.mm(pmisc, ptot_ap, ones, ones[:, :], sp_, sp_[:, :], True, True)
            k.act(EB, EB[:, :], pmisc, pcs_ap, AF.Exp, scale=-1.0, bias=(lnsc, lnsc[0:64, 0:1]))
            k.op("dve", lambda E: E.tensor_tensor(out=t1[:, :].rearrange("p (c g) -> p c g", g=2), in0=G[:, :, 0:2],
                                                  in1=pcs_ap.rearrange("p (c g) -> p c g", g=2), op=ALU.add),
                 reads=[G, pmisc], writes=[t1])
            k.act(Aa, Aa[:, :], t1, t1[:, :], AF.Exp)
            k.op("dve", lambda E: E.tensor_tensor(out=t2[:, :], in0=t1[:, :], in1=pmisc[0:64, 304:320], op=ALU.subtract),
                 reads=[t1, pmisc], writes=[t2])
            k.act(KK, KK[:, :], t2, t2[:, :], AF.Exp)
            k.act(Gt, Gt[:, :], pmisc, ptot_ap, AF.Exp, scale=-1.0)
            for c in range(8):
                for kc in range(8):
                    k.mm(pqk, pqk[0:64, :], hT, hT[:, kc, c * 64:(c + 1) * 64], wv_sb, wv_sb[:, kc, :], kc == 0, kc == 7)
                for kc in range(8):
                    k.mm(pmisc, pk_ap, hT, hT[:, kc, c * 64:(c + 1) * 64], wk_sb, wk_sb[:, kc, :], kc == 0, kc == 7)
                k.act(vext[c], vext[c][:, :, 0:256], pqk, pqk[0:64, :].rearrange("p (h e) -> p h e", h=2), AF.Copy)
                for h in range(2):
                    idx = c * 2 + h
                    k.op("dve", lambda E: E.tensor_scalar(out=kp[c][:, h, :], in0=pmisc[0:64, h * 128:(h + 1) * 128],
                                                          scalar1=KK[:, idx:idx + 1], scalar2=None, op0=ALU.mult),
                         reads=[pmisc, KK], writes=[kp[c]])
            for c in range(8):
                gc = t * 8 + c
                cs = slice(c * 64, (c + 1) * 64)
                pS = pqk
                for h in range(2):
                    k.mm(pS, pS[0:64, h * 64:(h + 1) * 64], kT, kT[:, h, cs], qT, qT[:, h, cs], True, True)
                for h in range(2):
                    idx = c * 2 + h
                    P = PT[h][gc % 2]
                    k.op("dve", lambda E: E.scalar_tensor_tensor(out=P[:, :], in0=pS[0:64, h * 64:(h + 1) * 64],
                                                                 scalar=Aa[:, idx:idx + 1], in1=tri[:, :],
                                                                 op0=ALU.mult, op1=ALU.mult),
                         reads=[pS, Aa, tri], writes=[P])
                for h in range(2):
                    P = PT[h][gc % 2]
                    k.mm(pN[h], pN[h][0:64, 0:257], qT, qT[:, h, cs], Cb[h], Cb[h][:, :], True, False)
                    k.mm(pN[h], pN[h][0:64, 0:257], P, P[:, :], vext[c], vext[c][:, h, :], False, True)
                    k.mm(pD[h], pD[h][:, 0:257], kp[c], kp[c][:, h, :], vext[c], vext[c][:, h, :], True, True)
                for h in range(2):
                    idx = c * 2 + h
                    k.op("dve", lambda E: E.scalar_tensor_tensor(out=C32[h][:, :], in0=C32[h][:, :],
                                                                 scalar=Gt[:, idx:idx + 1], in1=pD[h][:, 0:257],
                                                                 op0=ALU.mult, op1=ALU.add),
                         reads=[C32[h], Gt, pD[h]], writes=[C32[h]])
                    k.act(Cb[h], Cb[h][:, :], C32[h], C32[h][:, :], AF.Copy)
                hob = ho[gc % 3]
                for h in range(2):
                    idx = c * 2 + h
                    s_ = sm[h][gc % 2]
                    ebc = EB[:, idx:idx + 1]
                    k.act(s_, s_[:, 0:1], pN[h], pN[h][0:64, 256:257], AF.Abs, scale=(EB, ebc))
                    k.act(junk64, junk64[:, :], pN[h], pN[h][0:64, 0:256], AF.Square, accum=(s_, s_[:, 4:5]))
                    k.op("dve", lambda E: E.tensor_scalar(out=s_[:, 1:2], in0=s_[:, 0:1], scalar1=1.0, scalar2=None,
                                                          op0=ALU.max), reads=[s_], writes=[s_])
                    k.op("dve", lambda E: E.reciprocal(out=s_[:, 2:3], in_=s_[:, 1:2]), reads=[s_], writes=[s_])
                    k.op("dve", lambda E: E.tensor_tensor(out=s_[:, 3:4], in0=s_[:, 2:3], in1=ebc, op=ALU.mult),
                         reads=[s_, EB], writes=[s_])
                    k.op("dve", lambda E: E.scalar_tensor_tensor(out=s_[:, 5:6], in0=s_[:, 3:4], scalar=s_[:, 3:4],
                                                                 in1=s_[:, 4:5], op0=ALU.mult, op1=ALU.mult),
                         reads=[s_], writes=[s_])
                    k.act(s_, s_[:, 6:7], s_, s_[:, 5:6], AF.Ln, scale=1.0 / 256.0, bias=(eps, eps[0:64, 0:1]))
                    k.act(s_, s_[:, 7:8], s_, s_[:, 6:7], AF.Exp, scale=-0.5)
                    k.op("dve", lambda E: E.tensor_tensor(out=s_[:, 8:9], in0=s_[:, 7:8], in1=s_[:, 3:4], op=ALU.mult),
                         reads=[s_], writes=[s_])
                    k.op("dve", lambda E: E.scalar_tensor_tensor(out=hob[:, h, :], in0=pN[h][0:64, 0:256],
                                                                 scalar=s_[:, 8:9], in1=hnb_sb[:, h * 256:(h + 1) * 256],
                                                                 op0=ALU.mult, op1=ALU.mult),
                         reads=[pN[h], s_, hnb_sb], writes=[hob])
                k.dma("pool", hh[t * 512 + c * 64: t * 512 + (c + 1) * 64, :], hob[:, :, :].rearrange("p h e -> p (h e)"),
                      reads=[hob])


_INV = (10000.0 ** (-np.arange(0, 64, 2, dtype=np.float64) / 64)) / (2 * np.pi)
NSEL = 8


def k_collective(k, src, dst):
    k.barrier()
    if not hasattr(k, "cc_sem"):
        k.cc_sem = k.es.enter_context(k.nc.semaphore("cc_sem")); k.cc_cnt = 0
    k.nc.gpsimd.collective_compute("AllGather", ALU.bypass, replica_groups=[list(range(8))], ins=[src], outs=[dst]) \
        .then_inc(k.cc_sem, 1)
    k.cc_cnt += 1
    for e in ("sp", "pool", "pe", "act", "dve"):
        k.E[e].wait_ge(k.cc_sem, k.cc_cnt)


def build_fused(S):
    T = S // 2
    k = KB(); nc = k.nc
    D = k.dram_in
    x = D("x", [S, 1024]); cT = D("cT", [128, 8]); ident = D("ident", [128, 128]); tri = D("tri", [64, 64])
    sel = D("sel", [1, NSEL], I32)
    modw = [D("modw0", [1024, 6144]), D("modw1", [1024, 6144])]
    modb_cols = [D("modb_cols0", [128, 48]), D("modb_cols1", [128, 48])]
    modb_rows = [D("modb_rows0", [1, 6144]), D("modb_rows1", [1, 6144])]
    ng_cols = [D("ng_cols0", [128, 32]), D("ng_cols1", [128, 32])]
    ng_rows = [D("ng_rows0", [4, 1024]), D("ng_rows1", [4, 1024])]
    wq = D("wq", [1024, 256]); wk = D("wk", [1024, 256]); wv = D("wv", [1024, 512]); wg = D("wg", [1024, 4])
    bgb = D("bgb", [64, 32]); hnb = D("hnb", [64, 512])
    wo = D("wo", [1024, 1024]); wout0 = D("wout0", [1024, 1024])
    w1 = [D("w1_0", [1024, 4096]), D("w1_1", [1024, 4096])]; w2 = [D("w2_0", [4096, 1024]), D("w2_1", [4096, 1024])]
    posT = D("posT", [128, T // 128], I32); pos_row = D("pos_row", [1, S], I32)
    kvmodw = D("kvmodw", [1024, 2048]); kvmodb = D("kvmodb_cols", [128, 16]); kvng = D("kvng_cols", [128, 8])
    wqa = D("wqa", [1024, 384]); wdkv = D("wdkv", [1024, 320])
    qn_row = D("qn_row", [1, 384]); kvn_row = D("kvn_row", [1, 256]); invb = D("invb", [128, 32])
    wqb = D("wqb", [384, 768]); wqbs = D("wqbs", [384, 256]); wuk = D("wuk", [256, 512]); wuv = D("wuv", [256, 512])
    inv2 = D("inv2", [64, 1]); sgn = D("sgn", [64, 2]); maskd = D("maskd", [4, 128, 512])
    wout1 = D("wout1", [1024, 1024])
    y = k.dram_out("y", [T, 1024])
    I = lambda n, sh, dt: nc.dram_tensor(n, list(sh), dt).ap()
    hh_full = I("hh_full", [S, 512], BF16); hh_send = I("hh_send", [T, 512], BF16); hh_g = I("hh_g", [8 * T, 512], BF16)
    x1a = I("x1a", [T, 1024], F32); x2 = I("x2", [T, 1024], F32); x1b = I("x1b", [T, 1024], F32)
    lat_send = I("lat_send", [704, T], BF16); lat_g = I("lat_g", [8 * 704, T], BF16); lat_full = I("lat_full", [704, S], BF16)
    att_full = I("att_full", [S, 512], BF16); att_send = I("att_send", [T, 512], BF16); att_g = I("att_g", [8 * T, 512], BF16)
    regs = [k.es.enter_context(nc.sync.register(f"selreg{i}")) for i in range(5)]
    v = []
    for i, rg in enumerate(regs):
        nc.sync.reg_load(rg, sel[0:1, i:i + 1])
        v.append(nc.sync.snap(rg))
    v_own, v_oth, v_prank, v_latA, v_latB = v

    def dyn_copy(dst, src, rows):
        k.dma("sp", dst[:, :], src)

    x_own = I("x_own", [T, 1024], F32); hh_own = I("hh_own", [T, 512], BF16); hh_par = I("hh_par", [T, 512], BF16)
    att_own = I("att_own", [T, 512], BF16); att_par = I("att_par", [T, 512], BF16)
    dyn_copy(x_own, x[bass.ds(v_own, T), :], T)
    emit_mlstm(k, S, x, cT, modw[0][:, 0:2048], modb_cols[0][:, 0:16], ng_cols[0][:, 0:8], wq, wk, wv, wg, bgb, hnb, ident, tri, hh_full)
    k.barrier()
    dyn_copy(hh_send, hh_full[bass.ds(v_oth, T), :], T)
    dyn_copy(hh_own, hh_full[bass.ds(v_own, T), :], T)
    k_collective(k, hh_send, hh_g)
    dyn_copy(hh_par, hh_g[bass.ds(v_prank, T), :], T)
    k.barrier()
    emit_postmix(k, T, x_own, [hh_own, hh_par], x1a, cT,
                 modw[0], modb_cols[0], modb_rows[0], ng_cols[0], ng_rows[0], wo, wout0, ident, gate=True, mix_dt=BF16)
    emit_ffn(k, T, x1a, x2, cT, modw[0], modb_cols[0], modb_rows[0], ng_cols[0], ng_rows[0], w1[0], w2[0], ident)
    emit_l1prep(k, T, x2, posT, lat_send[0:384, :], lat_send[384:640, :], lat_send[640:704, :], cT,
                modw[1][:, 0:2048], modb_cols[1][:, 0:16], ng_cols[1][:, 0:8],
                kvmodw, kvmodb, kvng, wqa, wdkv, qn_row, kvn_row, invb, ident)
    k_collective(k, lat_send, lat_g)
    for (c0, vv) in ((0, v_latA), (T, v_latB)):
        k.dma("sp", lat_full[:, c0:c0 + T], lat_g[bass.ds(vv, 704), :])
    k.barrier()
    emit_attn(k, S, lat_full[0:384, :], lat_full[384:640, :], lat_full[640:704, :], pos_row, wqb, wqbs, wuk, wuv, inv2, sgn,
              maskd, att_full)
    k.barrier()
    dyn_copy(att_send, att_full[bass.ds(v_oth, T), :], T)
    dyn_copy(att_own, att_full[bass.ds(v_own, T), :], T)
    k_collective(k, att_send, att_g)
    dyn_copy(att_par, att_g[bass.ds(v_prank, T), :], T)
    k.barrier()
    emit_postmix(k, T, x2, [att_own, att_par], x1b, cT,
                 modw[1], modb_cols[1], modb_rows[1], ng_cols[1], ng_rows[1], None, wout1, ident, gate=False, mix_dt=BF16)
    emit_ffn(k, T, x1b, y, cT, modw[1], modb_cols[1], modb_rows[1], ng_cols[1], ng_rows[1], w1[1], w2[1], ident)
    k.finish()
    return k


def fused_inputs(inp, c, S):
    T = S // 2
    b, r = c // 2, c % 2
    p = 2 * b + (1 - r)
    a = inp["a_w_in"][0]
    gi = [3072 + 2 * r, 3073 + 2 * r, 3076 + 2 * r, 3077 + 2 * r]
    bg = inp["a_b_gates"][0][[2 * r, 2 * r + 1, 4 + 2 * r, 5 + 2 * r]]
    own = slice(r * 512, (r + 1) * 512); oth = slice((1 - r) * 512, (2 - r) * 512)
    perm = np.r_[np.arange(r * 512, (r + 1) * 512), np.arange((1 - r) * 512, (2 - r) * 512)]
    wqb = inp["b_w_qb"][0]; wukv = inp["w_ukv"]
    hs = range(r * 4, r * 4 + 4)
    m = {
        "x": np.ascontiguousarray(inp["x"][b, :S]), "cT": fm(inp["c"][b]), "ident": np.eye(128, dtype=np.float32),
        "tri": np.triu(np.ones((64, 64), np.float32)),
        "sel": np.array([[r * T, (1 - r) * T, p * T, (2 * b) * 704, (2 * b + 1) * 704, 0, 0, 0]], np.int32),
        "wq": np.ascontiguousarray(a[:, r * 256:(r + 1) * 256]),
        "wk": np.ascontiguousarray(a[:, 512 + r * 256:512 + (r + 1) * 256]),
        "wv": np.ascontiguousarray(a[:, 1024 + r * 512:1024 + (r + 1) * 512]),
        "wg": np.ascontiguousarray(a[:, gi]),
        "bgb": np.ascontiguousarray(np.tile(bg[None, :], (64, 8))),
        "hnb": np.ascontiguousarray(np.tile(inp["a_head_norm"][0][2 * r:2 * r + 2].reshape(1, 512), (64, 1))),
        "wo": np.ascontiguousarray(a[:, 2048:3072][:, perm]),
        "wout0": np.ascontiguousarray(inp["a_w_out"][0][perm, :]),
        "wout1": np.ascontiguousarray(inp["b_w_o"][0][perm, :]),
        "posT": np.ascontiguousarray(inp["positions"][b, r * T:(r + 1) * T].reshape(T // 128, 128).T.astype(np.int32)),
        "pos_row": np.ascontiguousarray(inp["positions"][b:b + 1, :S].astype(np.int32)),
        "kvmodw": np.ascontiguousarray(inp["kv_mod_w"]), "kvmodb_cols": fm(inp["kv_mod_b"]), "kvng_cols": fm(inp["kv_norm"]),
        "wqa": np.ascontiguousarray(inp["b_w_qa"][0]), "wdkv": np.ascontiguousarray(inp["w_dkv"]),
        "qn_row": np.ascontiguousarray(inp["b_q_norm"][0][None, :]), "kvn_row": np.ascontiguousarray(inp["kv_lora_norm"][None, :]),
        "invb": np.ascontiguousarray(np.tile(_INV.astype(np.float32)[None, :], (128, 1))),
        "wqb": np.ascontiguousarray(np.concatenate([wqb[:, h * 192:(h + 1) * 192] for h in hs], axis=1)),
        "wqbs": np.ascontiguousarray(np.concatenate(
            [np.concatenate([wqb[:, h * 192 + 160:h * 192 + 192], wqb[:, h * 192 + 128:h * 192 + 160]], axis=1) for h in hs], axis=1)),
        "wuk": np.ascontiguousarray(np.concatenate([wukv[:, h * 256:h * 256 + 128] for h in hs], axis=1)),
        "wuv": np.ascontiguousarray(np.concatenate([wukv[:, h * 256 + 128:h * 256 + 256] for h in hs], axis=1)),
    }
    inv2 = np.concatenate([_INV, _INV]).astype(np.float32)[:, None]
    sgn = np.zeros((64, 2), np.float32)
    sgn[0:32, 0] = -2 * np.pi; sgn[0:32, 1] = np.pi; sgn[32:, 0] = 2 * np.pi; sgn[32:, 1] = -np.pi
    kk = np.arange(128)[:, None]; q = np.arange(512)[None, :]
    m.update({"inv2": inv2, "sgn": sgn,
              "maskd": np.stack([(((mm * 128 + kk) // 64) <= (q // 64)).astype(np.float32) for mm in range(4)])})
    for l in range(2):
        m[f"modw{l}"] = np.ascontiguousarray(inp["mod_w"][l])
        m[f"modb_cols{l}"] = fm(inp["mod_b"][l]); m[f"modb_rows{l}"] = np.ascontiguousarray(inp["mod_b"][l][None, :])
        m[f"ng_cols{l}"] = np.ascontiguousarray(np.concatenate([fm(inp["norm_g"][l, i]) for i in range(4)], axis=1))
        m[f"ng_rows{l}"] = np.ascontiguousarray(inp["norm_g"][l])
        m[f"w1_{l}"] = np.ascontiguousarray(inp["ffn_w1"][l]); m[f"w2_{l}"] = np.ascontiguousarray(inp["ffn_w2"][l])
    return m


def kernel(**inp):
    inp = {kk: np.asarray(v) for kk, v in inp.items()}
    B, S = inp["x"].shape[0], inp["x"].shape[1]
    T = S // 2
    cores = list(range(8))
    k = build_fused(S)
    res = run_bass_kernel_spmd(k.nc, [fused_inputs(inp, c, S) for c in cores], core_ids=cores).results
    out = np.empty((B, S, 1024), np.float32)
    for c in cores:
        b, r = c // 2, c % 2
        out[b, r * T:(r + 1) * T] = res[c]["y"]
    return out
```

```python
import ml_dtypes
import contextlib, math
import numpy as np
import concourse.bass as bass
import concourse.mybir as mybir
from concourse.bass_utils import run_bass_kernel_spmd

F32 = mybir.dt.float32
BF16 = mybir.dt.bfloat16
I32 = mybir.dt.int32
AF = mybir.ActivationFunctionType
ALU = mybir.AluOpType
AX = mybir.AxisListType
NDS = 40


class Tile:
    def __init__(self, t, psum=False):
        self.t = t
        self.w = {}
        self.r = {}
        self.psum = psum

    def __getitem__(self, idx):
        return self.t[idx]


class KB:
    def __init__(self):
        self.nc = bass.Bass("TRN2", target_bir_lowering=False)
        self.es = contextlib.ExitStack()
        nc = self.nc
        self.E = {"pe": nc.tensor, "act": nc.scalar, "dve": nc.vector, "pool": nc.gpsimd, "sp": nc.sync}
        self.sem = {e: self.es.enter_context(nc.semaphore("s_" + e)) for e in ["pe", "act", "dve", "pool"]}
        self.cnt = {e: 0 for e in self.sem}
        self.known = {e: {} for e in self.E}
        self.dsems = [self.es.enter_context(nc.semaphore(f"d{i}")) for i in range(NDS)]
        self.dcnt = [0] * NDS
        self.dnext = 0
        self.nins = 0

    def _nm(self, n):
        self.uid = getattr(self, 'uid', 0) + 1
        return f"{n}_{self.uid}"

    def dram_in(self, name, shape, dt=F32):
        return self.nc.dram_tensor(name, list(shape), dt, kind="ExternalInput").ap()

    def dram_out(self, name, shape, dt=F32):
        return self.nc.dram_tensor(name, list(shape), dt, kind="ExternalOutput").ap()

    def sb(self, name, shape, dt=F32):
        return Tile(self.es.enter_context(self.nc.sbuf_tensor(self._nm("sb_" + name), list(shape), dt)))

    def ps(self, name, shape=(128, 512), dt=F32):
        return Tile(self.es.enter_context(self.nc.psum_tensor(self._nm("ps_" + name), list(shape), dt)), psum=True)

    def _need(self, e, key, v):
        if self.known[e].get(key, 0) >= v:
            return
        sem = self.sem[key] if isinstance(key, str) else self.dsems[key]
        self.E[e].wait_ge(sem, v)
        self.known[e][key] = v

    def _deps(self, e, reads, writes):
        for t in reads:
            if t.psum:
                continue
            for k, v in t.w.items():
                if e == "pe" and k == "pe":
                    continue
                self._need(e, k, v)
        for t in list(writes) + [t for t in reads if t.psum]:
            for k, v in list(t.w.items()) + list(t.r.items()):
                if e == "pe" and k == "pe":
                    continue
                self._need(e, k, v)

    def _mark(self, key, ev, reads, writes):
        for t in reads:
            if t.psum:
                t.w[key] = max(t.w.get(key, 0), ev)
            else:
                t.r[key] = max(t.r.get(key, 0), ev)
        for t in writes:
            t.w[key] = max(t.w.get(key, 0), ev)

    def op(self, e, fn, reads=(), writes=(), inc=True):
        self._deps(e, reads, writes)
        ins = fn(self.E[e])
        self.nins += 1
        if inc:
            self.cnt[e] += 1
            ins.then_inc(self.sem[e], 1)
            ev = self.cnt[e]
        else:
            ev = self.cnt[e] + 1
        self._mark(e, ev, reads, writes)
        return ins

    def dma(self, q, out_ap, in_ap, reads=(), writes=(), **kw):
        i = self.dnext
        self.dnext = (self.dnext + 1) % NDS
        if self.dcnt[i] > 0:
            self._need(q, i, self.dcnt[i])
        self._deps(q, reads, writes)
        ins = self.E[q].dma_start(out=out_ap, in_=in_ap, **kw)
        self.nins += 1
        self.dcnt[i] += 16
        ins.then_inc(self.dsems[i], 16)
        self._mark(i, self.dcnt[i], reads, writes)
        return ins

    def finish(self):
        for i in range(NDS):
            if self.dcnt[i] > 0:
                self._need("sp", i, self.dcnt[i])
        for e in self.sem:
            if self.cnt[e] > 0:
                self._need("sp", e, self.cnt[e])

    def mm(self, out_ps, out_ap, lhsT_t, lhsT_ap, rhs_t, rhs_ap, start, stop, inc=None):
        if inc is None:
            inc = stop
        return self.op("pe", lambda E: E.matmul(out_ap, lhsT_ap, rhs_ap, start=start, stop=stop),
                       reads=[lhsT_t, rhs_t], writes=[out_ps], inc=inc)

    def tr(self, out_ps, out_ap, in_t, in_ap, ident_t, ident_ap, inc=True):
        return self.op("pe", lambda E: E.transpose(out_ap, in_ap, ident_ap),
                       reads=[in_t, ident_t], writes=[out_ps], inc=inc)

    def act(self, out_t, out_ap, in_t, in_ap, func, bias=None, scale=None, accum=None, extra_reads=(), e="act"):
        kw = {}
        rd = [in_t] + list(extra_reads)
        wr = [out_t]
        if bias is not None:
            if isinstance(bias, tuple):
                rd.append(bias[0]); kw["bias"] = bias[1]
            else:
                kw["bias"] = bias
        if scale is not None:
            if isinstance(scale, tuple):
                rd.append(scale[0]); kw["scale"] = scale[1]
            else:
                kw["scale"] = scale
        if accum is not None:
            wr.append(accum[0]); kw["accum_out"] = accum[1]
        return self.op("act", lambda E: E.activation(out=out_ap, in_=in_ap, func=func, **kw), reads=rd, writes=wr)


def _scope(self):
    @contextlib.contextmanager
    def cm():
        old = self.es
        self.es = contextlib.ExitStack()
        try:
            yield
        finally:
            self.barrier()
            self.es.close()
            self.es = old
    return cm()


def _barrier(self):
    for e in ["pe", "act", "dve", "pool", "sp"]:
        for e2 in self.sem:
            if self.cnt[e2] > 0:
                self._need(e, e2, self.cnt[e2])
        for i in range(NDS):
            if self.dcnt[i] > 0:
                self._need(e, i, self.dcnt[i])


KB.scope = _scope
KB.barrier = _barrier


EPS = 1e-6
TWO_PI = 2.0 * math.pi


def fm(v):
    v = np.asarray(v, np.float32)
    return np.ascontiguousarray(v.reshape(-1, 128).T)


def bc(ap, n=128):
    return bass.AP(ap.tensor, ap.offset, [[0, n]] + [list(x) for x in ap.ap[1:]])


def ada_cols(k, cact, mw_d, ncols, modb_cols_d, pm, pm_ap, mw, modv):
    noc = ncols // 128
    for kc in range(8):
        k.dma("sp", mw[:, kc, 0:ncols], mw_d[kc * 128:(kc + 1) * 128, :], writes=[mw])
    for oc in range(noc):
        for kc in range(8):
            k.mm(pm, pm_ap[:, oc:oc + 1], mw, mw[:, kc, oc * 128:(oc + 1) * 128], cact, cact[:, kc:kc + 1],
                 kc == 0, kc == 7)
    mb = k.sb("adac_b", [128, noc])
    k.dma("sp", mb[:, :], modb_cols_d, writes=[mb])
    k.op("dve", lambda E: E.tensor_tensor(out=modv[:, 0:noc], in0=pm_ap[:, 0:noc], in1=mb[:, :], op=ALU.add),
         reads=[pm, mb], writes=[modv])


def ada_AB(k, cact, mw_d, modb_cols_d, ng_cols_d, pm, pm_ap, mw, A, modv):
    ada_cols(k, cact, mw_d, 2048, modb_cols_d, pm, pm_ap, mw, modv)
    ng = k.sb("adaab_ng", [128, 8])
    k.dma("sp", ng[:, :], ng_cols_d, writes=[ng])
    k.op("dve", lambda E: E.scalar_tensor_tensor(out=A[:, :], in0=modv[:, 8:16], scalar=1.0, in1=ng[:, :],
                                                 op0=ALU.add, op1=ALU.mult), reads=[modv, ng], writes=[A])


def make_crep(k, cact, name="crep"):
    ones = k.sb(name + "_1", [128, 128])
    k.op("dve", lambda E: E.memset(ones[:, :], 1.0), writes=[ones])
    crep = k.sb(name, [128, 8, 128])
    for kc in range(8):
        k.op("dve", lambda E: E.tensor_scalar(out=crep[:, kc, :], in0=ones[:, :], scalar1=cact[:, kc:kc + 1],
                                              scalar2=None, op0=ALU.mult), reads=[ones, cact], writes=[crep])
    return crep


def ada_rows(k, crep, mw_d, modb_row_d, ng_row_d, pb, mw, Gb):
    for kc in range(8):
        k.dma("sp", mw[:, kc, 0:1024], mw_d[kc * 128:(kc + 1) * 128, :], writes=[mw])
    rb = k.sb("adar_rb", [128, 1024]); rg = k.sb("adar_rg", [128, 1024])
    k.dma("sp", rb[:, :], bc(modb_row_d), writes=[rb])
    k.dma("sp", rg[:, :], bc(ng_row_d), writes=[rg])
    for n2 in range(2):
        for kc in range(8):
            k.mm(pb[n2], pb[n2][:, :], crep, crep[:, kc, :], mw, mw[:, kc, n2 * 512:(n2 + 1) * 512], kc == 0, kc == 7)
        sl = slice(n2 * 512, (n2 + 1) * 512)
        k.op("dve", lambda E: E.tensor_tensor(out=Gb[:, sl], in0=pb[n2][:, :], in1=rb[:, sl], op=ALU.add),
             reads=[pb[n2], rb], writes=[Gb])
    k.op("dve", lambda E: E.tensor_tensor(out=Gb[:, :], in0=Gb[:, :], in1=rg[:, :], op=ALU.mult),
         reads=[Gb, rg], writes=[Gb])


def norm_T(k, xt, j4, ABs, outs, ident, pT, sc):
    ss, rstd, junk, xn, eps = sc["ss"], sc["rstd"], sc["junk"], sc["xn"], sc["eps"]
    for j in range(j4):
        k.act(junk, junk[:, :], xt, xt[:, j, :], AF.Square, scale=1.0 / 32.0, accum=(ss, ss[:, j:j + 1]))
    k.act(rstd, rstd[:, 0:j4], ss, ss[:, 0:j4], AF.Ln, bias=(eps, eps[:, 0:1]))
    k.act(rstd, rstd[:, 0:j4], rstd, rstd[:, 0:j4], AF.Exp, scale=-0.5)
    for j in range(j4):
        k.op("dve", lambda E: E.tensor_scalar(out=xn[:, j, :], in0=xt[:, j, :], scalar1=rstd[:, j:j + 1],
                                              scalar2=None, op0=ALU.mult), reads=[xt, rstd], writes=[xn])
    W = j4 * 128
    for kc in range(8):
        p = pT[kc % 2]
        for j in range(j4):
            k.tr(p, p[:, j * 128:(j + 1) * 128], xn, xn[:, j, kc * 128:(kc + 1) * 128], ident, ident[:, :],
                 inc=(j == j4 - 1))
        for (A, B), hT in zip(ABs, outs):
            k.act(hT, hT[:, kc, 0:W], p, p[:, 0:W], AF.Identity, scale=(A, A[:, kc:kc + 1]), bias=(B, B[:, kc:kc + 1]))


def load_consts(k, ident_d):
    ident = k.sb("ident", [128, 128], BF16)
    k.dma("pool", ident[:, :], ident_d, writes=[ident])
    eps = k.sb("eps", [128, 1])
    k.op("dve", lambda E: E.memset(eps[:, :], EPS), writes=[eps])
    return ident, eps


def load_w(k, name, w_d, K, N, chunk=None):
    kc = K // 128
    t = k.sb(name, [128, kc, N], BF16)
    v = w_d.rearrange("(k p) n -> p k n", p=128)
    step = chunk or kc
    for c0 in range(0, kc, step):
        k.dma("pool", t[:, c0:c0 + step, :], v[:, c0:c0 + step, :], writes=[t])
    return t


def sandwich(k, pY, xrow_t, xrow_ap_fn, Gb, sc2, eps):
    ss2, rs, tmp, junk = sc2["ss2"], sc2["rs"], sc2["tmp"], sc2["junk"]
    for n2 in range(2):
        k.act(junk, junk[:, 0:512], pY[n2], pY[n2][:, :], AF.Square, scale=1.0 / 32.0, accum=(ss2, ss2[:, n2:n2 + 1]))
    k.op("dve", lambda E: E.tensor_tensor(out=rs[:, 0:1], in0=ss2[:, 0:1], in1=ss2[:, 1:2], op=ALU.add),
         reads=[ss2], writes=[rs])
    k.act(rs, rs[:, 1:2], rs, rs[:, 0:1], AF.Ln, bias=(eps, eps[:, 0:1]))
    k.act(rs, rs[:, 2:3], rs, rs[:, 1:2], AF.Exp, scale=-0.5)
    for n2 in range(2):
        sl = slice(n2 * 512, (n2 + 1) * 512)
        tm = tmp[n2]
        k.op("dve", lambda E: E.scalar_tensor_tensor(out=tm[:, :], in0=pY[n2][:, :], scalar=rs[:, 2:3], in1=Gb[:, sl],
                                                     op0=ALU.mult, op1=ALU.mult), reads=[pY[n2], rs, Gb], writes=[tm])
        xa = xrow_ap_fn(sl)
        k.op("pool", lambda E: E.tensor_tensor(out=xa, in0=xa, in1=tm[:, :], op=ALU.add),
             reads=[xrow_t, tm], writes=[xrow_t])


def emit_postmix(k, T, x_d, mix_parts, x1_d, cT_d, modw_d, modb_cols_d, modb_rows_d, ng_cols_d, ng_rows_d,
                 wo_d, wout_d, ident_d, gate, mix_dt=F32):
    NT = T // 512
    with k.scope():
        ident, eps = load_consts(k, ident_d)
        pT = [k.ps("pT0", [128, 1024], BF16), k.ps("pT1", [128, 1024], BF16)]
        pA = [k.ps("pA0"), k.ps("pA1")]; pY = [k.ps("pY0"), k.ps("pY1")]
        A1 = k.sb("A1", [128, 8]); mv1 = k.sb("mv1", [128, 16]); Gb1 = k.sb("Gb1", [128, 1024])
        with k.scope():
            cact = k.sb("cact", [128, 8])
            k.dma("sp", cact[:, :], cT_d, writes=[cact])
            k.act(cact, cact[:, :], cact, cact[:, :], AF.Silu)
            mw = k.sb("mw", [128, 8, 2048])
            if gate:
                ada_AB(k, cact, modw_d[:, 0:2048], modb_cols_d[:, 0:16], ng_cols_d[:, 0:8], pA[0], pA[0][:, 0:16],
                       mw, A1, mv1)
            crep = make_crep(k, cact)
            ada_rows(k, crep, modw_d[:, 2048:3072], modb_rows_d[0:1, 2048:3072], ng_rows_d[1:2, :], pY, mw, Gb1)
        wout = load_w(k, "wout", wout_d, 1024, 1024)
        wo = load_w(k, "wo", wo_d, 1024, 1024) if gate else None
        xt = [k.sb(f"xt{i}", [128, 4, 1024]) for i in range(2)]
        hts = [k.sb(f"ht{i}", [128, 4, 1024], mix_dt) for i in range(2)]
        if gate or mix_dt != BF16:
            gateds = [k.sb(f"gated{i}", [128, 4, 1024], BF16) for i in range(2)]
        else:
            gateds = hts
        gTs = [k.sb(f"gT{i}", [128, 8, 512], BF16) for i in range(2)]
        sc = dict(ss=k.sb("ss", [128, 4]), rstd=k.sb("rstd", [128, 4]), junk=k.sb("junk", [128, 1024], BF16),
                  xn=k.sb("xn", [128, 4, 1024], BF16), eps=eps)
        hT = k.sb("hT", [128, 8, 512], BF16) if gate else None
        sg = [k.sb(f"sg{i}", [128, 512]) for i in range(2)]
        sc2 = dict(ss2=k.sb("ss2", [128, 2]), rs=k.sb("rs", [128, 4]),
                   tmp=[k.sb("tmpa", [128, 512]), k.sb("tmpb", [128, 512])], junk=sc["junk"])

        def load(t):
            r = slice(t * 512, (t + 1) * 512)
            k.dma("sp", xt[t % 2][:, :, :], x_d[r, :].rearrange("(j p) d -> p j d", p=128), writes=[xt[t % 2]])
            ht = hts[t % 2]
            for i, part in enumerate(mix_parts):
                k.dma("sp", ht[:, :, i * 512:(i + 1) * 512], part[r, :].rearrange("(j p) d -> p j d", p=128), writes=[ht])

        def stage_a(t):
            X = xt[t % 2]; ht = hts[t % 2]; gated = gateds[t % 2]; gT = gTs[t % 2]
            if gate:
                norm_T(k, X, 4, [(A1, mv1)], [hT], ident, pT, sc)
                for j in range(4):
                    for n2 in range(2):
                        sl = slice(n2 * 512, (n2 + 1) * 512)
                        for kc in range(8):
                            k.mm(pA[n2], pA[n2][:, :], hT, hT[:, kc, j * 128:(j + 1) * 128], wo, wo[:, kc, sl],
                                 kc == 0, kc == 7)
                        s_ = sg[n2]
                        k.act(s_, s_[:, :], pA[n2], pA[n2][:, :], AF.Sigmoid)
                        k.op("dve", lambda E: E.tensor_tensor(out=gated[:, j, sl], in0=s_[:, :], in1=ht[:, j, sl],
                                                              op=ALU.mult), reads=[s_, ht], writes=[gated])
            elif gated is not ht:
                for j in range(4):
                    k.op("dve", lambda E: E.tensor_copy(out=gated[:, j, :], in_=ht[:, j, :]), reads=[ht], writes=[gated])
            for c in range(8):
                p = pT[c % 2]
                for j in range(4):
                    k.tr(p, p[:, j * 128:(j + 1) * 128], gated, gated[:, j, c * 128:(c + 1) * 128], ident, ident[:, :],
                         inc=(j == 3))
                if c % 2 == 0:
                    k.act(gT, gT[:, c, :], p, p[:, 0:512], AF.Copy)
                else:
                    k.op("dve", lambda E: E.tensor_copy(out=gT[:, c, :], in_=p[:, 0:512]), reads=[p], writes=[gT])

        def stage_b(t):
            r = slice(t * 512, (t + 1) * 512)
            X = xt[t % 2]; gT = gTs[t % 2]
            for j in range(4):
                for n2 in range(2):
                    sl = slice(n2 * 512, (n2 + 1) * 512)
                    for c in range(8):
                        k.mm(pY[n2], pY[n2][:, :], gT, gT[:, c, j * 128:(j + 1) * 128], wout, wout[:, c, sl],
                             c == 0, c == 7)
                sandwich(k, pY, X, lambda sl, X=X, j=j: X[:, j, sl], Gb1, sc2, eps)
            k.dma("pool", x1_d[r, :].rearrange("(j p) d -> p j d", p=128), X[:, :, :], reads=[X])

        load(0)
        stage_a(0)
        for t in range(NT):
            if t + 1 < NT:
                load(t + 1)
                stage_a(t + 1)
            stage_b(t)


def emit_ffn(k, T, xin_d, xout_d, cT_d, modw_d, modb_cols_d, modb_rows_d, ng_cols_d, ng_rows_d, w1_d, w2_d, ident_d):
    TT = 256
    NT = T // TT
    with k.scope():
        ident, eps = load_consts(k, ident_d)
        pT = [k.ps("pT0", [128, 1024], BF16), k.ps("pT1", [128, 1024], BF16)]
        pU = [k.ps("pU0"), k.ps("pU1")]; pY = [k.ps("pY0"), k.ps("pY1")]
        A3 = k.sb("A3", [128, 8]); mv3 = k.sb("mv3", [128, 16]); Gb2 = k.sb("Gb2", [128, 1024])
        with k.scope():
            cact = k.sb("cact", [128, 8])
            k.dma("sp", cact[:, :], cT_d, writes=[cact])
            k.act(cact, cact[:, :], cact, cact[:, :], AF.Silu)
            mw = k.sb("mw", [128, 8, 2048])
            ada_AB(k, cact, modw_d[:, 3072:5120], modb_cols_d[:, 24:40], ng_cols_d[:, 16:24], pU[0], pU[0][:, 0:16],
                   mw, A3, mv3)
            crep = make_crep(k, cact)
            ada_rows(k, crep, modw_d[:, 5120:6144], modb_rows_d[0:1, 5120:6144], ng_rows_d[3:4, :], pY, mw, Gb2)
        W1 = load_w(k, "W1", w1_d, 1024, 4096, chunk=1)
        W2 = load_w(k, "W2", w2_d, 4096, 1024, chunk=4)
        xt = [k.sb(f"xt{i}", [128, 2, 1024]) for i in range(2)]
        sc = dict(ss=k.sb("ss", [128, 4]), rstd=k.sb("rstd", [128, 4]), junk=k.sb("junk", [128, 1024], BF16),
                  xn=k.sb("xn", [128, 2, 1024], BF16), eps=eps)
        hT = [k.sb(f"hT{i}", [128, 8, TT], BF16) for i in range(2)]
        uT = k.sb("uT", [128, 32, TT], BF16)
        rr = [k.sb(f"rr{i}", [128, TT]) for i in range(2)]
        sc2 = dict(ss2=k.sb("ss2", [128, 2]), rs=k.sb("rs", [128, 4]),
                   tmp=[k.sb("tmpa", [128, 512]), k.sb("tmpb", [128, 512])], junk=sc["junk"])

        def load(t):
            r = slice(t * TT, (t + 1) * TT)
            k.dma("sp", xt[t % 2][:, :, :], xin_d[r, :].rearrange("(j p) d -> p j d", p=128), writes=[xt[t % 2]])

        load(0)
        norm_T(k, xt[0], 2, [(A3, mv3)], [hT[0]], ident, pT, sc)
        for t in range(NT):
            r = slice(t * TT, (t + 1) * TT)
            X = xt[t % 2]; H = hT[t % 2]
            if t + 1 < NT:
                load(t + 1)
            for fc in range(32):
                p = pU[fc % 2]
                for kc in range(8):
                    k.mm(p, p[:, 0:TT], W1, W1[:, kc, fc * 128:(fc + 1) * 128], H, H[:, kc, :], kc == 0, kc == 7)
                r_ = rr[fc % 2]
                k.act(r_, r_[:, :], p, p[:, 0:TT], AF.Relu)
                k.op("dve", lambda E: E.tensor_tensor(out=uT[:, fc, :], in0=r_[:, :], in1=r_[:, :], op=ALU.mult),
                     reads=[r_], writes=[uT])
            if t + 1 < NT:
                norm_T(k, xt[(t + 1) % 2], 2, [(A3, mv3)], [hT[(t + 1) % 2]], ident, pT, sc)
            for j in range(2):
                for n2 in range(2):
                    sl = slice(n2 * 512, (n2 + 1) * 512)
                    for fc in range(32):
                        k.mm(pY[n2], pY[n2][:, :], uT, uT[:, fc, j * 128:(j + 1) * 128], W2, W2[:, fc, sl],
                             fc == 0, fc == 31)
                sandwich(k, pY, X, lambda sl, X=X, j=j: X[:, j, sl], Gb2, sc2, eps)
            k.dma("pool", xout_d[r, :].rearrange("(j p) d -> p j d", p=128), X[:, :, :], reads=[X])


def rope_sincos(k, tt, ti, tf, out_sc, scale_ap, bias_ap, shape_ap):
    a = shape_ap
    k.op("dve", lambda E: E.tensor_copy(out=a(ti), in_=a(tt)), reads=[tt], writes=[ti])
    k.op("dve", lambda E: E.tensor_copy(out=a(tf), in_=a(ti)), reads=[ti], writes=[tf])
    k.op("dve", lambda E: E.tensor_tensor(out=a(tf), in0=a(tt), in1=a(tf), op=ALU.subtract), reads=[tt, tf], writes=[tf])
    k.op("dve", lambda E: E.scalar_tensor_tensor(out=a(tt), in0=a(tf), scalar=0.0, in1=a(tf), op0=ALU.is_lt, op1=ALU.add),
         reads=[tf], writes=[tt])
    k.act(out_sc, a(out_sc), tt, a(tt), AF.Sin, scale=scale_ap, bias=bias_ap)


def emit_l1prep(k, T, x_d, posT_d, qlatT_d, ckvT_d, kropeT_d, cT_d, modw_d, modb_cols_d, ng_cols_d,
                kvmodw_d, kvmodb_cols_d, kvng_cols_d, wqa_d, wdkv_d, qn_row_d, kvn_row_d, invb_d, ident_d):
    NT = T // 512
    with k.scope():
        ident, eps = load_consts(k, ident_d)
        pT = [k.ps("pT0", [128, 1024], BF16), k.ps("pT1", [128, 1024], BF16)]
        pQ = k.ps("pQ"); pKV = k.ps("pKV"); pm = k.ps("pm")
        Aq = k.sb("Aq", [128, 8]); mvq = k.sb("mvq", [128, 16]); Akv = k.sb("Akv", [128, 8]); mvkv = k.sb("mvkv", [128, 16])
        with k.scope():
            cact = k.sb("cact", [128, 8])
            k.dma("sp", cact[:, :], cT_d, writes=[cact])
            k.act(cact, cact[:, :], cact, cact[:, :], AF.Silu)
            mw = k.sb("mw", [128, 8, 2048])
            ada_AB(k, cact, modw_d[:, 0:2048], modb_cols_d[:, 0:16], ng_cols_d[:, 0:8], pm, pm[:, 0:16], mw, Aq, mvq)
            ada_AB(k, cact, kvmodw_d, kvmodb_cols_d, kvng_cols_d, pm, pm[:, 16:32], mw, Akv, mvkv)
        wqa = load_w(k, "wqa", wqa_d, 1024, 384)
        wdkv = load_w(k, "wdkv", wdkv_d, 1024, 320)
        qnb = k.sb("qnb", [128, 384]); kvnb = k.sb("kvnb", [128, 256]); invb = k.sb("invb", [128, 32])
        k.dma("sp", qnb[:, :], bc(qn_row_d), writes=[qnb]); k.dma("sp", kvnb[:, :], bc(kvn_row_d), writes=[kvnb])
        k.dma("sp", invb[:, :], invb_d, writes=[invb])
        negpi = k.sb("negpi", [128, 1])
        k.op("dve", lambda E: E.memset(negpi[:, :], -math.pi), writes=[negpi])
        xt = [k.sb(f"xt{i}", [128, 4, 1024]) for i in range(2)]
        sc = dict(ss=k.sb("ss", [128, 4]), rstd=k.sb("rstd", [128, 4]), junk=k.sb("junk", [128, 1024], BF16),
                  xn=k.sb("xn", [128, 4, 1024], BF16), eps=eps)
        h1T = k.sb("h1T", [128, 8, 512], BF16); hsT = k.sb("hsT", [128, 8, 512], BF16)
        posi = k.sb("posi", [128, 4], I32); posf = k.sb("posf", [128, 4])
        tt = k.sb("tt", [128, 2, 4, 32]); ti = k.sb("ti", [128, 2, 4, 32], I32); tf = k.sb("tf", [128, 2, 4, 32])
        scs = k.sb("scs", [128, 2, 4, 32])
        st = k.sb("st", [128, 8]); junk2 = k.sb("junk2", [128, 384], BF16)
        qn = k.sb("qn", [128, 384], BF16); cn = k.sb("cn", [128, 256], BF16); kr = k.sb("kr", [128, 64], BF16)
        r1 = k.sb("r1", [128, 64]); r2 = k.sb("r2", [128, 64])
        qlT = k.sb("qlT", [128, 3, 512], BF16); ckT = k.sb("ckT", [128, 2, 512], BF16); krT = k.sb("krT", [64, 512], BF16)

        def load(t):
            r = slice(t * 512, (t + 1) * 512)
            k.dma("sp", xt[t % 2][:, :, :], x_d[r, :].rearrange("(j p) d -> p j d", p=128), writes=[xt[t % 2]])

        full = lambda T_: T_[:, :, :, :]
        load(0)
        for t in range(NT):
            r = slice(t * 512, (t + 1) * 512)
            if t + 1 < NT:
                load(t + 1)
            X = xt[t % 2]
            k.dma("sp", posi[:, :], posT_d[:, t * 4:(t + 1) * 4], writes=[posi])
            k.op("dve", lambda E: E.tensor_copy(out=posf[:, :], in_=posi[:, :]), reads=[posi], writes=[posf])
            for j in range(4):
                k.op("dve", lambda E: E.tensor_scalar(out=tt[:, 0, j, :], in0=invb[:, :], scalar1=posf[:, j:j + 1],
                                                      scalar2=0.5, op0=ALU.mult, op1=ALU.add), reads=[invb, posf], writes=[tt])
            k.op("dve", lambda E: E.tensor_scalar(out=tt[:, 1, :, :], in0=tt[:, 0, :, :], scalar1=0.25, scalar2=None,
                                                  op0=ALU.add), reads=[tt], writes=[tt])
            rope_sincos(k, tt, ti, tf, scs, TWO_PI, (negpi, negpi[:, 0:1]), full)
            norm_T(k, X, 4, [(Aq, mvq), (Akv, mvkv)], [h1T, hsT], ident, pT, sc)
            for j in range(4):
                js = slice(j * 128, (j + 1) * 128)
                for kc in range(8):
                    k.mm(pQ, pQ[:, 0:384], h1T, h1T[:, kc, js], wqa, wqa[:, kc, :], kc == 0, kc == 7)
                for kc in range(8):
                    k.mm(pKV, pKV[:, 0:320], hsT, hsT[:, kc, js], wdkv, wdkv[:, kc, :], kc == 0, kc == 7)
                k.act(junk2, junk2[:, 0:384], pQ, pQ[:, 0:384], AF.Square, scale=384.0 ** -0.5, accum=(st, st[:, 0:1]))
                k.act(junk2, junk2[:, 0:256], pKV, pKV[:, 0:256], AF.Square, scale=1.0 / 16.0, accum=(st, st[:, 1:2]))
                k.act(st, st[:, 2:4], st, st[:, 0:2], AF.Ln, bias=(eps, eps[:, 0:1]))
                k.act(st, st[:, 4:6], st, st[:, 2:4], AF.Exp, scale=-0.5)
                k.op("dve", lambda E: E.scalar_tensor_tensor(out=qn[:, :], in0=pQ[:, 0:384], scalar=st[:, 4:5], in1=qnb[:, :],
                                                             op0=ALU.mult, op1=ALU.mult), reads=[pQ, st, qnb], writes=[qn])
                k.op("dve", lambda E: E.scalar_tensor_tensor(out=cn[:, :], in0=pKV[:, 0:256], scalar=st[:, 5:6], in1=kvnb[:, :],
                                                             op0=ALU.mult, op1=ALU.mult), reads=[pKV, st, kvnb], writes=[cn])
                sin_ = scs[:, 0, j, :]; cos_ = scs[:, 1, j, :]
                k.act(r1, r1[:, :], pKV, pKV[:, 256:320], AF.Copy)
                k.op("dve", lambda E: E.tensor_tensor(out=r2[:, 0:32], in0=r1[:, 32:64], in1=sin_, op=ALU.mult), reads=[r1, scs], writes=[r2])
                k.op("dve", lambda E: E.tensor_tensor(out=r2[:, 32:64], in0=r1[:, 0:32], in1=sin_, op=ALU.mult), reads=[r1, scs], writes=[r2])
                k.op("pool", lambda E: E.tensor_tensor(out=r1[:, 0:32], in0=r1[:, 0:32], in1=cos_, op=ALU.mult), reads=[r1, scs, r2], writes=[r1])
                k.op("pool", lambda E: E.tensor_tensor(out=r1[:, 32:64], in0=r1[:, 32:64], in1=cos_, op=ALU.mult), reads=[r1, scs], writes=[r1])
                k.op("dve", lambda E: E.tensor_tensor(out=kr[:, 0:32], in0=r1[:, 0:32], in1=r2[:, 0:32], op=ALU.subtract), reads=[r1, r2], writes=[kr])
                k.op("dve", lambda E: E.tensor_tensor(out=kr[:, 32:64], in0=r1[:, 32:64], in1=r2[:, 32:64], op=ALU.add), reads=[r1, r2], writes=[kr])
                p = pT[j % 2]
                for c in range(3):
                    k.tr(p, p[:, c * 128:(c + 1) * 128], qn, qn[:, c * 128:(c + 1) * 128], ident, ident[:, :], inc=False)
                for c in range(2):
                    k.tr(p, p[:, (3 + c) * 128:(4 + c) * 128], cn, cn[:, c * 128:(c + 1) * 128], ident, ident[:, :], inc=False)
                k.tr(p, p[0:64, 640:768], kr, kr[:, :], ident, ident[:, :], inc=True)
                k.act(qlT, qlT[:, :, js], p, p[:, 0:384].rearrange("p (c t) -> p c t", c=3), AF.Copy)
                k.op("dve", lambda E: E.tensor_copy(out=ckT[:, :, js], in_=p[:, 384:640].rearrange("p (c t) -> p c t", c=2)),
                     reads=[p], writes=[ckT])
                k.op("dve", lambda E: E.tensor_copy(out=krT[:, js], in_=p[0:64, 640:768]), reads=[p], writes=[krT])
            k.dma("pool", qlatT_d.rearrange("(c p) t -> p c t", p=128)[:, :, r], qlT[:, :, :], reads=[qlT])
            k.dma("pool", ckvT_d.rearrange("(c p) t -> p c t", p=128)[:, :, r], ckT[:, :, :], reads=[ckT])
            k.dma("pool", kropeT_d[:, r], krT[:, :], reads=[krT])


def emit_attn(k, S, qlatT_d, ckvT_d, kropeT_d, pos_row_d, wqb_d, wqbs_d, wuk_d, wuv_d, inv2_d, sgn_d, maskd_d, out_d):
    NQ = S // 512
    NKB = S // 128
    SCALE = 192.0 ** -0.5
    with k.scope():
        pS = [k.ps("pS0"), k.ps("pS1")]
        pAcc = [k.ps("pAcc0"), k.ps("pAcc1")]
        pQ = [k.ps("pQ0"), k.ps("pQ1")]
        wqb = load_w(k, "wqb", wqb_d, 384, 768); wqbs = load_w(k, "wqbs", wqbs_d, 384, 256)
        wuk = load_w(k, "wuk", wuk_d, 256, 512); wuv = load_w(k, "wuv", wuv_d, 256, 512)
        krT = k.sb("krT", [64, S], BF16)
        k.dma("sp", krT[:, :], kropeT_d, writes=[krT])
        knT = k.sb("knT", [128, 4, S], BF16)
        vext = k.sb("vext", [128, NKB, 4, 129], BF16)
        k.op("pool", lambda E: E.memset(vext[:, :, :, 128:129], 1.0), writes=[vext])
        maskd = k.sb("maskd", [128, 4, 512], BF16)
        k.dma("pool", maskd[:, :, :], maskd_d.rearrange("m p q -> p m q"), writes=[maskd])
        inv2 = k.sb("inv2", [64, 1]); sgn = k.sb("sgn", [64, 2])
        k.dma("sp", inv2[:, :], inv2_d, writes=[inv2]); k.dma("sp", sgn[:, :], sgn_d, writes=[sgn])
        with k.scope():
            ckT = k.sb("ckT", [128, 2, S], BF16)
            k.dma("sp", ckT[:, :, :], ckvT_d.rearrange("(c p) t -> p c t", p=128), writes=[ckT])
            for tq in range(NQ):
                ts = slice(tq * 512, (tq + 1) * 512)
                for h in range(4):
                    p = pQ[h % 2]
                    for c in range(2):
                        k.mm(p, p[:, :], wuk, wuk[:, c, h * 128:(h + 1) * 128], ckT, ckT[:, c, ts], c == 0, c == 1)
                    if h % 2 == 0:
                        k.act(knT, knT[:, h, ts], p, p[:, :], AF.Copy)
                    else:
                        k.op("dve", lambda E: E.tensor_copy(out=knT[:, h, ts], in_=p[:, :]), reads=[p], writes=[knT])
            for kb in range(NKB):
                p = pS[kb % 2]
                for c in range(2):
                    k.mm(p, p[:, :], ckT, ckT[:, c, kb * 128:(kb + 1) * 128], wuv, wuv[:, c, :], c == 0, c == 1)
                if kb % 2 == 0:
                    k.act(vext, vext[:, kb, :, 0:128], p, p[:, :].rearrange("p (h e) -> p h e", h=4), AF.Copy)
                else:
                    k.op("dve", lambda E: E.tensor_copy(out=vext[:, kb, :, 0:128],
                                                        in_=p[:, :].rearrange("p (h e) -> p h e", h=4)),
                         reads=[p], writes=[vext])
        qlT = [k.sb(f"qlT{i}", [128, 3, 512], BF16) for i in range(2)]
        posi = k.sb("posi", [64, 512], I32)
        tt = k.sb("tt", [64, 2, 512]); ti = k.sb("ti", [64, 2, 512], I32); tf = k.sb("tf", [64, 2, 512])
        cs2 = k.sb("cs2", [64, 2, 512])
        qnT = [k.sb(f"qnT{i}", [128, 512], BF16) for i in range(2)]
        qrT = [k.sb(f"qrT{i}", [64, 512], BF16) for i in range(2)]
        ra = k.sb("ra", [64, 512]); rb = k.sb("rb", [64, 512])
        PTb = [k.sb(f"PTb{i}", [128, 512], BF16) for i in range(3)]
        ot = [k.sb(f"ot{i}", [128, 4, 512]) for i in range(1)]
        rec = k.sb("rec", [128, 4])
        negpi = k.sb("negpi", [64, 1]); twopi = k.sb("twopi", [64, 1])
        k.op("dve", lambda E: E.memset(negpi[:, :], -math.pi), writes=[negpi])
        k.op("dve", lambda E: E.memset(twopi[:, :], TWO_PI), writes=[twopi])
        qv = qlatT_d.rearrange("(c p) t -> p c t", p=128)
        unit = 0
        for tq in range(NQ):
            q0 = tq * 512
            ts = slice(q0, q0 + 512)
            QL = qlT[tq % 2]
            k.dma("sp", QL[:, :, :], qv[:, :, ts], writes=[QL])
            k.dma("sp", posi[:, :], bc(pos_row_d[0:1, ts], 64), writes=[posi])
            k.op("dve", lambda E: E.tensor_copy(out=tf[:, 0, :], in_=posi[:, :]), reads=[posi], writes=[tf])
            k.op("dve", lambda E: E.tensor_scalar(out=tt[:, 0, :], in0=tf[:, 0, :], scalar1=inv2[:, 0:1], scalar2=0.5,
                                                  op0=ALU.mult, op1=ALU.add), reads=[tf, inv2], writes=[tt])
            k.op("dve", lambda E: E.tensor_scalar(out=tt[:, 1, :], in0=tt[:, 0, :], scalar1=0.25, scalar2=None,
                                                  op0=ALU.add), reads=[tt], writes=[tt])
            a3 = lambda T_: T_[:, :, :]
            k.op("dve", lambda E: E.tensor_copy(out=a3(ti), in_=a3(tt)), reads=[tt], writes=[ti])
            k.op("dve", lambda E: E.tensor_copy(out=a3(tf), in_=a3(ti)), reads=[ti], writes=[tf])
            k.op("dve", lambda E: E.tensor_tensor(out=a3(tf), in0=a3(tt), in1=a3(tf), op=ALU.subtract), reads=[tt, tf], writes=[tf])
            k.op("dve", lambda E: E.scalar_tensor_tensor(out=a3(tt), in0=a3(tf), scalar=0.0, in1=a3(tf), op0=ALU.is_lt,
                                                         op1=ALU.add), reads=[tf], writes=[tt])
            k.act(cs2, cs2[:, 0, :], tt, tt[:, 0, :], AF.Sin, scale=(sgn, sgn[:, 0:1]), bias=(sgn, sgn[:, 1:2]))
            k.act(cs2, cs2[:, 1, :], tt, tt[:, 1, :], AF.Sin, scale=(twopi, twopi[:, 0:1]), bias=(negpi, negpi[:, 0:1]))
            O = ot[0]
            nkb = (q0 + 512) // 128

            def qproj(h):
                QN = qnT[h % 2]; QR = qrT[h % 2]
                p = pQ[0]
                for c in range(3):
                    k.mm(p, p[:, :], wqb, wqb[:, c, h * 192:h * 192 + 128], QL, QL[:, c, :], c == 0, c == 2)
                k.act(QN, QN[:, :], p, p[:, :], AF.Copy)
                p = pQ[1]
                for c in range(3):
                    k.mm(p, p[0:64, :], wqb, wqb[:, c, h * 192 + 128:h * 192 + 192], QL, QL[:, c, :], c == 0, c == 2)
                k.op("dve", lambda E: E.tensor_tensor(out=ra[:, :], in0=p[0:64, :], in1=cs2[:, 1, :], op=ALU.mult),
                     reads=[p, cs2], writes=[ra])
                for c in range(3):
                    k.mm(p, p[0:64, :], wqbs, wqbs[:, c, h * 64:(h + 1) * 64], QL, QL[:, c, :], c == 0, c == 2)
                k.op("dve", lambda E: E.tensor_tensor(out=rb[:, :], in0=p[0:64, :], in1=cs2[:, 0, :], op=ALU.mult),
                     reads=[p, cs2], writes=[rb])
                k.op("pool", lambda E: E.tensor_tensor(out=QR[:, :], in0=ra[:, :], in1=rb[:, :], op=ALU.add),
                     reads=[ra, rb], writes=[QR])

            def smm(h, kb, u):
                ks = slice(kb * 128, kb * 128 + 128)
                ps_ = pS[u % 2]
                k.mm(ps_, ps_[:, :], knT, knT[:, h, ks], qnT[h % 2], qnT[h % 2][:, :], True, False)
                k.mm(ps_, ps_[:, :], krT, krT[:, ks], qrT[h % 2], qrT[h % 2][:, :], False, True)

            qproj(0)
            for h in range(4):
                u0 = unit
                smm(h, 0, u0)
                if h + 1 < 4:
                    qproj(h + 1)
                for kb in range(nkb):
                    k0 = kb * 128
                    u = u0 + kb
                    ps_ = pS[u % 2]; PB = PTb[u % 3]
                    if kb + 1 < nkb:
                        smm(h, kb + 1, u + 1)
                    k.act(PB, PB[:, :], ps_, ps_[:, :], AF.Exp, scale=SCALE)
                    if k0 >= q0:
                        m = (k0 - q0) // 128
                        k.op("dve", lambda E: E.tensor_tensor(out=PB[:, :], in0=PB[:, :], in1=maskd[:, m, :], op=ALU.mult),
                             reads=[PB, maskd], writes=[PB])
                    qss = [qs for qs in range(4) if not (k0 > q0 + qs * 128 + 127)]
                    for qs in qss:
                        acc = pAcc[qs // 2]
                        a_ap = acc[:, (qs % 2) * 129:(qs % 2) * 129 + 129]
                        first = (kb == 0)
                        last = (kb == (q0 + qs * 128) // 128)
                        k.op("pe", lambda E: E.matmul(a_ap, PB[:, qs * 128:(qs + 1) * 128], vext[:, kb, h, :],
                                                      start=(first and qs % 2 == 0), stop=last, skip_group_check=True),
                             reads=[PB, vext], writes=[acc], inc=(qs == qss[-1]))
                unit = u0 + nkb
                for qs in range(4):
                    acc = pAcc[qs // 2]
                    a_ap = acc[:, (qs % 2) * 129:(qs % 2) * 129 + 129]
                    k.op("dve", lambda E: E.reciprocal(out=rec[:, qs:qs + 1], in_=a_ap[:, 128:129]), reads=[acc], writes=[rec])
                    k.op("dve", lambda E: E.tensor_scalar(out=O[:, qs, h * 128:(h + 1) * 128], in0=a_ap[:, 0:128],
                                                          scalar1=rec[:, qs:qs + 1], scalar2=None, op0=ALU.mult),
                         reads=[acc, rec], writes=[O])
            k.dma("pool", out_d[ts, :].rearrange("(j p) d -> p j d", p=128), O[:, :, :], reads=[O])


def rows_to_T(k, xt, j4, A, B, ident, pT, hT, sc):
    norm_T(k, xt, j4, [(A, B)], [hT], ident, pT, sc)


def adaln_cols(k, cact, mw_dram, ncols, modb_sb, pm, pm_ap, mw, name):
    noc = ncols // 128
    for kc in range(8):
        k.dma("sp", mw[:, kc, 0:ncols], mw_dram[kc * 128:(kc + 1) * 128, :], writes=[mw])
    for oc in range(noc):
        for kc in range(8):
            k.mm(pm, pm_ap[:, oc:oc + 1], mw, mw[:, kc, oc * 128:(oc + 1) * 128], cact, cact[:, kc:kc + 1],
                 kc == 0, kc == 7)
    modv = k.sb(name, [128, noc])
    k.op("dve", lambda E: E.tensor_tensor(out=modv[:, :], in0=pm_ap[:, 0:noc], in1=modb_sb[:, 0:noc], op=ALU.add),
         reads=[pm, modb_sb], writes=[modv])
    return modv


def emit_mlstm(k, S, x, cT, modw, modb, ng, wq, wk, wv, wg, bgb, hnb, ident_d, tri_d, hh):
    NT = S // 512
    with k.scope():
        ident = k.sb("ident", [128, 128], BF16); tri = k.sb("tri", [64, 64]); ones = k.sb("ones", [64, 128])
        bgb_sb = k.sb("bgb_sb", [64, 32]); hnb_sb = k.sb("hnb_sb", [64, 512])
        eps = k.sb("eps", [128, 1]); one_c = k.sb("one_c", [128, 1]); lnsc = k.sb("lnsc", [128, 1])
        cact = k.sb("cact", [128, 8]); modb_sb = k.sb("modb_sb", [128, 16]); ng_sb = k.sb("ng_sb", [128, 8])
        k.dma("pool", ident[:, :], ident_d, writes=[ident])
        k.dma("sp", tri[:, :], tri_d, writes=[tri])
        k.dma("sp", bgb_sb[:, :], bgb, writes=[bgb_sb]); k.dma("sp", hnb_sb[:, :], hnb, writes=[hnb_sb])
        k.dma("sp", cact[:, :], cT, writes=[cact]); k.dma("sp", modb_sb[:, :], modb, writes=[modb_sb])
        k.dma("sp", ng_sb[:, :], ng, writes=[ng_sb])
        k.op("dve", lambda E: E.memset(ones[:, :], 1.0), writes=[ones])
        k.op("dve", lambda E: E.memset(eps[:, :], EPS), writes=[eps])
        k.op("dve", lambda E: E.memset(one_c[:, :], 1.0), writes=[one_c])
        k.op("dve", lambda E: E.memset(lnsc[:, :], math.log(128.0 ** -0.5)), writes=[lnsc])
        wq_sb = k.sb("wq_sb", [128, 8, 256], BF16); wk_sb = k.sb("wk_sb", [128, 8, 256], BF16)
        wv_sb = k.sb("wv_sb", [128, 8, 512], BF16); wg_sb = k.sb("wg_sb", [128, 8, 4], BF16)
        for (t, d) in ((wq_sb, wq), (wk_sb, wk), (wv_sb, wv), (wg_sb, wg)):
            k.dma("pool", t[:, :, :], d.rearrange("(k p) n -> p k n", p=128), writes=[t])

        pT = [k.ps("pT0", [128, 1024], BF16), k.ps("pT1", [128, 1024], BF16)]
        pqk = k.ps("pqk"); pmisc = k.ps("pmisc")
        pN = [k.ps("pN0"), k.ps("pN1")]; pD = [k.ps("pD0"), k.ps("pD1")]
        pk_ap = pmisc[0:64, 0:256]; pg_ap = pmisc[0:64, 256:288]; pcs_ap = pmisc[0:64, 288:304]
        ptot_ap = pmisc[:, 304:320]; pmod_ap = pmisc[:, 320:336]

        k.act(cact, cact[:, :], cact, cact[:, :], AF.Silu)
        modv = k.sb("modv", [128, 16])
        with k.scope():
            mw = k.sb("mw", [128, 8, 2048])
            ada_cols(k, cact, modw, 2048, modb, pmisc, pmod_ap, mw, modv)
        A1 = k.sb("A1", [128, 8])
        k.op("dve", lambda E: E.scalar_tensor_tensor(out=A1[:, :], in0=modv[:, 8:16], scalar=1.0, in1=ng_sb[:, :],
                                                     op0=ALU.add, op1=ALU.mult), reads=[modv, ng_sb], writes=[A1])
        B1 = modv

        xt = [k.sb(f"xt{i}", [128, 4, 1024]) for i in range(2)]
        sc = dict(ss=k.sb("ss", [128, 4]), rstd=k.sb("rstd", [128, 4]), junk=k.sb("junk", [128, 1024], BF16),
                  xn=k.sb("xn", [128, 4, 1024], BF16), eps=eps)
        hT = k.sb("hT", [128, 8, 512], BF16)
        qT = k.sb("qT", [128, 2, 512], BF16); kT = k.sb("kT", [128, 2, 512], BF16)
        G = k.sb("G", [64, 8, 4]); e1 = k.sb("e1", [64, 16]); sp_ = k.sb("sp_", [64, 16])
        t1 = k.sb("t1", [64, 16]); t2 = k.sb("t2", [64, 16])
        Aa = k.sb("Aa", [64, 16]); KK = k.sb("KK", [64, 16]); EB = k.sb("EB", [64, 16]); Gt = k.sb("Gt", [128, 16])
        vext = [k.sb(f"vext{i}", [64, 2, 257], BF16) for i in range(8)]
        kp = [k.sb(f"kp{i}", [64, 2, 128], BF16) for i in range(8)]
        PT = [[k.sb(f"PT{h}_{i}", [64, 64], BF16) for i in range(2)] for h in range(2)]
        sm = [[k.sb(f"sm{h}_{i}", [64, 16]) for i in range(2)] for h in range(2)]
        junk64 = k.sb("junk64", [64, 256], BF16)
        nraw = [k.sb(f"nraw{i}", [64, 16, 257]) for i in range(2)]
        HOt = [k.sb(f"HO{i}", [64, 8, 512], hh.dtype) for i in range(2)]
        tq = [k.sb(f"tq{i}", [64, 160]) for i in range(2)]
        C32 = [k.sb(f"C32_{h}", [128, 257]) for h in range(2)]
        Cb = [k.sb(f"Cb_{h}", [128, 257], BF16) for h in range(2)]
        for h in range(2):
            k.op("dve", lambda E: E.memset(C32[h][:, :], 0.0), writes=[C32[h]])
            k.op("dve", lambda E: E.memset(Cb[h][:, :], 0.0), writes=[Cb[h]])
        for i in range(8):
            k.op("dve", lambda E: E.memset(vext[i][:, :, 256:257], 1.0), writes=[vext[i]])

        def load_x(t):
            k.dma("sp", xt[t % 2][:, :, :], x[t * 512:(t + 1) * 512, :].rearrange("(j p) d -> p j d", p=128),
                  writes=[xt[t % 2]])

        load_x(0)
        for t in range(NT):
            if t + 1 < NT:
                load_x(t + 1)
            rows_to_T(k, xt[t % 2], 4, A1, B1, ident, pT, hT, sc)
            for (dst, w) in ((qT, wq_sb), (kT, wk_sb)):
                for h in range(2):
                    for kc in range(8):
                        k.mm(pqk, pqk[:, :], w, w[:, kc, h * 128:(h + 1) * 128], hT, hT[:, kc, :], kc == 0, kc == 7)
                    k.act(dst, dst[:, h, :], pqk, pqk[:, :], AF.Copy)
            for c in range(8):
                for kc in range(8):
                    k.mm(pmisc, pmisc[0:64, 256 + c * 4:260 + c * 4], hT, hT[:, kc, c * 64:(c + 1) * 64],
                         wg_sb, wg_sb[:, kc, :], kc == 0, kc == 7, inc=(kc == 7 and c == 7))
            k.op("dve", lambda E: E.tensor_tensor(out=G[:, :, :], in0=pg_ap.rearrange("p (c g) -> p c g", g=4),
                                                  in1=bgb_sb[:, :].rearrange("p (c g) -> p c g", g=4), op=ALU.add),
                 reads=[pmisc, bgb_sb], writes=[G])
            k.act(e1, e1[:, :].rearrange("p (c g) -> p c g", g=2), G, G[:, :, 2:4], AF.Exp, scale=-1.0)
            k.act(sp_, sp_[:, :], e1, e1[:, :], AF.Ln, bias=(one_c, one_c[0:64, 0:1]))
            k.mm(pmisc, pcs_ap, tri, tri[:, :], sp_, sp_[:, :], True, True)
            k.mm(pmisc, ptot_ap, ones, ones[:, :], sp_, sp_[:, :], True, True)
            k.act(EB, EB[:, :], pmisc, pcs_ap, AF.Exp, scale=-1.0, bias=(lnsc, lnsc[0:64, 0:1]))
            k.op("dve", lambda E: E.tensor_tensor(out=t1[:, :].rearrange("p (c g) -> p c g", g=2), in0=G[:, :, 0:2],
                                                  in1=pcs_ap.rearrange("p (c g) -> p c g", g=2), op=ALU.add),
                 reads=[G, pmisc], writes=[t1])
            k.act(Aa, Aa[:, :], t1, t1[:, :], AF.Exp)
            k.op("dve", lambda E: E.tensor_tensor(out=t2[:, :], in0=t1[:, :], in1=pmisc[0:64, 304:320], op=ALU.subtract),
                 reads=[t1, pmisc], writes=[t2])
            k.act(KK, KK[:, :], t2, t2[:, :], AF.Exp)
            k.act(Gt, Gt[:, :], pmisc, ptot_ap, AF.Exp, scale=-1.0)
            for c in range(8):
                for kc in range(8):
                    k.mm(pqk, pqk[0:64, :], hT, hT[:, kc, c * 64:(c + 1) * 64], wv_sb, wv_sb[:, kc, :], kc == 0, kc == 7)
                for kc in range(8):
                    k.mm(pmisc, pk_ap, hT, hT[:, kc, c * 64:(c + 1) * 64], wk_sb, wk_sb[:, kc, :], kc == 0, kc == 7)
                k.act(vext[c], vext[c][:, :, 0:256], pqk, pqk[0:64, :].rearrange("p (h e) -> p h e", h=2), AF.Copy)
                for h in range(2):
                    idx = c * 2 + h
                    k.op("dve", lambda E: E.tensor_scalar(out=kp[c][:, h, :], in0=pmisc[0:64, h * 128:(h + 1) * 128],
                                                          scalar1=KK[:, idx:idx + 1], scalar2=None, op0=ALU.mult),
                         reads=[pmisc, KK], writes=[kp[c]])
            NR = nraw[t % 2]; HO = HOt[t % 2]; Q = tq[t % 2]
            for c in range(8):
                gc = t * 8 + c
                cs = slice(c * 64, (c + 1) * 64)
                pS = pqk
                for h in range(2):
                    k.mm(pS, pS[0:64, h * 64:(h + 1) * 64], kT, kT[:, h, cs], qT, qT[:, h, cs], True, True)
                for h in range(2):
                    idx = c * 2 + h
                    P = PT[h][gc % 2]
                    k.op("dve", lambda E: E.scalar_tensor_tensor(out=P[:, :], in0=pS[0:64, h * 64:(h + 1) * 64],
                                                                 scalar=Aa[:, idx:idx + 1], in1=tri[:, :],
                                                                 op0=ALU.mult, op1=ALU.mult),
                         reads=[pS, Aa, tri], writes=[P])
                for h in range(2):
                    P = PT[h][gc % 2]
                    k.mm(pN[h], pN[h][0:64, 0:257], qT, qT[:, h, cs], Cb[h], Cb[h][:, :], True, False)
                    k.mm(pN[h], pN[h][0:64, 0:257], P, P[:, :], vext[c], vext[c][:, h, :], False, True)
                    k.mm(pD[h], pD[h][:, 0:257], kp[c], kp[c][:, h, :], vext[c], vext[c][:, h, :], True, True)
                for h in range(2):
                    idx = c * 2 + h
                    k.op("dve", lambda E: E.scalar_tensor_tensor(out=Cb[h][:, :], in0=C32[h][:, :],
                                                                 scalar=Gt[:, idx:idx + 1], in1=pD[h][:, 0:257],
                                                                 op0=ALU.mult, op1=ALU.add),
                         reads=[C32[h], Gt, pD[h]], writes=[Cb[h]])
                    k.op("dve", lambda E: E.scalar_tensor_tensor(out=C32[h][:, :], in0=C32[h][:, :],
                                                                 scalar=Gt[:, idx:idx + 1], in1=pD[h][:, 0:257],
                                                                 op0=ALU.mult, op1=ALU.add),
                         reads=[C32[h], Gt, pD[h]], writes=[C32[h]])
                    k.act(NR, NR[:, idx, :], pN[h], pN[h][0:64, 0:257], AF.Copy)
            den = NR[:, :, 256]
            k.op("dve", lambda E: E.tensor_tensor(out=Q[:, 0:16], in0=den, in1=EB[:, :], op=ALU.mult),
                 reads=[NR, EB], writes=[Q])
            k.op("dve", lambda E: E.scalar_tensor_tensor(out=Q[:, 16:32], in0=Q[:, 0:16], scalar=-1.0, in1=Q[:, 0:16],
                                                         op0=ALU.mult, op1=ALU.max), reads=[Q], writes=[Q])
            k.op("dve", lambda E: E.tensor_scalar(out=Q[:, 32:48], in0=Q[:, 16:32], scalar1=1.0, scalar2=None,
                                                  op0=ALU.max), reads=[Q], writes=[Q])
            k.op("dve", lambda E: E.reciprocal(out=Q[:, 48:64], in_=Q[:, 32:48]), reads=[Q], writes=[Q])
            k.op("dve", lambda E: E.tensor_tensor(out=Q[:, 64:80], in0=Q[:, 48:64], in1=EB[:, :], op=ALU.mult),
                 reads=[Q, EB], writes=[Q])
            for u in range(16):
                k.act(junk64, junk64[:, :], NR, NR[:, u, 0:256], AF.Square, accum=(Q, Q[:, 80 + u:81 + u]))
            k.op("dve", lambda E: E.tensor_tensor(out=Q[:, 96:112], in0=Q[:, 64:80], in1=Q[:, 64:80], op=ALU.mult),
                 reads=[Q], writes=[Q])
            k.op("dve", lambda E: E.tensor_tensor(out=Q[:, 96:112], in0=Q[:, 96:112], in1=Q[:, 80:96], op=ALU.mult),
                 reads=[Q], writes=[Q])
            k.act(Q, Q[:, 112:128], Q, Q[:, 96:112], AF.Ln, scale=1.0 / 256.0, bias=(eps, eps[0:64, 0:1]))
            k.act(Q, Q[:, 112:128], Q, Q[:, 112:128], AF.Exp, scale=-0.5)
            k.op("dve", lambda E: E.tensor_tensor(out=Q[:, 128:144], in0=Q[:, 112:128], in1=Q[:, 64:80], op=ALU.mult),
                 reads=[Q], writes=[Q])
            for u in range(16):
                c, h = u // 2, u % 2
                k.op("dve", lambda E: E.scalar_tensor_tensor(out=HO[:, c, h * 256:(h + 1) * 256], in0=NR[:, u, 0:256],
                                                             scalar=Q[:, 128 + u:129 + u],
                                                             in1=hnb_sb[:, h * 256:(h + 1) * 256],
                                                             op0=ALU.mult, op1=ALU.mult),
                     reads=[NR, Q, hnb_sb], writes=[HO])
            k.dma("pool", hh[t * 512:(t + 1) * 512, :].rearrange("(c l) d -> l c d", l=64), HO[:, :, :], reads=[HO])


_INV = (10000.0 ** (-np.arange(0, 64, 2, dtype=np.float64) / 64)) / (2 * np.pi)
NSEL = 8


def k_collective(k, src, dst):
    k.barrier()
    if not hasattr(k, "cc_sem"):
        k.cc_sem = k.es.enter_context(k.nc.semaphore("cc_sem")); k.cc_cnt = 0
    k.nc.gpsimd.collective_compute("AllGather", ALU.bypass, replica_groups=[list(range(8))], ins=[src], outs=[dst]) \
        .then_inc(k.cc_sem, 1)
    k.cc_cnt += 1
    for e in ("sp", "pool", "pe", "act", "dve"):
        k.E[e].wait_ge(k.cc_sem, k.cc_cnt)


def build_fused(S):
    T = S // 2
    k = KB(); nc = k.nc
    D = k.dram_in
    x = D("x", [S, 1024]); cT = D("cT", [128, 8]); ident = D("ident", [128, 128]); tri = D("tri", [64, 64])
    sel = D("sel", [1, NSEL], I32)
    modw = [D("modw0", [1024, 6144]), D("modw1", [1024, 6144])]
    modb_cols = [D("modb_cols0", [128, 48]), D("modb_cols1", [128, 48])]
    modb_rows = [D("modb_rows0", [1, 6144]), D("modb_rows1", [1, 6144])]
    ng_cols = [D("ng_cols0", [128, 32]), D("ng_cols1", [128, 32])]
    ng_rows = [D("ng_rows0", [4, 1024]), D("ng_rows1", [4, 1024])]
    wq = D("wq", [1024, 256]); wk = D("wk", [1024, 256]); wv = D("wv", [1024, 512]); wg = D("wg", [1024, 4])
    bgb = D("bgb", [64, 32]); hnb = D("hnb", [64, 512])
    wo = D("wo", [1024, 1024]); wout0 = D("wout0", [1024, 1024])
    w1 = [D("w1_0", [1024, 4096]), D("w1_1", [1024, 4096])]; w2 = [D("w2_0", [4096, 1024]), D("w2_1", [4096, 1024])]
    posT = D("posT", [128, T // 128], I32); pos_row = D("pos_row", [1, S], I32)
    kvmodw = D("kvmodw", [1024, 2048]); kvmodb = D("kvmodb_cols", [128, 16]); kvng = D("kvng_cols", [128, 8])
    wqa = D("wqa", [1024, 384]); wdkv = D("wdkv", [1024, 320])
    qn_row = D("qn_row", [1, 384]); kvn_row = D("kvn_row", [1, 256]); invb = D("invb", [128, 32])
    wqb = D("wqb", [384, 768]); wqbs = D("wqbs", [384, 256]); wuk = D("wuk", [256, 512]); wuv = D("wuv", [256, 512])
    inv2 = D("inv2", [64, 1]); sgn = D("sgn", [64, 2]); maskd = D("maskd", [4, 128, 512])
    wout1 = D("wout1", [1024, 1024])
    y = k.dram_out("y", [T, 1024])
    I = lambda n, sh, dt: nc.dram_tensor(n, list(sh), dt).ap()
    hh_full = I("hh_full", [S, 512], BF16); hh_send = I("hh_send", [T, 512], BF16); hh_g = I("hh_g", [8 * T, 512], BF16)
    x1a = I("x1a", [T, 1024], F32); x2 = I("x2", [T, 1024], F32); x1b = I("x1b", [T, 1024], F32)
    lat_send = I("lat_send", [704, T], BF16); lat_g = I("lat_g", [8 * 704, T], BF16); lat_full = I("lat_full", [704, S], BF16)
    att_full = I("att_full", [S, 512], BF16); att_send = I("att_send", [T, 512], BF16); att_g = I("att_g", [8 * T, 512], BF16)
    regs = [k.es.enter_context(nc.sync.register(f"selreg{i}")) for i in range(5)]
    v = []
    for i, rg in enumerate(regs):
        nc.sync.reg_load(rg, sel[0:1, i:i + 1])
        v.append(nc.sync.snap(rg))
    v_own, v_oth, v_prank, v_latA, v_latB = v

    def dyn_copy(dst, src, rows):
        k.dma("sp", dst[:, :], src)

    x_own = I("x_own", [T, 1024], F32); hh_own = I("hh_own", [T, 512], BF16); hh_par = I("hh_par", [T, 512], BF16)
    att_own = I("att_own", [T, 512], BF16); att_par = I("att_par", [T, 512], BF16)
    dyn_copy(x_own, x[bass.ds(v_own, T), :], T)
    emit_mlstm(k, S, x, cT, modw[0][:, 0:2048], modb_cols[0][:, 0:16], ng_cols[0][:, 0:8], wq, wk, wv, wg, bgb, hnb, ident, tri, hh_full)
    k.barrier()
    dyn_copy(hh_send, hh_full[bass.ds(v_oth, T), :], T)
    dyn_copy(hh_own, hh_full[bass.ds(v_own, T), :], T)
    k_collective(k, hh_send, hh_g)
    dyn_copy(hh_par, hh_g[bass.ds(v_prank, T), :], T)
    k.barrier()
    emit_postmix(k, T, x_own, [hh_own, hh_par], x1a, cT,
                 modw[0], modb_cols[0], modb_rows[0], ng_cols[0], ng_rows[0], wo, wout0, ident, gate=True, mix_dt=BF16)
    emit_ffn(k, T, x1a, x2, cT, modw[0], modb_cols[0], modb_rows[0], ng_cols[0], ng_rows[0], w1[0], w2[0], ident)
    emit_l1prep(k, T, x2, posT, lat_send[0:384, :], lat_send[384:640, :], lat_send[640:704, :], cT,
                modw[1][:, 0:2048], modb_cols[1][:, 0:16], ng_cols[1][:, 0:8],
                kvmodw, kvmodb, kvng, wqa, wdkv, qn_row, kvn_row, invb, ident)
    k_collective(k, lat_send, lat_g)
    for (c0, vv) in ((0, v_latA), (T, v_latB)):
        k.dma("sp", lat_full[:, c0:c0 + T], lat_g[bass.ds(vv, 704), :])
    k.barrier()
    emit_attn(k, S, lat_full[0:384, :], lat_full[384:640, :], lat_full[640:704, :], pos_row, wqb, wqbs, wuk, wuv, inv2, sgn,
              maskd, att_full)
    k.barrier()
    dyn_copy(att_send, att_full[bass.ds(v_oth, T), :], T)
    dyn_copy(att_own, att_full[bass.ds(v_own, T), :], T)
    k_collective(k, att_send, att_g)
    dyn_copy(att_par, att_g[bass.ds(v_prank, T), :], T)
    k.barrier()
    emit_postmix(k, T, x2, [att_own, att_par], x1b, cT,
                 modw[1], modb_cols[1], modb_rows[1], ng_cols[1], ng_rows[1], None, wout1, ident, gate=False, mix_dt=BF16)
    emit_ffn(k, T, x1b, y, cT, modw[1], modb_cols[1], modb_rows[1], ng_cols[1], ng_rows[1], w1[1], w2[1], ident)
    k.finish()
    return k


def fused_inputs(inp, c, S):
    T = S // 2
    b, r = c // 2, c % 2
    p = 2 * b + (1 - r)
    a = inp["a_w_in"][0]
    gi = [3072 + 2 * r, 3073 + 2 * r, 3076 + 2 * r, 3077 + 2 * r]
    bg = inp["a_b_gates"][0][[2 * r, 2 * r + 1, 4 + 2 * r, 5 + 2 * r]]
    own = slice(r * 512, (r + 1) * 512); oth = slice((1 - r) * 512, (2 - r) * 512)
    perm = np.r_[np.arange(r * 512, (r + 1) * 512), np.arange((1 - r) * 512, (2 - r) * 512)]
    wqb = inp["b_w_qb"][0]; wukv = inp["w_ukv"]
    hs = range(r * 4, r * 4 + 4)
    m = {
        "x": np.ascontiguousarray(inp["x"][b, :S]), "cT": fm(inp["c"][b]), "ident": np.eye(128, dtype=np.float32),
        "tri": np.triu(np.ones((64, 64), np.float32)),
        "sel": np.array([[r * T, (1 - r) * T, p * T, (2 * b) * 704, (2 * b + 1) * 704, 0, 0, 0]], np.int32),
        "wq": np.ascontiguousarray(a[:, r * 256:(r + 1) * 256]),
        "wk": np.ascontiguousarray(a[:, 512 + r * 256:512 + (r + 1) * 256]),
        "wv": np.ascontiguousarray(a[:, 1024 + r * 512:1024 + (r + 1) * 512]),
        "wg": np.ascontiguousarray(a[:, gi]),
        "bgb": np.ascontiguousarray(np.tile(bg[None, :], (64, 8))),
        "hnb": np.ascontiguousarray(np.tile(inp["a_head_norm"][0][2 * r:2 * r + 2].reshape(1, 512), (64, 1))),
        "wo": np.ascontiguousarray(a[:, 2048:3072][:, perm]),
        "wout0": np.ascontiguousarray(inp["a_w_out"][0][perm, :]),
        "wout1": np.ascontiguousarray(inp["b_w_o"][0][perm, :]),
        "posT": np.ascontiguousarray(inp["positions"][b, r * T:(r + 1) * T].reshape(T // 128, 128).T.astype(np.int32)),
        "pos_row": np.ascontiguousarray(inp["positions"][b:b + 1, :S].astype(np.int32)),
        "kvmodw": np.ascontiguousarray(inp["kv_mod_w"]), "kvmodb_cols": fm(inp["kv_mod_b"]), "kvng_cols": fm(inp["kv_norm"]),
        "wqa": np.ascontiguousarray(inp["b_w_qa"][0]), "wdkv": np.ascontiguousarray(inp["w_dkv"]),
        "qn_row": np.ascontiguousarray(inp["b_q_norm"][0][None, :]), "kvn_row": np.ascontiguousarray(inp["kv_lora_norm"][None, :]),
        "invb": np.ascontiguousarray(np.tile(_INV.astype(np.float32)[None, :], (128, 1))),
        "wqb": np.ascontiguousarray(np.concatenate([wqb[:, h * 192:(h + 1) * 192] for h in hs], axis=1)),
        "wqbs": np.ascontiguousarray(np.concatenate(
            [np.concatenate([wqb[:, h * 192 + 160:h * 192 + 192], wqb[:, h * 192 + 128:h * 192 + 160]], axis=1) for h in hs], axis=1)),
        "wuk": np.ascontiguousarray(np.concatenate([wukv[:, h * 256:h * 256 + 128] for h in hs], axis=1)),
        "wuv": np.ascontiguousarray(np.concatenate([wukv[:, h * 256 + 128:h * 256 + 256] for h in hs], axis=1)),
    }
    inv2 = np.concatenate([_INV, _INV]).astype(np.float32)[:, None]
    sgn = np.zeros((64, 2), np.float32)
    sgn[0:32, 0] = -2 * np.pi; sgn[0:32, 1] = np.pi; sgn[32:, 0] = 2 * np.pi; sgn[32:, 1] = -np.pi
    kk = np.arange(128)[:, None]; q = np.arange(512)[None, :]
    m.update({"inv2": inv2, "sgn": sgn,
              "maskd": np.stack([(((mm * 128 + kk) // 64) <= (q // 64)).astype(np.float32) for mm in range(4)])})
    for l in range(2):
        m[f"modw{l}"] = np.ascontiguousarray(inp["mod_w"][l])
        m[f"modb_cols{l}"] = fm(inp["mod_b"][l]); m[f"modb_rows{l}"] = np.ascontiguousarray(inp["mod_b"][l][None, :])
        m[f"ng_cols{l}"] = np.ascontiguousarray(np.concatenate([fm(inp["norm_g"][l, i]) for i in range(4)], axis=1))
        m[f"ng_rows{l}"] = np.ascontiguousarray(inp["norm_g"][l])
        m[f"w1_{l}"] = np.ascontiguousarray(inp["ffn_w1"][l]); m[f"w2_{l}"] = np.ascontiguousarray(inp["ffn_w2"][l])
    return m


def kernel(**inp):
    inp = {kk: np.asarray(v) for kk, v in inp.items()}
    B, S = inp["x"].shape[0], inp["x"].shape[1]
    T = S // 2
    cores = list(range(8))
    k = build_fused(S)
    res = run_bass_kernel_spmd(k.nc, [fused_inputs(inp, c, S) for c in cores], core_ids=cores).results
    out = np.empty((B, S, 1024), np.float32)
    for c in cores:
        b, r = c // 2, c % 2
        out[b, r * T:(r + 1) * T] = res[c]["y"]
    return out
```

```python
import ml_dtypes
import contextlib, math
import numpy as np
import concourse.bass as bass
import concourse.mybir as mybir
from concourse.bass_utils import run_bass_kernel_spmd

F32 = mybir.dt.float32
BF16 = mybir.dt.bfloat16
I32 = mybir.dt.int32
AF = mybir.ActivationFunctionType
ALU = mybir.AluOpType
AX = mybir.AxisListType
NDS = 40


class Tile:
    def __init__(self, t, psum=False):
        self.t = t
        self.w = {}
        self.r = {}
        self.psum = psum

    def __getitem__(self, idx):
        return self.t[idx]


class KB:
    def __init__(self):
        self.nc = bass.Bass("TRN2", target_bir_lowering=False)
        self.es = contextlib.ExitStack()
        nc = self.nc
        self.E = {"pe": nc.tensor, "act": nc.scalar, "dve": nc.vector, "pool": nc.gpsimd, "sp": nc.sync}
        self.sem = {e: self.es.enter_context(nc.semaphore("s_" + e)) for e in ["pe", "act", "dve", "pool"]}
        self.cnt = {e: 0 for e in self.sem}
        self.known = {e: {} for e in self.E}
        self.dsems = [self.es.enter_context(nc.semaphore(f"d{i}")) for i in range(NDS)]
        self.dcnt = [0] * NDS
        self.dnext = 0
        self.nins = 0

    def _nm(self, n):
        self.uid = getattr(self, 'uid', 0) + 1
        return f"{n}_{self.uid}"

    def dram_in(self, name, shape, dt=F32):
        return self.nc.dram_tensor(name, list(shape), dt, kind="ExternalInput").ap()

    def dram_out(self, name, shape, dt=F32):
        return self.nc.dram_tensor(name, list(shape), dt, kind="ExternalOutput").ap()

    def sb(self, name, shape, dt=F32):
        return Tile(self.es.enter_context(self.nc.sbuf_tensor(self._nm("sb_" + name), list(shape), dt)))

    def ps(self, name, shape=(128, 512), dt=F32):
        return Tile(self.es.enter_context(self.nc.psum_tensor(self._nm("ps_" + name), list(shape), dt)), psum=True)

    def _need(self, e, key, v):
        if self.known[e].get(key, 0) >= v:
            return
        sem = self.sem[key] if isinstance(key, str) else self.dsems[key]
        self.E[e].wait_ge(sem, v)
        self.known[e][key] = v

    def _deps(self, e, reads, writes):
        for t in reads:
            if t.psum:
                continue
            for k, v in t.w.items():
                if e == "pe" and k == "pe":
                    continue
                self._need(e, k, v)
        for t in list(writes) + [t for t in reads if t.psum]:
            for k, v in list(t.w.items()) + list(t.r.items()):
                if e == "pe" and k == "pe":
                    continue
                self._need(e, k, v)

    def _mark(self, key, ev, reads, writes):
        for t in reads:
            if t.psum:
                t.w[key] = max(t.w.get(key, 0), ev)
            else:
                t.r[key] = max(t.r.get(key, 0), ev)
        for t in writes:
            t.w[key] = max(t.w.get(key, 0), ev)

    def op(self, e, fn, reads=(), writes=(), inc=True):
        self._deps(e, reads, writes)
        ins = fn(self.E[e])
        self.nins += 1
        if inc:
            self.cnt[e] += 1
            ins.then_inc(self.sem[e], 1)
            ev = self.cnt[e]
        else:
            ev = self.cnt[e] + 1
        self._mark(e, ev, reads, writes)
        return ins

    def dma(self, q, out_ap, in_ap, reads=(), writes=(), **kw):
        i = self.dnext
        self.dnext = (self.dnext + 1) % NDS
        if self.dcnt[i] > 0:
            self._need(q, i, self.dcnt[i])
        self._deps(q, reads, writes)
        ins = self.E[q].dma_start(out=out_ap, in_=in_ap, **kw)
        self.nins += 1
        self.dcnt[i] += 16
        ins.then_inc(self.dsems[i], 16)
        self._mark(i, self.dcnt[i], reads, writes)
        return ins

    def finish(self):
        for i in range(NDS):
            if self.dcnt[i] > 0:
                self._need("sp", i, self.dcnt[i])
        for e in self.sem:
            if self.cnt[e] > 0:
                self._need("sp", e, self.cnt[e])

    def mm(self, out_ps, out_ap, lhsT_t, lhsT_ap, rhs_t, rhs_ap, start, stop, inc=None):
        if inc is None:
            inc = stop
        return self.op("pe", lambda E: E.matmul(out_ap, lhsT_ap, rhs_ap, start=start, stop=stop),
                       reads=[lhsT_t, rhs_t], writes=[out_ps], inc=inc)

    def tr(self, out_ps, out_ap, in_t, in_ap, ident_t, ident_ap, inc=True):
        return self.op("pe", lambda E: E.transpose(out_ap, in_ap, ident_ap),
                       reads=[in_t, ident_t], writes=[out_ps], inc=inc)

    def act(self, out_t, out_ap, in_t, in_ap, func, bias=None, scale=None, accum=None, extra_reads=(), e="act"):
        kw = {}
        rd = [in_t] + list(extra_reads)
        wr = [out_t]
        if bias is not None:
            if isinstance(bias, tuple):
                rd.append(bias[0]); kw["bias"] = bias[1]
            else:
                kw["bias"] = bias
        if scale is not None:
            if isinstance(scale, tuple):
                rd.append(scale[0]); kw["scale"] = scale[1]
            else:
                kw["scale"] = scale
        if accum is not None:
            wr.append(accum[0]); kw["accum_out"] = accum[1]
        return self.op("act", lambda E: E.activation(out=out_ap, in_=in_ap, func=func, **kw), reads=rd, writes=wr)


def _scope(self):
    @contextlib.contextmanager
    def cm():
        old = self.es
        self.es = contextlib.ExitStack()
        try:
            yield
        finally:
            self.barrier()
            self.es.close()
            self.es = old
    return cm()


def _barrier(self):
    for e in ["pe", "act", "dve", "pool", "sp"]:
        for e2 in self.sem:
            if self.cnt[e2] > 0:
                self._need(e, e2, self.cnt[e2])
        for i in range(NDS):
            if self.dcnt[i] > 0:
                self._need(e, i, self.dcnt[i])


KB.scope = _scope
KB.barrier = _barrier


EPS = 1e-6
TWO_PI = 2.0 * math.pi


def fm(v):
    v = np.asarray(v, np.float32)
    return np.ascontiguousarray(v.reshape(-1, 128).T)


def bc(ap, n=128):
    return bass.AP(ap.tensor, ap.offset, [[0, n]] + [list(x) for x in ap.ap[1:]])


def ada_cols(k, cact, mw_d, ncols, modb_cols_d, pm, pm_ap, mw, modv):
    noc = ncols // 128
    for kc in range(8):
        k.dma("sp", mw[:, kc, 0:ncols], mw_d[kc * 128:(kc + 1) * 128, :], writes=[mw])
    for oc in range(noc):
        for kc in range(8):
            k.mm(pm, pm_ap[:, oc:oc + 1], mw, mw[:, kc, oc * 128:(oc + 1) * 128], cact, cact[:, kc:kc + 1],
                 kc == 0, kc == 7)
    mb = k.sb("adac_b", [128, noc])
    k.dma("sp", mb[:, :], modb_cols_d, writes=[mb])
    k.op("dve", lambda E: E.tensor_tensor(out=modv[:, 0:noc], in0=pm_ap[:, 0:noc], in1=mb[:, :], op=ALU.add),
         reads=[pm, mb], writes=[modv])


def ada_AB(k, cact, mw_d, modb_cols_d, ng_cols_d, pm, pm_ap, mw, A, modv):
    ada_cols(k, cact, mw_d, 2048, modb_cols_d, pm, pm_ap, mw, modv)
    ng = k.sb("adaab_ng", [128, 8])
    k.dma("sp", ng[:, :], ng_cols_d, writes=[ng])
    k.op("dve", lambda E: E.scalar_tensor_tensor(out=A[:, :], in0=modv[:, 8:16], scalar=1.0, in1=ng[:, :],
                                                 op0=ALU.add, op1=ALU.mult), reads=[modv, ng], writes=[A])


def make_crep(k, cact, name="crep"):
    ones = k.sb(name + "_1", [128, 128])
    k.op("dve", lambda E: E.memset(ones[:, :], 1.0), writes=[ones])
    crep = k.sb(name, [128, 8, 128])
    for kc in range(8):
        k.op("dve", lambda E: E.tensor_scalar(out=crep[:, kc, :], in0=ones[:, :], scalar1=cact[:, kc:kc + 1],
                                              scalar2=None, op0=ALU.mult), reads=[ones, cact], writes=[crep])
    return crep


def ada_rows(k, crep, mw_d, modb_row_d, ng_row_d, pb, mw, Gb):
    for kc in range(8):
        k.dma("sp", mw[:, kc, 0:1024], mw_d[kc * 128:(kc + 1) * 128, :], writes=[mw])
    rb = k.sb("adar_rb", [128, 1024]); rg = k.sb("adar_rg", [128, 1024])
    k.dma("sp", rb[:, :], bc(modb_row_d), writes=[rb])
    k.dma("sp", rg[:, :], bc(ng_row_d), writes=[rg])
    for n2 in range(2):
        for kc in range(8):
            k.mm(pb[n2], pb[n2][:, :], crep, crep[:, kc, :], mw, mw[:, kc, n2 * 512:(n2 + 1) * 512], kc == 0, kc == 7)
        sl = slice(n2 * 512, (n2 + 1) * 512)
        k.op("dve", lambda E: E.tensor_tensor(out=Gb[:, sl], in0=pb[n2][:, :], in1=rb[:, sl], op=ALU.add),
             reads=[pb[n2], rb], writes=[Gb])
    k.op("dve", lambda E: E.tensor_tensor(out=Gb[:, :], in0=Gb[:, :], in1=rg[:, :], op=ALU.mult),
         reads=[Gb, rg], writes=[Gb])


def norm_T(k, xt, j4, ABs, outs, ident, pT, sc):
    ss, rstd, junk, xn, eps = sc["ss"], sc["rstd"], sc["junk"], sc["xn"], sc["eps"]
    for j in range(j4):
        k.act(junk, junk[:, :], xt, xt[:, j, :], AF.Square, scale=1.0 / 32.0, accum=(ss, ss[:, j:j + 1]))
    k.act(rstd, rstd[:, 0:j4], ss, ss[:, 0:j4], AF.Ln, bias=(eps, eps[:, 0:1]))
    k.act(rstd, rstd[:, 0:j4], rstd, rstd[:, 0:j4], AF.Exp, scale=-0.5)
    for j in range(j4):
        k.op("dve", lambda E: E.tensor_scalar(out=xn[:, j, :], in0=xt[:, j, :], scalar1=rstd[:, j:j + 1],
                                              scalar2=None, op0=ALU.mult), reads=[xt, rstd], writes=[xn])
    W = j4 * 128
    for kc in range(8):
        p = pT[kc % 2]
        for j in range(j4):
            k.tr(p, p[:, j * 128:(j + 1) * 128], xn, xn[:, j, kc * 128:(kc + 1) * 128], ident, ident[:, :],
                 inc=(j == j4 - 1))
        for (A, B), hT in zip(ABs, outs):
            k.act(hT, hT[:, kc, 0:W], p, p[:, 0:W], AF.Identity, scale=(A, A[:, kc:kc + 1]), bias=(B, B[:, kc:kc + 1]))


def load_consts(k, ident_d):
    ident = k.sb("ident", [128, 128], BF16)
    k.dma("pool", ident[:, :], ident_d, writes=[ident])
    eps = k.sb("eps", [128, 1])
    k.op("dve", lambda E: E.memset(eps[:, :], EPS), writes=[eps])
    return ident, eps


def load_w(k, name, w_d, K, N, chunk=None):
    kc = K // 128
    t = k.sb(name, [128, kc, N], BF16)
    v = w_d.rearrange("(k p) n -> p k n", p=128)
    step = chunk or kc
    for c0 in range(0, kc, step):
        k.dma("pool", t[:, c0:c0 + step, :], v[:, c0:c0 + step, :], writes=[t])
    return t


def sandwich(k, pY, xrow_t, xrow_ap_fn, Gb, sc2, eps):
    ss2, rs, tmp, junk = sc2["ss2"], sc2["rs"], sc2["tmp"], sc2["junk"]
    for n2 in range(2):
        k.act(junk, junk[:, 0:512], pY[n2], pY[n2][:, :], AF.Square, scale=1.0 / 32.0, accum=(ss2, ss2[:, n2:n2 + 1]))
    k.op("dve", lambda E: E.tensor_tensor(out=rs[:, 0:1], in0=ss2[:, 0:1], in1=ss2[:, 1:2], op=ALU.add),
         reads=[ss2], writes=[rs])
    k.act(rs, rs[:, 1:2], rs, rs[:, 0:1], AF.Ln, bias=(eps, eps[:, 0:1]))
    k.act(rs, rs[:, 2:3], rs, rs[:, 1:2], AF.Exp, scale=-0.5)
    for n2 in range(2):
        sl = slice(n2 * 512, (n2 + 1) * 512)
        tm = tmp[n2]
        k.op("dve", lambda E: E.scalar_tensor_tensor(out=tm[:, :], in0=pY[n2][:, :], scalar=rs[:, 2:3], in1=Gb[:, sl],
                                                     op0=ALU.mult, op1=ALU.mult), reads=[pY[n2], rs, Gb], writes=[tm])
        xa = xrow_ap_fn(sl)
        k.op("pool", lambda E: E.tensor_tensor(out=xa, in0=xa, in1=tm[:, :], op=ALU.add),
             reads=[xrow_t, tm], writes=[xrow_t])


def emit_postmix(k, T, x_d, mix_parts, x1_d, cT_d, modw_d, modb_cols_d, modb_rows_d, ng_cols_d, ng_rows_d,
                 wo_d, wout_d, ident_d, gate, mix_dt=F32, mix_fm=False):
    NT = T // 512
    with k.scope():
        ident, eps = load_consts(k, ident_d)
        pT = [k.ps("pT0", [128, 1024], BF16), k.ps("pT1", [128, 1024], BF16)]
        pA = [k.ps("pA0"), k.ps("pA1")]; pY = [k.ps("pY0"), k.ps("pY1")]
        pY4 = [pY, [k.ps("pY2"), k.ps("pY3")]]
        A1 = k.sb("A1", [128, 8]); mv1 = k.sb("mv1", [128, 16]); Gb1 = k.sb("Gb1", [128, 1024])
        with k.scope():
            cact = k.sb("cact", [128, 8])
            k.dma("sp", cact[:, :], cT_d, writes=[cact])
            k.act(cact, cact[:, :], cact, cact[:, :], AF.Silu)
            mw = k.sb("mw", [128, 8, 2048])
            if gate:
                ada_AB(k, cact, modw_d[:, 0:2048], modb_cols_d[:, 0:16], ng_cols_d[:, 0:8], pA[0], pA[0][:, 0:16],
                       mw, A1, mv1)
            crep = make_crep(k, cact)
            ada_rows(k, crep, modw_d[:, 2048:3072], modb_rows_d[0:1, 2048:3072], ng_rows_d[1:2, :], pY, mw, Gb1)
        wout = load_w(k, "wout", wout_d, 1024, 1024)
        wo = load_w(k, "wo", wo_d, 1024, 1024) if gate else None
        xt = [k.sb(f"xt{i}", [128, 4, 1024]) for i in range(2)]
        hts = [k.sb(f"ht{i}", [128, 4, 1024], mix_dt) for i in range(2)] if not mix_fm else None
        if mix_fm:
            gateds = None
        elif gate or mix_dt != BF16:
            gateds = [k.sb(f"gated{i}", [128, 4, 1024], BF16) for i in range(2)]
        else:
            gateds = hts
        gTs = [k.sb(f"gT{i}", [128, 8, 512], BF16) for i in range(2)]
        sc = dict(ss=k.sb("ss", [128, 4]), rstd=k.sb("rstd", [128, 4]), junk=k.sb("junk", [128, 1024], BF16),
                  xn=k.sb("xn", [128, 4, 1024], BF16), eps=eps)
        hT = k.sb("hT", [128, 8, 512], BF16) if gate else None
        sg = [k.sb(f"sg{i}", [128, 512]) for i in range(2)]
        sc2s = [dict(ss2=k.sb(f"ss2{i}", [128, 2]), rs=k.sb(f"rs{i}", [128, 4]),
                     tmp=[k.sb(f"tmpa{i}", [128, 512]), k.sb(f"tmpb{i}", [128, 512])],
                     junk=k.sb(f"junks{i}", [128, 512], BF16)) for i in range(2)]

        def load(t):
            r = slice(t * 512, (t + 1) * 512)
            k.dma("sp", xt[t % 2][:, :, :], x_d[r, :].rearrange("(j p) d -> p j d", p=128), writes=[xt[t % 2]])
            if mix_fm:
                gT = gTs[t % 2]
                for i, part in enumerate(mix_parts):
                    fmv = part.rearrange("r c -> (r c)").rearrange("(f t) -> f t", t=T)
                    k.dma("sp", gT[:, i * 4:(i + 1) * 4, :], fmv.rearrange("(c p) t -> p c t", p=128)[:, :, r], writes=[gT])
                return
            ht = hts[t % 2]
            for i, part in enumerate(mix_parts):
                k.dma("sp", ht[:, :, i * 512:(i + 1) * 512], part[r, :].rearrange("(j p) d -> p j d", p=128), writes=[ht])

        def stage_a(t):
            if mix_fm:
                return
            X = xt[t % 2]; ht = hts[t % 2]; gated = gateds[t % 2]; gT = gTs[t % 2]
            if gate:
                norm_T(k, X, 4, [(A1, mv1)], [hT], ident, pT, sc)
                for j in range(4):
                    for n2 in range(2):
                        sl = slice(n2 * 512, (n2 + 1) * 512)
                        for kc in range(8):
                            k.mm(pA[n2], pA[n2][:, :], hT, hT[:, kc, j * 128:(j + 1) * 128], wo, wo[:, kc, sl],
                                 kc == 0, kc == 7)
                        s_ = sg[n2]
                        k.act(s_, s_[:, :], pA[n2], pA[n2][:, :], AF.Sigmoid)
                        k.op("dve", lambda E: E.tensor_tensor(out=gated[:, j, sl], in0=s_[:, :], in1=ht[:, j, sl],
                                                              op=ALU.mult), reads=[s_, ht], writes=[gated])
            elif gated is not ht:
                for j in range(4):
                    k.op("dve", lambda E: E.tensor_copy(out=gated[:, j, :], in_=ht[:, j, :]), reads=[ht], writes=[gated])
            for c in range(8):
                p = pT[c % 2]
                for j in range(4):
                    k.tr(p, p[:, j * 128:(j + 1) * 128], gated, gated[:, j, c * 128:(c + 1) * 128], ident, ident[:, :],
                         inc=(j == 3))
                if c % 2 == 0:
                    k.act(gT, gT[:, c, :], p, p[:, 0:512], AF.Copy)
                else:
                    k.op("dve", lambda E: E.tensor_copy(out=gT[:, c, :], in_=p[:, 0:512]), reads=[p], writes=[gT])

        def stage_b(t):
            r = slice(t * 512, (t + 1) * 512)
            X = xt[t % 2]; gT = gTs[t % 2]
            for j in range(4):
                py = pY4[j % 2]
                for n2 in range(2):
                    sl = slice(n2 * 512, (n2 + 1) * 512)
                    for c in range(8):
                        k.mm(py[n2], py[n2][:, :], gT, gT[:, c, j * 128:(j + 1) * 128], wout, wout[:, c, sl],
                             c == 0, c == 7)
                sandwich(k, py, X, lambda sl, X=X, j=j: X[:, j, sl], Gb1, sc2s[j % 2], eps)
            k.dma("pool", x1_d[r, :].rearrange("(j p) d -> p j d", p=128), X[:, :, :], reads=[X])

        load(0)
        stage_a(0)
        for t in range(NT):
            if t + 1 < NT:
                load(t + 1)
                stage_a(t + 1)
            stage_b(t)


def emit_ffn(k, T, xin_d, xout_d, cT_d, modw_d, modb_cols_d, modb_rows_d, ng_cols_d, ng_rows_d, w1_d, w2_d, ident_d):
    TT = 256
    NT = T // TT
    with k.scope():
        ident, eps = load_consts(k, ident_d)
        pT = [k.ps("pT0", [128, 1024], BF16), k.ps("pT1", [128, 1024], BF16)]
        pU = [k.ps("pU0"), k.ps("pU1")]; pY = [k.ps("pY0"), k.ps("pY1")]
        pY4 = [pY, [k.ps("pY2"), k.ps("pY3")]]
        A3 = k.sb("A3", [128, 8]); mv3 = k.sb("mv3", [128, 16]); Gb2 = k.sb("Gb2", [128, 1024])
        with k.scope():
            cact = k.sb("cact", [128, 8])
            k.dma("sp", cact[:, :], cT_d, writes=[cact])
            k.act(cact, cact[:, :], cact, cact[:, :], AF.Silu)
            mw = k.sb("mw", [128, 8, 2048])
            ada_AB(k, cact, modw_d[:, 3072:5120], modb_cols_d[:, 24:40], ng_cols_d[:, 16:24], pU[0], pU[0][:, 0:16],
                   mw, A3, mv3)
            crep = make_crep(k, cact)
            ada_rows(k, crep, modw_d[:, 5120:6144], modb_rows_d[0:1, 5120:6144], ng_rows_d[3:4, :], pY, mw, Gb2)
        W1 = load_w(k, "W1", w1_d, 1024, 4096, chunk=1)
        W2 = load_w(k, "W2", w2_d, 4096, 1024, chunk=4)
        xt = [k.sb(f"xt{i}", [128, 2, 1024]) for i in range(2)]
        sc = dict(ss=k.sb("ss", [128, 4]), rstd=k.sb("rstd", [128, 4]), junk=k.sb("junk", [128, 1024], BF16),
                  xn=k.sb("xn", [128, 2, 1024], BF16), eps=eps)
        hT = [k.sb(f"hT{i}", [128, 8, TT], BF16) for i in range(2)]
        uT = k.sb("uT", [128, 32, TT], BF16)
        rr = [k.sb(f"rr{i}", [128, TT]) for i in range(2)]
        sc2s = [dict(ss2=k.sb(f"ss2{i}", [128, 2]), rs=k.sb(f"rs{i}", [128, 4]),
                     tmp=[k.sb(f"tmpa{i}", [128, 512]), k.sb(f"tmpb{i}", [128, 512])],
                     junk=k.sb(f"junks{i}", [128, 512], BF16)) for i in range(2)]

        def load(t):
            r = slice(t * TT, (t + 1) * TT)
            k.dma("sp", xt[t % 2][:, :, :], xin_d[r, :].rearrange("(j p) d -> p j d", p=128), writes=[xt[t % 2]])

        load(0)
        norm_T(k, xt[0], 2, [(A3, mv3)], [hT[0]], ident, pT, sc)
        for t in range(NT):
            r = slice(t * TT, (t + 1) * TT)
            X = xt[t % 2]; H = hT[t % 2]
            if t + 1 < NT:
                load(t + 1)
            for fc in range(32):
                p = pU[fc % 2]
                for kc in range(8):
                    k.mm(p, p[:, 0:TT], W1, W1[:, kc, fc * 128:(fc + 1) * 128], H, H[:, kc, :], kc == 0, kc == 7)
                r_ = rr[fc % 2]
                k.act(r_, r_[:, :], p, p[:, 0:TT], AF.Relu)
                k.op("dve", lambda E: E.tensor_tensor(out=uT[:, fc, :], in0=r_[:, :], in1=r_[:, :], op=ALU.mult),
                     reads=[r_], writes=[uT])
            if t + 1 < NT:
                norm_T(k, xt[(t + 1) % 2], 2, [(A3, mv3)], [hT[(t + 1) % 2]], ident, pT, sc)
            for j in range(2):
                py = pY4[j % 2]
                for n2 in range(2):
                    sl = slice(n2 * 512, (n2 + 1) * 512)
                    for fc in range(32):
                        k.mm(py[n2], py[n2][:, :], uT, uT[:, fc, j * 128:(j + 1) * 128], W2, W2[:, fc, sl],
                             fc == 0, fc == 31)
                sandwich(k, py, X, lambda sl, X=X, j=j: X[:, j, sl], Gb2, sc2s[j % 2], eps)
            k.dma("pool", xout_d[r, :].rearrange("(j p) d -> p j d", p=128), X[:, :, :], reads=[X])


def rope_sincos(k, tt, ti, tf, out_sc, scale_ap, bias_ap, shape_ap):
    a = shape_ap
    k.op("dve", lambda E: E.tensor_copy(out=a(ti), in_=a(tt)), reads=[tt], writes=[ti])
    k.op("dve", lambda E: E.tensor_copy(out=a(tf), in_=a(ti)), reads=[ti], writes=[tf])
    k.op("dve", lambda E: E.tensor_tensor(out=a(tf), in0=a(tt), in1=a(tf), op=ALU.subtract), reads=[tt, tf], writes=[tf])
    k.op("dve", lambda E: E.scalar_tensor_tensor(out=a(tt), in0=a(tf), scalar=0.0, in1=a(tf), op0=ALU.is_lt, op1=ALU.add),
         reads=[tf], writes=[tt])
    k.act(out_sc, a(out_sc), tt, a(tt), AF.Sin, scale=scale_ap, bias=bias_ap)


def emit_l1prep(k, T, x_d, posT_d, qlatT_d, ckvT_d, kropeT_d, cT_d, modw_d, modb_cols_d, ng_cols_d,
                kvmodw_d, kvmodb_cols_d, kvng_cols_d, wqa_d, wdkv_d, qn_row_d, kvn_row_d, invb_d, ident_d):
    NT = T // 512
    with k.scope():
        ident, eps = load_consts(k, ident_d)
        pT = [k.ps("pT0", [128, 1024], BF16), k.ps("pT1", [128, 1024], BF16)]
        pQs = [k.ps("pQ0"), k.ps("pQ1")]; pKVs = [k.ps("pKV0"), k.ps("pKV1")]; pm = k.ps("pm")
        Aq = k.sb("Aq", [128, 8]); mvq = k.sb("mvq", [128, 16]); Akv = k.sb("Akv", [128, 8]); mvkv = k.sb("mvkv", [128, 16])
        with k.scope():
            cact = k.sb("cact", [128, 8])
            k.dma("sp", cact[:, :], cT_d, writes=[cact])
            k.act(cact, cact[:, :], cact, cact[:, :], AF.Silu)
            mw = k.sb("mw", [128, 8, 2048])
            ada_AB(k, cact, modw_d[:, 0:2048], modb_cols_d[:, 0:16], ng_cols_d[:, 0:8], pm, pm[:, 0:16], mw, Aq, mvq)
            ada_AB(k, cact, kvmodw_d, kvmodb_cols_d, kvng_cols_d, pm, pm[:, 16:32], mw, Akv, mvkv)
        wqa = load_w(k, "wqa", wqa_d, 1024, 384)
        wdkv = load_w(k, "wdkv", wdkv_d, 1024, 320)
        qnb = k.sb("qnb", [128, 384]); kvnb = k.sb("kvnb", [128, 256]); invb = k.sb("invb", [128, 32])
        k.dma("sp", qnb[:, :], bc(qn_row_d), writes=[qnb]); k.dma("sp", kvnb[:, :], bc(kvn_row_d), writes=[kvnb])
        k.dma("sp", invb[:, :], invb_d, writes=[invb])
        negpi = k.sb("negpi", [128, 1])
        k.op("dve", lambda E: E.memset(negpi[:, :], -math.pi), writes=[negpi])
        xt = [k.sb(f"xt{i}", [128, 4, 1024]) for i in range(2)]
        sc = dict(ss=k.sb("ss", [128, 4]), rstd=k.sb("rstd", [128, 4]), junk=k.sb("junk", [128, 1024], BF16),
                  xn=k.sb("xn", [128, 4, 1024], BF16), eps=eps)
        h1T = k.sb("h1T", [128, 8, 512], BF16); hsT = k.sb("hsT", [128, 8, 512], BF16)
        posi = k.sb("posi", [128, 4], I32); posf = k.sb("posf", [128, 4])
        tt = k.sb("tt", [128, 2, 4, 32]); ti = k.sb("ti", [128, 2, 4, 32], I32); tf = k.sb("tf", [128, 2, 4, 32])
        scs = k.sb("scs", [128, 2, 4, 32])
        sts = [k.sb(f"st{i}", [128, 8]) for i in range(2)]; junk2 = k.sb("junk2", [128, 384], BF16)
        qns = [k.sb(f"qn{i}", [128, 384], BF16) for i in range(2)]; cns = [k.sb(f"cn{i}", [128, 256], BF16) for i in range(2)]
        krs = [k.sb(f"kr{i}", [128, 64], BF16) for i in range(2)]
        r1s = [k.sb(f"r1{i}", [128, 64]) for i in range(2)]; r2s = [k.sb(f"r2{i}", [128, 64]) for i in range(2)]
        qlT = k.sb("qlT", [128, 3, 512], BF16); ckT = k.sb("ckT", [128, 2, 512], BF16); krT = k.sb("krT", [64, 512], BF16)

        def load(t):
            r = slice(t * 512, (t + 1) * 512)
            k.dma("sp", xt[t % 2][:, :, :], x_d[r, :].rearrange("(j p) d -> p j d", p=128), writes=[xt[t % 2]])

        full = lambda T_: T_[:, :, :, :]
        load(0)
        for t in range(NT):
            r = slice(t * 512, (t + 1) * 512)
            if t + 1 < NT:
                load(t + 1)
            X = xt[t % 2]
            k.dma("sp", posi[:, :], posT_d[:, t * 4:(t + 1) * 4], writes=[posi])
            k.op("dve", lambda E: E.tensor_copy(out=posf[:, :], in_=posi[:, :]), reads=[posi], writes=[posf])
            for j in range(4):
                k.op("dve", lambda E: E.tensor_scalar(out=tt[:, 0, j, :], in0=invb[:, :], scalar1=posf[:, j:j + 1],
                                                      scalar2=0.5, op0=ALU.mult, op1=ALU.add), reads=[invb, posf], writes=[tt])
            k.op("dve", lambda E: E.tensor_scalar(out=tt[:, 1, :, :], in0=tt[:, 0, :, :], scalar1=0.25, scalar2=None,
                                                  op0=ALU.add), reads=[tt], writes=[tt])
            rope_sincos(k, tt, ti, tf, scs, TWO_PI, (negpi, negpi[:, 0:1]), full)
            norm_T(k, X, 4, [(Aq, mvq), (Akv, mvkv)], [h1T, hsT], ident, pT, sc)
            for j in range(4):
                js = slice(j * 128, (j + 1) * 128)
                pQ = pQs[j % 2]; pKV = pKVs[j % 2]; st = sts[j % 2]; qn = qns[j % 2]; cn = cns[j % 2]; kr = krs[j % 2]
                r1 = r1s[j % 2]; r2 = r2s[j % 2]
                for kc in range(8):
                    k.mm(pQ, pQ[:, 0:384], h1T, h1T[:, kc, js], wqa, wqa[:, kc, :], kc == 0, kc == 7)
                for kc in range(8):
                    k.mm(pKV, pKV[:, 0:320], hsT, hsT[:, kc, js], wdkv, wdkv[:, kc, :], kc == 0, kc == 7)
                k.act(junk2, junk2[:, 0:384], pQ, pQ[:, 0:384], AF.Square, scale=384.0 ** -0.5, accum=(st, st[:, 0:1]))
                k.act(junk2, junk2[:, 0:256], pKV, pKV[:, 0:256], AF.Square, scale=1.0 / 16.0, accum=(st, st[:, 1:2]))
                k.act(st, st[:, 2:4], st, st[:, 0:2], AF.Ln, bias=(eps, eps[:, 0:1]))
                k.act(st, st[:, 4:6], st, st[:, 2:4], AF.Exp, scale=-0.5)
                k.op("dve", lambda E: E.scalar_tensor_tensor(out=qn[:, :], in0=pQ[:, 0:384], scalar=st[:, 4:5], in1=qnb[:, :],
                                                             op0=ALU.mult, op1=ALU.mult), reads=[pQ, st, qnb], writes=[qn])
                k.op("dve", lambda E: E.scalar_tensor_tensor(out=cn[:, :], in0=pKV[:, 0:256], scalar=st[:, 5:6], in1=kvnb[:, :],
                                                             op0=ALU.mult, op1=ALU.mult), reads=[pKV, st, kvnb], writes=[cn])
                sin_ = scs[:, 0, j, :]; cos_ = scs[:, 1, j, :]
                k.act(r1, r1[:, :], pKV, pKV[:, 256:320], AF.Copy)
                k.op("dve", lambda E: E.tensor_tensor(out=r2[:, 0:32], in0=r1[:, 32:64], in1=sin_, op=ALU.mult), reads=[r1, scs], writes=[r2])
                k.op("dve", lambda E: E.tensor_tensor(out=r2[:, 32:64], in0=r1[:, 0:32], in1=sin_, op=ALU.mult), reads=[r1, scs], writes=[r2])
                k.op("pool", lambda E: E.tensor_tensor(out=r1[:, 0:32], in0=r1[:, 0:32], in1=cos_, op=ALU.mult), reads=[r1, scs, r2], writes=[r1])
                k.op("pool", lambda E: E.tensor_tensor(out=r1[:, 32:64], in0=r1[:, 32:64], in1=cos_, op=ALU.mult), reads=[r1, scs], writes=[r1])
                k.op("dve", lambda E: E.tensor_tensor(out=kr[:, 0:32], in0=r1[:, 0:32], in1=r2[:, 0:32], op=ALU.subtract), reads=[r1, r2], writes=[kr])
                k.op("dve", lambda E: E.tensor_tensor(out=kr[:, 32:64], in0=r1[:, 32:64], in1=r2[:, 32:64], op=ALU.add), reads=[r1, r2], writes=[kr])
                p = pT[j % 2]
                for c in range(3):
                    k.tr(p, p[:, c * 128:(c + 1) * 128], qn, qn[:, c * 128:(c + 1) * 128], ident, ident[:, :], inc=False)
                for c in range(2):
                    k.tr(p, p[:, (3 + c) * 128:(4 + c) * 128], cn, cn[:, c * 128:(c + 1) * 128], ident, ident[:, :], inc=False)
                k.tr(p, p[0:64, 640:768], kr, kr[:, :], ident, ident[:, :], inc=True)
                k.act(qlT, qlT[:, :, js], p, p[:, 0:384].rearrange("p (c t) -> p c t", c=3), AF.Copy)
                k.op("dve", lambda E: E.tensor_copy(out=ckT[:, :, js], in_=p[:, 384:640].rearrange("p (c t) -> p c t", c=2)),
                     reads=[p], writes=[ckT])
                k.op("dve", lambda E: E.tensor_copy(out=krT[:, js], in_=p[0:64, 640:768]), reads=[p], writes=[krT])
            k.dma("pool", qlatT_d.rearrange("(c p) t -> p c t", p=128)[:, :, r], qlT[:, :, :], reads=[qlT])
            k.dma("pool", ckvT_d.rearrange("(c p) t -> p c t", p=128)[:, :, r], ckT[:, :, :], reads=[ckT])
            k.dma("pool", kropeT_d[:, r], krT[:, :], reads=[krT])


def emit_attn(k, S, qlatT_d, ckvT_d, kropeT_d, pos_row_d, wqb_d, wqbs_d, wuk_d, wuv_d, inv2_d, sgn_d, maskd_d, out_d):
    NQ = S // 512
    NKB = S // 128
    SCALE = 192.0 ** -0.5
    with k.scope():
        pS = [k.ps("pS0"), k.ps("pS1")]
        pAccO = [k.ps("pAccO0"), k.ps("pAccO1")]; pAccD = [k.ps("pAccD0"), k.ps("pAccD1")]
        pQ = [k.ps("pQ0"), k.ps("pQ1")]
        wqb = load_w(k, "wqb", wqb_d, 384, 768); wqbs = load_w(k, "wqbs", wqbs_d, 384, 256)
        wuk = load_w(k, "wuk", wuk_d, 256, 512); wuv = load_w(k, "wuv", wuv_d, 256, 512)
        krT = k.sb("krT", [64, S], BF16)
        k.dma("sp", krT[:, :], kropeT_d, writes=[krT])
        knT = k.sb("knT", [128, 4, S], BF16)
        vext = k.sb("vext", [128, NKB, 4, 129], BF16)
        k.op("pool", lambda E: E.memset(vext[:, :, :, 128:129], 1.0), writes=[vext])
        maskd = k.sb("maskd", [128, 4, 512], BF16)
        k.dma("pool", maskd[:, :, :], maskd_d.rearrange("m p q -> p m q"), writes=[maskd])
        inv2 = k.sb("inv2", [64, 1]); sgn = k.sb("sgn", [64, 2])
        k.dma("sp", inv2[:, :], inv2_d, writes=[inv2]); k.dma("sp", sgn[:, :], sgn_d, writes=[sgn])
        with k.scope():
            ckT = k.sb("ckT", [128, 2, S], BF16)
            k.dma("sp", ckT[:, :, :], ckvT_d.rearrange("(c p) t -> p c t", p=128), writes=[ckT])
            for tq in range(NQ):
                ts = slice(tq * 512, (tq + 1) * 512)
                for h in range(4):
                    p = pQ[h % 2]
                    for c in range(2):
                        k.mm(p, p[:, :], wuk, wuk[:, c, h * 128:(h + 1) * 128], ckT, ckT[:, c, ts], c == 0, c == 1)
                    if h % 2 == 0:
                        k.act(knT, knT[:, h, ts], p, p[:, :], AF.Copy)
                    else:
                        k.op("dve", lambda E: E.tensor_copy(out=knT[:, h, ts], in_=p[:, :]), reads=[p], writes=[knT])
            for kb in range(NKB):
                p = pS[kb % 2]
                for c in range(2):
                    k.mm(p, p[:, :], ckT, ckT[:, c, kb * 128:(kb + 1) * 128], wuv, wuv[:, c, :], c == 0, c == 1)
                if kb % 2 == 0:
                    k.act(vext, vext[:, kb, :, 0:128], p, p[:, :].rearrange("p (h e) -> p h e", h=4), AF.Copy)
                else:
                    k.op("dve", lambda E: E.tensor_copy(out=vext[:, kb, :, 0:128],
                                                        in_=p[:, :].rearrange("p (h e) -> p h e", h=4)),
                         reads=[p], writes=[vext])
        qlT = [k.sb(f"qlT{i}", [128, 3, 512], BF16) for i in range(2)]
        posi = k.sb("posi", [64, 512], I32)
        tt = k.sb("tt", [64, 2, 512]); ti = k.sb("ti", [64, 2, 512], I32); tf = k.sb("tf", [64, 2, 512])
        cs2 = k.sb("cs2", [64, 2, 512])
        qnT = [k.sb(f"qnT{i}", [128, 512], BF16) for i in range(2)]
        qrT = [k.sb(f"qrT{i}", [64, 512], BF16) for i in range(2)]
        ra = k.sb("ra", [64, 512]); rb = k.sb("rb", [64, 512])
        PTb = [k.sb(f"PTb{i}", [128, 512], BF16) for i in range(3)]
        obs = [k.sb(f"ob{i}", [128, 512], BF16) for i in range(2)]
        recs = [k.sb(f"rec{i}", [128, 512]) for i in range(2)]
        onesb = k.sb("onesb", [128, 128], BF16)
        k.op("dve", lambda E: E.memset(onesb[:, :], 1.0), writes=[onesb])
        negpi = k.sb("negpi", [64, 1]); twopi = k.sb("twopi", [64, 1])
        k.op("dve", lambda E: E.memset(negpi[:, :], -math.pi), writes=[negpi])
        k.op("dve", lambda E: E.memset(twopi[:, :], TWO_PI), writes=[twopi])
        qv = qlatT_d.rearrange("(c p) t -> p c t", p=128)
        unit = 0
        for tq in range(NQ):
            q0 = tq * 512
            ts = slice(q0, q0 + 512)
            QL = qlT[tq % 2]
            k.dma("sp", QL[:, :, :], qv[:, :, ts], writes=[QL])
            k.dma("sp", posi[:, :], bc(pos_row_d[0:1, ts], 64), writes=[posi])
            k.op("dve", lambda E: E.tensor_copy(out=tf[:, 0, :], in_=posi[:, :]), reads=[posi], writes=[tf])
            k.op("dve", lambda E: E.tensor_scalar(out=tt[:, 0, :], in0=tf[:, 0, :], scalar1=inv2[:, 0:1], scalar2=0.5,
                                                  op0=ALU.mult, op1=ALU.add), reads=[tf, inv2], writes=[tt])
            k.op("dve", lambda E: E.tensor_scalar(out=tt[:, 1, :], in0=tt[:, 0, :], scalar1=0.25, scalar2=None,
                                                  op0=ALU.add), reads=[tt], writes=[tt])
            a3 = lambda T_: T_[:, :, :]
            k.op("dve", lambda E: E.tensor_copy(out=a3(ti), in_=a3(tt)), reads=[tt], writes=[ti])
            k.op("dve", lambda E: E.tensor_copy(out=a3(tf), in_=a3(ti)), reads=[ti], writes=[tf])
            k.op("dve", lambda E: E.tensor_tensor(out=a3(tf), in0=a3(tt), in1=a3(tf), op=ALU.subtract), reads=[tt, tf], writes=[tf])
            k.op("dve", lambda E: E.scalar_tensor_tensor(out=a3(tt), in0=a3(tf), scalar=0.0, in1=a3(tf), op0=ALU.is_lt,
                                                         op1=ALU.add), reads=[tf], writes=[tt])
            k.act(cs2, cs2[:, 0, :], tt, tt[:, 0, :], AF.Sin, scale=(sgn, sgn[:, 0:1]), bias=(sgn, sgn[:, 1:2]))
            k.act(cs2, cs2[:, 1, :], tt, tt[:, 1, :], AF.Sin, scale=(twopi, twopi[:, 0:1]), bias=(negpi, negpi[:, 0:1]))
            nkb = (q0 + 512) // 128
            T_half = S // 2
            hq, tl = q0 // T_half, q0 % T_half

            def qproj(h):
                QN = qnT[h % 2]; QR = qrT[h % 2]
                p = pQ[0]
                for c in range(3):
                    k.mm(p, p[:, :], wqb, wqb[:, c, h * 192:h * 192 + 128], QL, QL[:, c, :], c == 0, c == 2)
                k.act(QN, QN[:, :], p, p[:, :], AF.Copy)
                p = pQ[1]
                for c in range(3):
                    k.mm(p, p[0:64, :], wqb, wqb[:, c, h * 192 + 128:h * 192 + 192], QL, QL[:, c, :], c == 0, c == 2)
                k.op("dve", lambda E: E.tensor_tensor(out=ra[:, :], in0=p[0:64, :], in1=cs2[:, 1, :], op=ALU.mult),
                     reads=[p, cs2], writes=[ra])
                for c in range(3):
                    k.mm(p, p[0:64, :], wqbs, wqbs[:, c, h * 64:(h + 1) * 64], QL, QL[:, c, :], c == 0, c == 2)
                k.op("dve", lambda E: E.tensor_tensor(out=rb[:, :], in0=p[0:64, :], in1=cs2[:, 0, :], op=ALU.mult),
                     reads=[p, cs2], writes=[rb])
                k.op("pool", lambda E: E.tensor_tensor(out=QR[:, :], in0=ra[:, :], in1=rb[:, :], op=ALU.add),
                     reads=[ra, rb], writes=[QR])

            def col0(kb):
                return max(0, kb * 128 - q0)

            def smm(h, kb, u):
                ks = slice(kb * 128, kb * 128 + 128)
                c0 = col0(kb)
                ps_ = pS[u % 2]
                k.mm(ps_, ps_[:, c0:512], knT, knT[:, h, ks], qnT[h % 2], qnT[h % 2][:, c0:512], True, False)
                k.mm(ps_, ps_[:, c0:512], krT, krT[:, ks], qrT[h % 2], qrT[h % 2][:, c0:512], False, True)

            qproj(0)
            for h in range(4):
                u0 = unit
                accO = pAccO[h % 2]; accD = pAccD[h % 2]
                smm(h, 0, u0)
                if h + 1 < 4:
                    qproj(h + 1)
                for kb in range(nkb):
                    k0 = kb * 128
                    u = u0 + kb
                    c0 = col0(kb)
                    ps_ = pS[u % 2]; PB = PTb[u % 3]
                    if kb + 1 < nkb:
                        smm(h, kb + 1, u + 1)
                    k.act(PB, PB[:, c0:512], ps_, ps_[:, c0:512], AF.Exp, scale=SCALE)
                    if k0 >= q0:
                        m = (k0 - q0) // 128
                        k.op("dve", lambda E: E.tensor_tensor(out=PB[:, c0:512], in0=PB[:, c0:512], in1=maskd[:, m, c0:512],
                                                              op=ALU.mult), reads=[PB, maskd], writes=[PB])
                    last = (kb == nkb - 1)
                    k.op("pe", lambda E: E.matmul(accO[:, c0:512], vext[:, kb, h, 0:128], PB[:, c0:512],
                                                  start=(kb == 0), stop=last, skip_group_check=True),
                         reads=[PB, vext], writes=[accO], inc=False)
                    k.op("pe", lambda E: E.matmul(accD[:, c0:512], onesb[:, :], PB[:, c0:512],
                                                  start=(kb == 0), stop=last, skip_group_check=True),
                         reads=[PB, onesb], writes=[accD], inc=True)
                unit = u0 + nkb
                rc = recs[h % 2]; ob = obs[h % 2]
                k.op("dve", lambda E: E.reciprocal(out=rc[:, :], in_=accD[:, :]), reads=[accD], writes=[rc])
                k.op("dve", lambda E: E.tensor_tensor(out=ob[:, :], in0=accO[:, :], in1=rc[:, :], op=ALU.mult),
                     reads=[accO, rc], writes=[ob])
                k.dma("pool", out_d[hq, h * 128:(h + 1) * 128, tl:tl + 512], ob[:, :], reads=[ob])


def rows_to_T(k, xt, j4, A, B, ident, pT, hT, sc):
    norm_T(k, xt, j4, [(A, B)], [hT], ident, pT, sc)


def adaln_cols(k, cact, mw_dram, ncols, modb_sb, pm, pm_ap, mw, name):
    noc = ncols // 128
    for kc in range(8):
        k.dma("sp", mw[:, kc, 0:ncols], mw_dram[kc * 128:(kc + 1) * 128, :], writes=[mw])
    for oc in range(noc):
        for kc in range(8):
            k.mm(pm, pm_ap[:, oc:oc + 1], mw, mw[:, kc, oc * 128:(oc + 1) * 128], cact, cact[:, kc:kc + 1],
                 kc == 0, kc == 7)
    modv = k.sb(name, [128, noc])
    k.op("dve", lambda E: E.tensor_tensor(out=modv[:, :], in0=pm_ap[:, 0:noc], in1=modb_sb[:, 0:noc], op=ALU.add),
         reads=[pm, modb_sb], writes=[modv])
    return modv


def emit_mlstm(k, S, x, cT, modw, modb, ng, wq, wk, wv, wg, bgb, hnb, ident_d, tri_d, hh):
    NT = S // 512
    with k.scope():
        ident = k.sb("ident", [128, 128], BF16); tri = k.sb("tri", [64, 64]); ones = k.sb("ones", [64, 128])
        bgb_sb = k.sb("bgb_sb", [64, 32]); hnb_sb = k.sb("hnb_sb", [64, 512])
        eps = k.sb("eps", [128, 1]); one_c = k.sb("one_c", [128, 1]); lnsc = k.sb("lnsc", [128, 1])
        cact = k.sb("cact", [128, 8]); modb_sb = k.sb("modb_sb", [128, 16]); ng_sb = k.sb("ng_sb", [128, 8])
        k.dma("pool", ident[:, :], ident_d, writes=[ident])
        k.dma("sp", tri[:, :], tri_d, writes=[tri])
        k.dma("sp", bgb_sb[:, :], bgb, writes=[bgb_sb]); k.dma("sp", hnb_sb[:, :], hnb, writes=[hnb_sb])
        k.dma("sp", cact[:, :], cT, writes=[cact]); k.dma("sp", modb_sb[:, :], modb, writes=[modb_sb])
        k.dma("sp", ng_sb[:, :], ng, writes=[ng_sb])
        k.op("dve", lambda E: E.memset(ones[:, :], 1.0), writes=[ones])
        k.op("dve", lambda E: E.memset(eps[:, :], EPS), writes=[eps])
        k.op("dve", lambda E: E.memset(one_c[:, :], 1.0), writes=[one_c])
        k.op("dve", lambda E: E.memset(lnsc[:, :], math.log(128.0 ** -0.5)), writes=[lnsc])
        wq_sb = k.sb("wq_sb", [128, 8, 256], BF16); wk_sb = k.sb("wk_sb", [128, 8, 256], BF16)
        wv_sb = k.sb("wv_sb", [128, 8, 512], BF16); wg_sb = k.sb("wg_sb", [128, 8, 4], BF16)
        for (t, d) in ((wq_sb, wq), (wk_sb, wk), (wv_sb, wv), (wg_sb, wg)):
            k.dma("pool", t[:, :, :], d.rearrange("(k p) n -> p k n", p=128), writes=[t])

        pT = [k.ps("pT0", [128, 1024], BF16), k.ps("pT1", [128, 1024], BF16)]
        pqk = k.ps("pqk"); pmisc = k.ps("pmisc")
        pN = [k.ps("pN0"), k.ps("pN1")]; pD = [k.ps("pD0"), k.ps("pD1")]
        pk_ap = pmisc[0:64, 0:256]; pg_ap = pmisc[0:64, 256:288]; pcs_ap = pmisc[0:64, 288:304]
        ptot_ap = pmisc[:, 304:320]; pmod_ap = pmisc[:, 320:336]

        k.act(cact, cact[:, :], cact, cact[:, :], AF.Silu)
        modv = k.sb("modv", [128, 16])
        with k.scope():
            mw = k.sb("mw", [128, 8, 2048])
            ada_cols(k, cact, modw, 2048, modb, pmisc, pmod_ap, mw, modv)
        A1 = k.sb("A1", [128, 8])
        k.op("dve", lambda E: E.scalar_tensor_tensor(out=A1[:, :], in0=modv[:, 8:16], scalar=1.0, in1=ng_sb[:, :],
                                                     op0=ALU.add, op1=ALU.mult), reads=[modv, ng_sb], writes=[A1])
        B1 = modv

        xt = [k.sb(f"xt{i}", [128, 4, 1024]) for i in range(2)]
        sc = dict(ss=k.sb("ss", [128, 4]), rstd=k.sb("rstd", [128, 4]), junk=k.sb("junk", [128, 1024], BF16),
                  xn=k.sb("xn", [128, 4, 1024], BF16), eps=eps)
        hT = k.sb("hT", [128, 8, 512], BF16)
        qT = k.sb("qT", [128, 2, 512], BF16); kT = k.sb("kT", [128, 2, 512], BF16)
        G = k.sb("G", [64, 8, 4]); e1 = k.sb("e1", [64, 16]); sp_ = k.sb("sp_", [64, 16])
        t1 = k.sb("t1", [64, 16]); t2 = k.sb("t2", [64, 16])
        Aa = k.sb("Aa", [64, 16]); KK = k.sb("KK", [64, 16]); EB = k.sb("EB", [64, 16]); Gt = k.sb("Gt", [128, 16])
        vext = [k.sb(f"vext{i}", [64, 2, 257], BF16) for i in range(8)]
        kp = [k.sb(f"kp{i}", [64, 2, 128], BF16) for i in range(8)]
        PT = [[k.sb(f"PT{h}_{i}", [64, 64], BF16) for i in range(2)] for h in range(2)]
        sm = [[k.sb(f"sm{h}_{i}", [64, 16]) for i in range(2)] for h in range(2)]
        junk64 = k.sb("junk64", [64, 256], BF16)
        nraw = [k.sb(f"nraw{i}", [64, 16, 257]) for i in range(2)]
        HOt = [k.sb(f"HO{i}", [64, 8, 512], hh.dtype) for i in range(2)]
        tq = [k.sb(f"tq{i}", [64, 160]) for i in range(2)]
        C32 = [k.sb(f"C32_{h}", [128, 257]) for h in range(2)]
        Cb = [k.sb(f"Cb_{h}", [128, 257], BF16) for h in range(2)]
        for h in range(2):
            k.op("dve", lambda E: E.memset(C32[h][:, :], 0.0), writes=[C32[h]])
            k.op("dve", lambda E: E.memset(Cb[h][:, :], 0.0), writes=[Cb[h]])
        for i in range(8):
            k.op("dve", lambda E: E.memset(vext[i][:, :, 256:257], 1.0), writes=[vext[i]])

        def load_x(t):
            k.dma("sp", xt[t % 2][:, :, :], x[t * 512:(t + 1) * 512, :].rearrange("(j p) d -> p j d", p=128),
                  writes=[xt[t % 2]])

        load_x(0)
        for t in range(NT):
            if t + 1 < NT:
                load_x(t + 1)
            rows_to_T(k, xt[t % 2], 4, A1, B1, ident, pT, hT, sc)
            for (dst, w) in ((qT, wq_sb), (kT, wk_sb)):
                for h in range(2):
                    for kc in range(8):
                        k.mm(pqk, pqk[:, :], w, w[:, kc, h * 128:(h + 1) * 128], hT, hT[:, kc, :], kc == 0, kc == 7)
                    k.act(dst, dst[:, h, :], pqk, pqk[:, :], AF.Copy)
            for c in range(8):
                for kc in range(8):
                    k.mm(pmisc, pmisc[0:64, 256 + c * 4:260 + c * 4], hT, hT[:, kc, c * 64:(c + 1) * 64],
                         wg_sb, wg_sb[:, kc, :], kc == 0, kc == 7, inc=(kc == 7 and c == 7))
            k.op("dve", lambda E: E.tensor_tensor(out=G[:, :, :], in0=pg_ap.rearrange("p (c g) -> p c g", g=4),
                                                  in1=bgb_sb[:, :].rearrange("p (c g) -> p c g", g=4), op=ALU.add),
                 reads=[pmisc, bgb_sb], writes=[G])
            k.act(e1, e1[:, :].rearrange("p (c g) -> p c g", g=2), G, G[:, :, 2:4], AF.Exp, scale=-1.0)
            k.act(sp_, sp_[:, :], e1, e1[:, :], AF.Ln, bias=(one_c, one_c[0:64, 0:1]))
            k.mm(pmisc, pcs_ap, tri, tri[:, :], sp_, sp_[:, :], True, True)
            k.mm(pmisc, ptot_ap, ones, ones[:, :], sp_, sp_[:, :], True, True)
            k.act(EB, EB[:, :], pmisc, pcs_ap, AF.Exp, scale=-1.0, bias=(lnsc, lnsc[0:64, 0:1]))
            k.op("dve", lambda E: E.tensor_tensor(out=t1[:, :].rearrange("p (c g) -> p c g", g=2), in0=G[:, :, 0:2],
                                                  in1=pcs_ap.rearrange("p (c g) -> p c g", g=2), op=ALU.add),
                 reads=[G, pmisc], writes=[t1])
            k.act(Aa, Aa[:, :], t1, t1[:, :], AF.Exp)
            k.op("dve", lambda E: E.tensor_tensor(out=t2[:, :], in0=t1[:, :], in1=pmisc[0:64, 304:320], op=ALU.subtract),
                 reads=[t1, pmisc], writes=[t2])
            k.act(KK, KK[:, :], t2, t2[:, :], AF.Exp)
            k.act(Gt, Gt[:, :], pmisc, ptot_ap, AF.Exp, scale=-1.0)
            for c in range(8):
                for kc in range(8):
                    k.mm(pqk, pqk[0:64, :], hT, hT[:, kc, c * 64:(c + 1) * 64], wv_sb, wv_sb[:, kc, :], kc == 0, kc == 7)
                for kc in range(8):
                    k.mm(pmisc, pk_ap, hT, hT[:, kc, c * 64:(c + 1) * 64], wk_sb, wk_sb[:, kc, :], kc == 0, kc == 7)
                k.act(vext[c], vext[c][:, :, 0:256], pqk, pqk[0:64, :].rearrange("p (h e) -> p h e", h=2), AF.Copy)
                for h in range(2):
                    idx = c * 2 + h
                    k.op("dve", lambda E: E.tensor_scalar(out=kp[c][:, h, :], in0=pmisc[0:64, h * 128:(h + 1) * 128],
                                                          scalar1=KK[:, idx:idx + 1], scalar2=None, op0=ALU.mult),
                         reads=[pmisc, KK], writes=[kp[c]])
            NR = nraw[t % 2]; HO = HOt[t % 2]; Q = tq[t % 2]
            for c in range(8):
                gc = t * 8 + c
                cs = slice(c * 64, (c + 1) * 64)
                pS = pqk
                for h in range(2):
                    k.mm(pS, pS[0:64, h * 64:(h + 1) * 64], kT, kT[:, h, cs], qT, qT[:, h, cs], True, True)
                for h in range(2):
                    idx = c * 2 + h
                    P = PT[h][gc % 2]
                    k.op("dve", lambda E: E.scalar_tensor_tensor(out=P[:, :], in0=pS[0:64, h * 64:(h + 1) * 64],
                                                                 scalar=Aa[:, idx:idx + 1], in1=tri[:, :],
                                                                 op0=ALU.mult, op1=ALU.mult),
                         reads=[pS, Aa, tri], writes=[P])
                for h in range(2):
                    P = PT[h][gc % 2]
                    k.mm(pN[h], pN[h][0:64, 0:257], qT, qT[:, h, cs], Cb[h], Cb[h][:, :], True, False)
                    k.mm(pN[h], pN[h][0:64, 0:257], P, P[:, :], vext[c], vext[c][:, h, :], False, True)
                    k.mm(pD[h], pD[h][:, 0:257], kp[c], kp[c][:, h, :], vext[c], vext[c][:, h, :], True, True)
                for h in range(2):
                    idx = c * 2 + h
                    k.op("dve", lambda E: E.scalar_tensor_tensor(out=Cb[h][:, :], in0=C32[h][:, :],
                                                                 scalar=Gt[:, idx:idx + 1], in1=pD[h][:, 0:257],
                                                                 op0=ALU.mult, op1=ALU.add),
                         reads=[C32[h], Gt, pD[h]], writes=[Cb[h]])
                    k.op("dve", lambda E: E.scalar_tensor_tensor(out=C32[h][:, :], in0=C32[h][:, :],
                                                                 scalar=Gt[:, idx:idx + 1], in1=pD[h][:, 0:257],
                                                                 op0=ALU.mult, op1=ALU.add),
                         reads=[C32[h], Gt, pD[h]], writes=[C32[h]])
                    k.act(NR, NR[:, idx, :], pN[h], pN[h][0:64, 0:257], AF.Copy)
            den = NR[:, :, 256]
            k.op("dve", lambda E: E.tensor_tensor(out=Q[:, 0:16], in0=den, in1=EB[:, :], op=ALU.mult),
                 reads=[NR, EB], writes=[Q])
            k.op("dve", lambda E: E.scalar_tensor_tensor(out=Q[:, 16:32], in0=Q[:, 0:16], scalar=-1.0, in1=Q[:, 0:16],
                                                         op0=ALU.mult, op1=ALU.max), reads=[Q], writes=[Q])
            k.op("dve", lambda E: E.tensor_scalar(out=Q[:, 32:48], in0=Q[:, 16:32], scalar1=1.0, scalar2=None,
                                                  op0=ALU.max), reads=[Q], writes=[Q])
            k.op("dve", lambda E: E.reciprocal(out=Q[:, 48:64], in_=Q[:, 32:48]), reads=[Q], writes=[Q])
            k.op("dve", lambda E: E.tensor_tensor(out=Q[:, 64:80], in0=Q[:, 48:64], in1=EB[:, :], op=ALU.mult),
                 reads=[Q, EB], writes=[Q])
            for u in range(16):
                k.act(junk64, junk64[:, :], NR, NR[:, u, 0:256], AF.Square, accum=(Q, Q[:, 80 + u:81 + u]))
            k.op("dve", lambda E: E.tensor_tensor(out=Q[:, 96:112], in0=Q[:, 64:80], in1=Q[:, 64:80], op=ALU.mult),
                 reads=[Q], writes=[Q])
            k.op("dve", lambda E: E.tensor_tensor(out=Q[:, 96:112], in0=Q[:, 96:112], in1=Q[:, 80:96], op=ALU.mult),
                 reads=[Q], writes=[Q])
            k.act(Q, Q[:, 112:128], Q, Q[:, 96:112], AF.Ln, scale=1.0 / 256.0, bias=(eps, eps[0:64, 0:1]))
            k.act(Q, Q[:, 112:128], Q, Q[:, 112:128], AF.Exp, scale=-0.5)
            k.op("dve", lambda E: E.tensor_tensor(out=Q[:, 128:144], in0=Q[:, 112:128], in1=Q[:, 64:80], op=ALU.mult),
                 reads=[Q], writes=[Q])
            for u in range(16):
                c, h = u // 2, u % 2
                k.op("dve", lambda E: E.scalar_tensor_tensor(out=HO[:, c, h * 256:(h + 1) * 256], in0=NR[:, u, 0:256],
                                                             scalar=Q[:, 128 + u:129 + u],
                                                             in1=hnb_sb[:, h * 256:(h + 1) * 256],
                                                             op0=ALU.mult, op1=ALU.mult),
                     reads=[NR, Q, hnb_sb], writes=[HO])
            k.dma("pool", hh[t * 512:(t + 1) * 512, :].rearrange("(c l) d -> l c d", l=64), HO[:, :, :], reads=[HO])


_INV = (10000.0 ** (-np.arange(0, 64, 2, dtype=np.float64) / 64)) / (2 * np.pi)
NSEL = 8


def k_collective(k, src, dst):
    k.barrier()
    if not hasattr(k, "cc_sem"):
        k.cc_sem = k.es.enter_context(k.nc.semaphore("cc_sem")); k.cc_cnt = 0
    k.nc.gpsimd.collective_compute("AllGather", ALU.bypass, replica_groups=[list(range(8))], ins=[src], outs=[dst]) \
        .then_inc(k.cc_sem, 1)
    k.cc_cnt += 1
    for e in ("sp", "pool", "pe", "act", "dve"):
        k.E[e].wait_ge(k.cc_sem, k.cc_cnt)


def build_fused(S):
    T = S // 2
    k = KB(); nc = k.nc
    D = k.dram_in
    x = D("x", [S, 1024]); cT = D("cT", [128, 8]); ident = D("ident", [128, 128]); tri = D("tri", [64, 64])
    sel = D("sel", [1, NSEL], I32)
    modw = [D("modw0", [1024, 6144]), D("modw1", [1024, 6144])]
    modb_cols = [D("modb_cols0", [128, 48]), D("modb_cols1", [128, 48])]
    modb_rows = [D("modb_rows0", [1, 6144]), D("modb_rows1", [1, 6144])]
    ng_cols = [D("ng_cols0", [128, 32]), D("ng_cols1", [128, 32])]
    ng_rows = [D("ng_rows0", [4, 1024]), D("ng_rows1", [4, 1024])]
    wq = D("wq", [1024, 256]); wk = D("wk", [1024, 256]); wv = D("wv", [1024, 512]); wg = D("wg", [1024, 4])
    bgb = D("bgb", [64, 32]); hnb = D("hnb", [64, 512])
    wo = D("wo", [1024, 1024]); wout0 = D("wout0", [1024, 1024])
    w1 = [D("w1_0", [1024, 4096]), D("w1_1", [1024, 4096])]; w2 = [D("w2_0", [4096, 1024]), D("w2_1", [4096, 1024])]
    posT = D("posT", [128, T // 128], I32); pos_row = D("pos_row", [1, S], I32)
    kvmodw = D("kvmodw", [1024, 2048]); kvmodb = D("kvmodb_cols", [128, 16]); kvng = D("kvng_cols", [128, 8])
    wqa = D("wqa", [1024, 384]); wdkv = D("wdkv", [1024, 320])
    qn_row = D("qn_row", [1, 384]); kvn_row = D("kvn_row", [1, 256]); invb = D("invb", [128, 32])
    wqb = D("wqb", [384, 768]); wqbs = D("wqbs", [384, 256]); wuk = D("wuk", [256, 512]); wuv = D("wuv", [256, 512])
    inv2 = D("inv2", [64, 1]); sgn = D("sgn", [64, 2]); maskd = D("maskd", [4, 128, 512])
    wout1 = D("wout1", [1024, 1024])
    y = k.dram_out("y", [T, 1024])
    I = lambda n, sh, dt: nc.dram_tensor(n, list(sh), dt).ap()
    hh_full = I("hh_full", [S, 512], BF16); hh_send = I("hh_send", [T, 512], BF16); hh_g = I("hh_g", [8 * T, 512], BF16)
    x1a = I("x1a", [T, 1024], F32); x2 = I("x2", [T, 1024], F32); x1b = I("x1b", [T, 1024], F32)
    lat_send = I("lat_send", [704, T], BF16); lat_g = I("lat_g", [8 * 704, T], BF16); lat_full = I("lat_full", [704, S], BF16)
    att_fm = I("att_fm", [2, 512, T], BF16)
    att_full = att_fm.rearrange("a f t -> (a f t)").rearrange("(r c) -> r c", c=512); att_send = I("att_send", [T, 512], BF16); att_g = I("att_g", [8 * T, 512], BF16)
    regs = [k.es.enter_context(nc.sync.register(f"selreg{i}")) for i in range(5)]
    v = []
    for i, rg in enumerate(regs):
        nc.sync.reg_load(rg, sel[0:1, i:i + 1])
        v.append(nc.sync.snap(rg))
    v_own, v_oth, v_prank, v_latA, v_latB = v

    def dyn_copy(dst, src, rows):
        k.dma("sp", dst[:, :], src)

    x_own = I("x_own", [T, 1024], F32); hh_own = I("hh_own", [T, 512], BF16); hh_par = I("hh_par", [T, 512], BF16)
    att_own = I("att_own", [T, 512], BF16); att_par = I("att_par", [T, 512], BF16)
    dyn_copy(x_own, x[bass.ds(v_own, T), :], T)
    emit_mlstm(k, S, x, cT, modw[0][:, 0:2048], modb_cols[0][:, 0:16], ng_cols[0][:, 0:8], wq, wk, wv, wg, bgb, hnb, ident, tri, hh_full)
    k.barrier()
    dyn_copy(hh_send, hh_full[bass.ds(v_oth, T), :], T)
    dyn_copy(hh_own, hh_full[bass.ds(v_own, T), :], T)
    k_collective(k, hh_send, hh_g)
    dyn_copy(hh_par, hh_g[bass.ds(v_prank, T), :], T)
    k.barrier()
    emit_postmix(k, T, x_own, [hh_own, hh_par], x1a, cT,
                 modw[0], modb_cols[0], modb_rows[0], ng_cols[0], ng_rows[0], wo, wout0, ident, gate=True, mix_dt=BF16)
    emit_ffn(k, T, x1a, x2, cT, modw[0], modb_cols[0], modb_rows[0], ng_cols[0], ng_rows[0], w1[0], w2[0], ident)
    emit_l1prep(k, T, x2, posT, lat_send[0:384, :], lat_send[384:640, :], lat_send[640:704, :], cT,
                modw[1][:, 0:2048], modb_cols[1][:, 0:16], ng_cols[1][:, 0:8],
                kvmodw, kvmodb, kvng, wqa, wdkv, qn_row, kvn_row, invb, ident)
    k_collective(k, lat_send, lat_g)
    for (c0, vv) in ((0, v_latA), (T, v_latB)):
        k.dma("sp", lat_full[:, c0:c0 + T], lat_g[bass.ds(vv, 704), :])
    k.barrier()
    emit_attn(k, S, lat_full[0:384, :], lat_full[384:640, :], lat_full[640:704, :], pos_row, wqb, wqbs, wuk, wuv, inv2, sgn,
              maskd, att_fm)
    k.barrier()
    dyn_copy(att_send, att_full[bass.ds(v_oth, T), :], T)
    dyn_copy(att_own, att_full[bass.ds(v_own, T), :], T)
    k_collective(k, att_send, att_g)
    dyn_copy(att_par, att_g[bass.ds(v_prank, T), :], T)
    k.barrier()
    emit_postmix(k, T, x2, [att_own, att_par], x1b, cT,
                 modw[1], modb_cols[1], modb_rows[1], ng_cols[1], ng_rows[1], None, wout1, ident, gate=False, mix_dt=BF16, mix_fm=True)
    emit_ffn(k, T, x1b, y, cT, modw[1], modb_cols[1], modb_rows[1], ng_cols[1], ng_rows[1], w1[1], w2[1], ident)
    k.finish()
    return k


def fused_inputs(inp, c, S):
    T = S // 2
    b, r = c // 2, c % 2
    p = 2 * b + (1 - r)
    a = inp["a_w_in"][0]
    gi = [3072 + 2 * r, 3073 + 2 * r, 3076 + 2 * r, 3077 + 2 * r]
    bg = inp["a_b_gates"][0][[2 * r, 2 * r + 1, 4 + 2 * r, 5 + 2 * r]]
    own = slice(r * 512, (r + 1) * 512); oth = slice((1 - r) * 512, (2 - r) * 512)
    perm = np.r_[np.arange(r * 512, (r + 1) * 512), np.arange((1 - r) * 512, (2 - r) * 512)]
    wqb = inp["b_w_qb"][0]; wukv = inp["w_ukv"]
    hs = range(r * 4, r * 4 + 4)
    m = {
        "x": np.ascontiguousarray(inp["x"][b, :S]), "cT": fm(inp["c"][b]), "ident": np.eye(128, dtype=np.float32),
        "tri": np.triu(np.ones((64, 64), np.float32)),
        "sel": np.array([[r * T, (1 - r) * T, p * T, (2 * b) * 704, (2 * b + 1) * 704, 0, 0, 0]], np.int32),
        "wq": np.ascontiguousarray(a[:, r * 256:(r + 1) * 256]),
        "wk": np.ascontiguousarray(a[:, 512 + r * 256:512 + (r + 1) * 256]),
        "wv": np.ascontiguousarray(a[:, 1024 + r * 512:1024 + (r + 1) * 512]),
        "wg": np.ascontiguousarray(a[:, gi]),
        "bgb": np.ascontiguousarray(np.tile(bg[None, :], (64, 8))),
        "hnb": np.ascontiguousarray(np.tile(inp["a_head_norm"][0][2 * r:2 * r + 2].reshape(1, 512), (64, 1))),
        "wo": np.ascontiguousarray(a[:, 2048:3072][:, perm]),
        "wout0": np.ascontiguousarray(inp["a_w_out"][0][perm, :]),
        "wout1": np.ascontiguousarray(inp["b_w_o"][0][perm, :]),
        "posT": np.ascontiguousarray(inp["positions"][b, r * T:(r + 1) * T].reshape(T // 128, 128).T.astype(np.int32)),
        "pos_row": np.ascontiguousarray(inp["positions"][b:b + 1, :S].astype(np.int32)),
        "kvmodw": np.ascontiguousarray(inp["kv_mod_w"]), "kvmodb_cols": fm(inp["kv_mod_b"]), "kvng_cols": fm(inp["kv_norm"]),
        "wqa": np.ascontiguousarray(inp["b_w_qa"][0]), "wdkv": np.ascontiguousarray(inp["w_dkv"]),
        "qn_row": np.ascontiguousarray(inp["b_q_norm"][0][None, :]), "kvn_row": np.ascontiguousarray(inp["kv_lora_norm"][None, :]),
        "invb": np.ascontiguousarray(np.tile(_INV.astype(np.float32)[None, :], (128, 1))),
        "wqb": np.ascontiguousarray(np.concatenate([wqb[:, h * 192:(h + 1) * 192] for h in hs], axis=1)),
        "wqbs": np.ascontiguousarray(np.concatenate(
            [np.concatenate([wqb[:, h * 192 + 160:h * 192 + 192], wqb[:, h * 192 + 128:h * 192 + 160]], axis=1) for h in hs], axis=1)),
        "wuk": np.ascontiguousarray(np.concatenate([wukv[:, h * 256:h * 256 + 128] for h in hs], axis=1)),
        "wuv": np.ascontiguousarray(np.concatenate([wukv[:, h * 256 + 128:h * 256 + 256] for h in hs], axis=1)),
    }
    inv2 = np.concatenate([_INV, _INV]).astype(np.float32)[:, None]
    sgn = np.zeros((64, 2), np.float32)
    sgn[0:32, 0] = -2 * np.pi; sgn[0:32, 1] = np.pi; sgn[32:, 0] = 2 * np.pi; sgn[32:, 1] = -np.pi
    kk = np.arange(128)[:, None]; q = np.arange(512)[None, :]
    m.update({"inv2": inv2, "sgn": sgn,
              "maskd": np.stack([(((mm * 128 + kk) // 64) <= (q // 64)).astype(np.float32) for mm in range(4)])})
    for l in range(2):
        m[f"modw{l}"] = np.ascontiguousarray(inp["mod_w"][l])
        m[f"modb_cols{l}"] = fm(inp["mod_b"][l]); m[f"modb_rows{l}"] = np.ascontiguousarray(inp["mod_b"][l][None, :])
        m[f"ng_cols{l}"] = np.ascontiguousarray(np.concatenate([fm(inp["norm_g"][l, i]) for i in range(4)], axis=1))
        m[f"ng_rows{l}"] = np.ascontiguousarray(inp["norm_g"][l])
        m[f"w1_{l}"] = np.ascontiguousarray(inp["ffn_w1"][l]); m[f"w2_{l}"] = np.ascontiguousarray(inp["ffn_w2"][l])
    return m


def kernel(**inp):
    inp = {kk: np.asarray(v) for kk, v in inp.items()}
    B, S = inp["x"].shape[0], inp["x"].shape[1]
    T = S // 2
    cores = list(range(8))
    k = build_fused(S)
    res = run_bass_kernel_spmd(k.nc, [fused_inputs(inp, c, S) for c in cores], core_ids=cores).results
    out = np.empty((B, S, 1024), np.float32)
    for c in cores:
        b, r = c // 2, c % 2
        out[b, r * T:(r + 1) * T] = res[c]["y"]
    return out
```
